# Optimizing a Trainium2 kernel written in Bass

```python
import math
import jax, jax.numpy as jnp
from jax import lax
import numpy as np


D_MODEL = 2048
BATCH = 2
SEQ = 4096
DEPTH = 2

GRID_W = 64
A_HEAD_DIM = 128
A_WIDTH = D_MODEL // 2
A_HEADS = A_WIDTH // A_HEAD_DIM
HGRN_CHUNK = 64
B_WIDTH = D_MODEL // 2
B_GROUP_DIM = 128
B_GROUPS = B_WIDTH // B_GROUP_DIM
B_CHUNK = 128
IN0_COLS = 5 * A_WIDTH + 2 * B_WIDTH
MIX0_WIDTH = A_WIDTH + B_WIDTH
C_HEAD_DIM = 128
C_Q_HEADS = D_MODEL // C_HEAD_DIM
C_KV_HEADS = C_Q_HEADS // 4
C_GROUP = C_Q_HEADS // C_KV_HEADS
C_Q_BLOCK = 128
ROPE_THETA = 10000.0
IN1_COLS = (C_Q_HEADS + 2 * C_KV_HEADS) * C_HEAD_DIM
MIX1_WIDTH = C_Q_HEADS * C_HEAD_DIM
D_FF = 5632
CONV_W = 3
N_EVEN = (DEPTH + 1) // 2
N_ODD = DEPTH // 2
ALPHA = (2.0 * DEPTH) ** 0.25
BETA = (8.0 * DEPTH) ** -0.25
LN_EPS = 1e-5
RMS_EPS = 1e-6

kernel_name = 'hybrid_hgrn2_gmlp_axialgqa_convffn_deepnorm'


def layer_norm(x, g, b):
    xf = x.astype(jnp.float32)
    mu = jnp.mean(xf, axis=-1, keepdims=True)
    var = jnp.mean(jnp.square(xf - mu), axis=-1, keepdims=True)
    return ((xf - mu) * lax.rsqrt(var + LN_EPS) * g.astype(jnp.float32) + b.astype(jnp.float32)).astype(x.dtype)


def rms_norm(x, g):
    xf = x.astype(jnp.float32)
    y = xf * lax.rsqrt(jnp.mean(xf * xf, axis=-1, keepdims=True) + RMS_EPS)
    return (y * g.astype(jnp.float32)).astype(x.dtype)


def hgrn2_scan(q, k, v, log_f):
    Bn, H, T, DK = q.shape
    DV = v.shape[-1]
    C = HGRN_CHUNK
    N = T // C
    q, k, log_f = [a.astype(jnp.float32).reshape(Bn, H, N, C, DK) for a in (q, k, log_f)]
    v = v.astype(jnp.float32).reshape(Bn, H, N, C, DV)
    b = jnp.cumsum(log_f, axis=3)
    b_last = b[:, :, :, C - 1:C, :]
    b_mid = b[:, :, :, C // 2 - 1:C // 2, :]
    scores = jnp.einsum('bhntd,bhnsd->bhnts', q * jnp.exp(b - b_mid), k * jnp.exp(b_mid - b))
    mask = jnp.tril(jnp.ones((C, C), dtype=bool))
    scores = jnp.where(mask, scores, 0.0)
    o_intra = jnp.einsum('bhnts,bhnsv->bhntv', scores, v)
    kv = jnp.einsum('bhnsd,bhnsv->bhndv', k * jnp.exp(b_last - b), v)
    decay = jnp.exp(b_last[:, :, :, 0, :])

    def step(S, inp):
        kv_n, dec_n = inp
        return dec_n[..., None] * S + kv_n, S

    S0 = jnp.zeros((Bn, H, DK, DV), jnp.float32)
    _, S_prev = lax.scan(step, S0, (jnp.moveaxis(kv, 2, 0), jnp.moveaxis(decay, 2, 0)))
    S_prev = jnp.moveaxis(S_prev, 0, 2)
    o_inter = jnp.einsum('bhntd,bhndv->bhntv', q * jnp.exp(b), S_prev)
    return (o_intra + o_inter).reshape(Bn, H, T, DV)


def hgrn2_bidirectional(q, i_in, f_fw_pre, f_bw_pre, lb):
    Bn, T, _ = q.shape

    def heads(a):
        return a.reshape(Bn, T, A_HEADS, A_HEAD_DIM).transpose(0, 2, 1, 3)

    def gate(pre):
        return lb + (1.0 - lb) * jax.nn.sigmoid(pre.astype(jnp.float32))

    f_fw = gate(f_fw_pre)
    f_bw = gate(f_bw_pre)
    qh, ih = heads(q), heads(i_in)
    o_fw = hgrn2_scan(qh, heads(1.0 - f_fw), ih, heads(jnp.log(f_fw)))

    def flip(a):
        return jnp.flip(a, axis=2)

    o_bw = flip(hgrn2_scan(flip(qh), flip(heads(1.0 - f_bw)), flip(ih), flip(heads(jnp.log(f_bw)))))
    return (o_fw + o_bw).transpose(0, 2, 1, 3)


def even_mixer(x, w_in, lb, a_norm_g, b_ln_g, b_ln_b, b_ws, b_bias, w_out):
    Bn, T, _ = x.shape
    h = x @ w_in
    cuts = [A_WIDTH, 2 * A_WIDTH, 3 * A_WIDTH, 4 * A_WIDTH, 5 * A_WIDTH, 5 * A_WIDTH + B_WIDTH]
    q, f_fw, f_bw, i_in, g, u, v = jnp.split(h, cuts, axis=-1)
    o = hgrn2_bidirectional(q, i_in, f_fw, f_bw, lb).astype(x.dtype)
    o = rms_norm(o, a_norm_g.reshape(A_HEADS, A_HEAD_DIM)).reshape(Bn, T, A_WIDTH)
    o_a = o * jax.nn.silu(g)
    v = layer_norm(v, b_ln_g, b_ln_b)
    N = T // B_CHUNK
    v = v.reshape(Bn, N, B_CHUNK, B_GROUPS, B_GROUP_DIM)
    s = jnp.einsum('gts,bnsgc->bntgc', b_ws, v) + b_bias.T[None, None, :, :, None]
    o_b = (u.reshape(Bn, N, B_CHUNK, B_GROUPS, B_GROUP_DIM) * s).reshape(Bn, T, B_WIDTH)
    return jnp.concatenate([o_a, o_b], axis=-1) @ w_out


def axial_rope_angles(T):
    rows = T // GRID_W
    r = jnp.repeat(jnp.arange(rows, dtype=jnp.float32), GRID_W)
    c = jnp.tile(jnp.arange(GRID_W, dtype=jnp.float32), rows)
    half = C_HEAD_DIM // 2
    inv_freq = jnp.exp(-math.log(ROPE_THETA) * jnp.arange(0, half, 2, dtype=jnp.float32) / half)
    return r[:, None] * inv_freq, c[:, None] * inv_freq


def rope_rotate(x, ang):
    n2 = x.shape[-1] // 2
    cos = jnp.cos(ang)[:, None, :].astype(x.dtype)
    sin = jnp.sin(ang)[:, None, :].astype(x.dtype)
    x1, x2 = x[..., :n2], x[..., n2:]
    return jnp.concatenate([x1 * cos - x2 * sin, x2 * cos + x1 * sin], axis=-1)


def apply_axial_rope(x, ang_r, ang_c):
    half = C_HEAD_DIM // 2
    return jnp.concatenate([rope_rotate(x[..., :half], ang_r), rope_rotate(x[..., half:], ang_c)], axis=-1)


def odd_mixer(x, w_in, q_g, k_g, w_out):
    Bn, T, _ = x.shape
    h = x @ w_in
    qd, kd = C_Q_HEADS * C_HEAD_DIM, C_KV_HEADS * C_HEAD_DIM
    q, k, v = jnp.split(h, [qd, qd + kd], axis=-1)
    q = q.reshape(Bn, T, C_Q_HEADS, C_HEAD_DIM)
    k = k.reshape(Bn, T, C_KV_HEADS, C_HEAD_DIM)
    v = v.reshape(Bn, T, C_KV_HEADS, C_HEAD_DIM)
    ang_r, ang_c = axial_rope_angles(T)
    q = apply_axial_rope(rms_norm(q, q_g), ang_r, ang_c)
    k = apply_axial_rope(rms_norm(k, k_g), ang_r, ang_c)
    kk = k.transpose(0, 2, 1, 3)
    vv = v.transpose(0, 2, 1, 3)
    NB = T // C_Q_BLOCK
    qb = q.reshape(Bn, T, C_KV_HEADS, C_GROUP, C_HEAD_DIM).transpose(0, 2, 3, 1, 4)
    qb = jnp.moveaxis(qb.reshape(Bn, C_KV_HEADS, C_GROUP, NB, C_Q_BLOCK, C_HEAD_DIM), 3, 0)
    scale = C_HEAD_DIM ** -0.5

    def attend(q_blk):
        s = jnp.einsum('bkgqd,bksd->bkgqs', q_blk, kk).astype(jnp.float32) * scale
        p = jax.nn.softmax(s, axis=-1)
        return jnp.einsum('bkgqs,bksd->bkgqd', p.astype(vv.dtype), vv)

    out = lax.map(attend, qb)
    out = out.transpose(1, 0, 4, 2, 3, 5).reshape(Bn, T, MIX1_WIDTH)
    return out @ w_out


def conv_ffn(x, w_up, conv_w, conv_b, w_down):
    T = x.shape[1]
    h = x @ w_up
    pad = CONV_W // 2
    hp = jnp.pad(h, ((0, 0), (pad, pad), (0, 0)))
    h = sum(hp[:, j:j + T, :] * conv_w[j] for j in range(CONV_W)) + conv_b
    gate, val = jnp.split(h, 2, axis=-1)
    return (jax.nn.silu(gate) * val) @ w_down


def setup_inputs(seed: int = 0) -> dict:
    key = jax.random.key(seed)
    ks = jax.random.split(key, 24)
    f32 = jnp.float32

    def nrm(k, shape, scale):
        return jax.random.normal(k, shape, f32) * scale

    return {
        'x': nrm(ks[0], (BATCH, SEQ, D_MODEL), 1.0),
        'w_in_ab': nrm(ks[1], (N_EVEN, D_MODEL, IN0_COLS), D_MODEL ** -0.5),
        'hgrn_lb_table': nrm(ks[2], (DEPTH + 1, A_WIDTH), 0.1),
        'hgrn_norm_g': 1.0 + nrm(ks[3], (N_EVEN, A_WIDTH), 0.02),
        'gmlp_ln_g': 1.0 + nrm(ks[4], (N_EVEN, B_WIDTH), 0.02),
        'gmlp_ln_b': nrm(ks[5], (N_EVEN, B_WIDTH), 0.02),
        'gmlp_ws': nrm(ks[6], (N_EVEN, B_GROUPS, B_CHUNK, B_CHUNK), B_CHUNK ** -0.5),
        'gmlp_bias': 1.0 + nrm(ks[7], (N_EVEN, B_GROUPS, B_CHUNK), 0.02),
        'w_out_ab': nrm(ks[8], (N_EVEN, MIX0_WIDTH, D_MODEL), BETA * MIX0_WIDTH ** -0.5),
        'w_in_attn': nrm(ks[9], (N_ODD, D_MODEL, IN1_COLS), D_MODEL ** -0.5),
        'q_norm_g': 1.0 + nrm(ks[10], (N_ODD, C_HEAD_DIM), 0.02),
        'k_norm_g': 1.0 + nrm(ks[11], (N_ODD, C_HEAD_DIM), 0.02),
        'w_out_attn': nrm(ks[12], (N_ODD, MIX1_WIDTH, D_MODEL), BETA * MIX1_WIDTH ** -0.5),
        'ffn_up': nrm(ks[13], (DEPTH, D_MODEL, 2 * D_FF), D_MODEL ** -0.5),
        'ffn_conv_w': nrm(ks[14], (DEPTH, CONV_W, 2 * D_FF), CONV_W ** -0.5),
        'ffn_conv_b': nrm(ks[15], (DEPTH, 2 * D_FF), 0.02),
        'ffn_down': nrm(ks[16], (DEPTH, D_FF, D_MODEL), BETA * D_FF ** -0.5),
        'ln1_g': 1.0 + nrm(ks[17], (DEPTH, D_MODEL), 0.02),
        'ln1_b': nrm(ks[18], (DEPTH, D_MODEL), 0.02),
        'ln2_g': 1.0 + nrm(ks[19], (DEPTH, D_MODEL), 0.02),
        'ln2_b': nrm(ks[20], (DEPTH, D_MODEL), 0.02),
    }


def reference(x, w_in_ab, hgrn_lb_table, hgrn_norm_g, gmlp_ln_g, gmlp_ln_b, gmlp_ws, gmlp_bias,
              w_out_ab, w_in_attn, q_norm_g, k_norm_g, w_out_attn, ffn_up, ffn_conv_w, ffn_conv_b,
              ffn_down, ln1_g, ln1_b, ln2_g, ln2_b):
    lb_all = jnp.cumsum(jax.nn.softmax(hgrn_lb_table.astype(jnp.float32), axis=0), axis=0)
    for layer in range(DEPTH):
        j = layer // 2
        if layer % 2 == 0:
            mix = even_mixer(x, w_in_ab[j], lb_all[layer], hgrn_norm_g[j], gmlp_ln_g[j], gmlp_ln_b[j],
                             gmlp_ws[j], gmlp_bias[j], w_out_ab[j])
        else:
            mix = odd_mixer(x, w_in_attn[j], q_norm_g[j], k_norm_g[j], w_out_attn[j])
        x = layer_norm(ALPHA * x + mix, ln1_g[layer], ln1_b[layer])
        x = layer_norm(ALPHA * x + conv_ffn(x, ffn_up[layer], ffn_conv_w[layer], ffn_conv_b[layer], ffn_down[layer]),
                       ln2_g[layer], ln2_b[layer])
    return x
```

```python
import numpy as np
from contextlib import ExitStack
import concourse.bass as bass
import concourse.mybir as mybir
from concourse.bass_utils import run_bass_kernel_spmd

F32 = mybir.dt.float32
BF16 = mybir.dt.bfloat16
AF = mybir.ActivationFunctionType
ALU = mybir.AluOpType
AX = mybir.AxisListType

SEM_EPOCH = 24000
DMA_LANES = 6


class Op:
    pass


class Prog:
    ENGS = ("pe", "act", "dve", "pool", "sp")

    def __init__(self, nc, same_engine_sync=True):
        self.nc = nc
        self.ops = []
        self.last_w = {}
        self.readers = {}
        self.same_engine_sync = same_engine_sync
        self.stack = ExitStack()
        self.psum_banks = []

    def sbuf(self, name, shape, dtype):
        return self.stack.enter_context(self.nc.sbuf_tensor(name, list(shape), dtype))

    def psum(self, name, shape, dtype=F32):
        return self.stack.enter_context(self.nc.psum_tensor(name, list(shape), dtype))

    def add(self, eng, fn, reads=(), writes=(), dma=False, final=False):
        op = Op()
        op.eng = eng
        op.fn = fn
        op.is_dma = dma
        op.sig = False
        op.sem = None
        op.val = 0
        op.final = final
        op.idx = len(self.ops)
        deps = set()
        for k in reads:
            w = self.last_w.get(k)
            if w is not None:
                deps.add(w)
        for k in writes:
            w = self.last_w.get(k)
            if w is not None:
                deps.add(w)
            for r in self.readers.get(k, {}).values():
                deps.add(r)
        deps.discard(op.idx)
        op.deps = deps
        for k in reads:
            rd = self.readers.setdefault(k, {})
            if dma:
                rd[("dma", op.idx)] = op.idx
            else:
                rd[eng] = op.idx
        for k in writes:
            self.last_w[k] = op.idx
            self.readers[k] = {}
        self.ops.append(op)
        return op

    def dma(self, eng, out, in_, reads=(), writes=(), final=False, **kw):
        return self.add(eng, lambda e: e.dma_start(out=out, in_=in_, **kw), reads, writes, dma=True, final=final)

    def _needs_wait(self, op, d):
        if d.is_dma:
            return True
        if d.eng != op.eng:
            return True
        if op.is_dma:
            return True
        if op.eng == "pe":
            return False
        return self.same_engine_sync

    def emit(self):
        nc = self.nc
        ops = self.ops
        for op in ops:
            for di in op.deps:
                d = ops[di]
                if self._needs_wait(op, d):
                    d.sig = True
            if op.final:
                op.sig = True
        sems = {}

        def get_sem(name):
            if name not in sems:
                sems[name] = self.stack.enter_context(nc.semaphore(name))
            return sems[name]

        cnt = {}
        dma_n = {}
        lane_prev = {}
        for op in ops:
            if op.is_dma:
                n = dma_n.get(op.eng, 0)
                dma_n[op.eng] = n + 1
                lane = n % DMA_LANES
                key = (op.eng, lane)
                c = cnt.get(key, 0) + 1
                cnt[key] = c
                ep = (c * 16) // SEM_EPOCH
                base = 0
                kk = (op.eng, lane, ep)
                c2 = cnt.get(kk, 0) + 1
                cnt[kk] = c2
                op.sem = "d_%s_%d_%d" % (op.eng, lane, ep)
                op.val = 16 * c2
                op.prev_lane = lane_prev.get(key)
                lane_prev[key] = op.idx
                op.sig = True
            elif op.sig:
                c = cnt.get(op.eng, 0) + 1
                cnt[op.eng] = c
                ep = c // SEM_EPOCH
                kk = (op.eng, "e", ep)
                c2 = cnt.get(kk, 0) + 1
                cnt[kk] = c2
                op.sem = "c_%s_%d" % (op.eng, ep)
                op.val = c2
        for op in ops:
            if op.sem is not None:
                get_sem(op.sem)
        self.nsig = sum(1 for o in ops if o.sig)
        by_eng = {e: [o for o in ops if o.eng == e] for e in self.ENGS}
        finals = [o for o in ops if o.final]
        self.nwaits = 0
        prog = self

        def emit_engine(ename, e):
            known = {}

            def wait(d):
                if known.get(d.sem, 0) >= d.val:
                    return
                known[d.sem] = d.val
                e.wait_ge(sems[d.sem], d.val)
                prog.nwaits += 1

            for op in by_eng[ename]:
                for di in sorted(op.deps):
                    d = ops[di]
                    if prog._needs_wait(op, d):
                        wait(d)
                if op.is_dma and getattr(op, "prev_lane", None) is not None:
                    wait(ops[op.prev_lane])
                ins = op.fn(e)
                if op.sig:
                    ins.then_inc(sems[op.sem], 16 if op.is_dma else 1)
            if ename == "sp":
                for o in finals:
                    wait(o)

        with nc.Block() as block:
            @block.tensor
            def _(e):
                emit_engine("pe", e)

            @block.scalar
            def _(e):
                emit_engine("act", e)

            @block.vector
            def _(e):
                emit_engine("dve", e)

            @block.gpsimd
            def _(e):
                emit_engine("pool", e)

            @block.sync
            def _(e):
                emit_engine("sp", e)
        self.stack.close()


D = 2048
T = 4096
NB = 2
TOK = 1024
DFF = 5632
ALPHA = (2.0 * 2) ** 0.25
LN_EPS = 1e-5
RMS_EPS = 1e-6
NCORES = 8


def ln_tile(P, y_ap, ykey, g_bc, b_bc, out_ap, okey, st, mv, tagkey, eps=LN_EPS):
    for c in range(4):
        P.add("dve", lambda e, c=c: e.bn_stats(out=st[:, c, :], in_=y_ap[:, c * 512:(c + 1) * 512]),
              reads=[ykey], writes=[tagkey + "st"])
    P.add("dve", lambda e: e.bn_aggr(out=mv[:, 0:2], in_=st[:].rearrange("p a b -> p (a b)")),
          reads=[tagkey + "st"], writes=[tagkey + "mv"])
    P.add("dve", lambda e: e.tensor_scalar(out=mv[:, 2:3], in0=mv[:, 1:2], scalar1=eps, scalar2=None, op0=ALU.add),
          reads=[tagkey + "mv"], writes=[tagkey + "mv"])
    P.add("act", lambda e: e.activation(out=mv[:, 2:3], in_=mv[:, 2:3], func=AF.Sqrt),
          reads=[tagkey + "mv"], writes=[tagkey + "mv"])
    P.add("dve", lambda e: e.reciprocal(out=mv[:, 3:4], in_=mv[:, 2:3]),
          reads=[tagkey + "mv"], writes=[tagkey + "mv"])
    P.add("dve", lambda e: e.tensor_scalar(out=y_ap, in0=y_ap, scalar1=mv[:, 0:1], scalar2=mv[:, 3:4],
                                           op0=ALU.subtract, op1=ALU.mult),
          reads=[ykey, tagkey + "mv"], writes=[ykey])
    P.add("dve", lambda e: e.tensor_tensor(out=y_ap, in0=y_ap, in1=g_bc, op=ALU.mult),
          reads=[ykey, "lnp"], writes=[ykey])
    P.add("dve", lambda e: e.tensor_tensor(out=out_ap, in0=y_ap, in1=b_bc, op=ALU.add),
          reads=[ykey, "lnp"], writes=[okey])


def build_ffn():
    nc = bass.Bass("TRN2", target_bir_lowering=False)
    x1 = nc.dram_tensor("x1", [TOK, D], F32, kind="ExternalInput").ap()
    x1T = nc.dram_tensor("x1T", [D, TOK + 2], F32, kind="ExternalInput").ap()
    w_up = nc.dram_tensor("w_up", [D, 2 * DFF], F32, kind="ExternalInput").ap()
    cwb = nc.dram_tensor("cwb", [128, 88, 4], F32, kind="ExternalInput").ap()
    w_down = nc.dram_tensor("w_down", [DFF, D], F32, kind="ExternalInput").ap()
    lng = nc.dram_tensor("lng", [128, D], F32, kind="ExternalInput").ap()
    lnb = nc.dram_tensor("lnb", [128, D], F32, kind="ExternalInput").ap()
    y = nc.dram_tensor("y", [TOK, D], F32, kind="ExternalOutput").ap()
    P = Prog(nc)
    G = P.sbuf("G", [128, 44, TOK], BF16)
    XY = P.sbuf("XY", [128, 16 * (TOK + 2)], BF16)
    XT = XY[:].rearrange("p (k t) -> p k t", k=16)
    Y = XY[:, 0:16384].bitcast(F32).rearrange("p (t d) -> p t d", t=4)
    WB = [P.sbuf("WB%d" % i, [128, 44 * 256], BF16) for i in range(2)]
    H = [[P.sbuf("H%d_%d" % (gv, s), [128, 514], F32) for s in range(2)] for gv in range(2)]
    TT = [[P.sbuf("T%d_%d" % (gv, s), [128, 512], F32) for s in range(2)] for gv in range(2)]
    CW = P.sbuf("CW", [128, 88, 4], F32)
    LG = P.sbuf("LG", [128, D], F32)
    LB = P.sbuf("LB", [128, D], F32)
    ST = P.sbuf("ST", [128, 4, 6], F32)
    MV = P.sbuf("MV", [128, 4], F32)
    PS = [P.psum("ps%d" % i, [128, 512]) for i in range(8)]

    P.dma("sp", CW[:], cwb, writes=["CW"])
    P.dma("sp", LG[:], lng, writes=["lnp"])
    P.dma("sp", LB[:], lnb, writes=["lnp2"])
    x1T_v = x1T.rearrange("(k p) t -> p k t", p=128)
    for k in range(16):
        P.dma("pool", XT[:, k, :], x1T_v[:, k, :], writes=["XY"])
    w_up_v = w_up.rearrange("(k p) n -> p k n", p=128)
    NG = 22
    ps_i = 0
    for grp in range(NG):
        wb = WB[grp % 2]
        wv = wb[:, 0:2 * 16 * 256].rearrange("p (g k n) -> p g k n", g=2, k=16)
        wkey = "WB%d" % (grp % 2)
        for gv in range(2):
            c0 = gv * DFF + grp * 256
            P.dma("pool", wv[:, gv, :, :], w_up_v[:, :, c0:c0 + 256], writes=[wkey])
        for cc in range(2):
            c = grp * 2 + cc
            for blk in range(2):
                slot = (c * 2 + blk) % 2
                for gv in range(2):
                    pm = PS[ps_i % 3]
                    ph = PS[3 + ps_i % 3]
                    pmk = "ps%d" % (ps_i % 3)
                    phk = "ps%d" % (3 + ps_i % 3)
                    ps_i += 1
                    t0 = 1 + blk * 512
                    for k in range(16):
                        P.add("pe", lambda e, k=k, pm=pm, gv=gv, cc=cc, t0=t0, wv=wv: e.matmul(
                            pm[:], lhsT=wv[:, gv, k, cc * 128:(cc + 1) * 128], rhs=XT[:, k, t0:t0 + 512],
                            start=(k == 0), stop=(k == 15)), reads=[wkey, "XY"], writes=[pmk])
                    for k in range(16):
                        P.add("pe", lambda e, k=k, ph=ph, gv=gv, cc=cc, t0=t0, wv=wv: e.matmul(
                            ph[:, 0:2], lhsT=wv[:, gv, k, cc * 128:(cc + 1) * 128], rhs=XT[:, k, t0 - 1:t0 + 513:513],
                            start=(k == 0), stop=(k == 15)), reads=[wkey, "XY"], writes=[phk])
                    h = H[gv][slot]
                    hk = "H%d_%d" % (gv, slot)
                    P.add("act", lambda e, h=h, pm=pm: e.activation(out=h[:, 1:513], in_=pm[:], func=AF.Copy),
                          reads=[pmk], writes=[hk])
                    P.add("act", lambda e, h=h, ph=ph: e.activation(out=h[:, 0:514:513], in_=ph[:, 0:2], func=AF.Copy),
                          reads=[phk], writes=[hk])
                    tt = TT[gv][slot]
                    tk = "T%d_%d" % (gv, slot)
                    ch = gv * 44 + c
                    P.add("dve", lambda e, tt=tt, h=h, ch=ch: e.tensor_scalar(
                        out=tt[:], in0=h[:, 0:512], scalar1=CW[:, ch, 0:1], scalar2=CW[:, ch, 3:4],
                        op0=ALU.mult, op1=ALU.add), reads=[hk, "CW"], writes=[tk])
                    for j in (1, 2):
                        P.add("dve", lambda e, tt=tt, h=h, ch=ch, j=j: e.scalar_tensor_tensor(
                            out=tt[:], in0=h[:, j:j + 512], scalar=CW[:, ch, j:j + 1], in1=tt[:],
                            op0=ALU.mult, op1=ALU.add), reads=[hk, "CW", tk], writes=[tk])
                tg = TT[0][slot]
                tv = TT[1][slot]
                P.add("act", lambda e, tg=tg: e.activation(out=tg[:], in_=tg[:], func=AF.Silu),
                      reads=["T0_%d" % slot], writes=["T0_%d" % slot])
                P.add("pool", lambda e, tg=tg, tv=tv, c=c, blk=blk: e.tensor_tensor(
                    out=G[:, c, blk * 512:(blk + 1) * 512], in0=tg[:], in1=tv[:], op=ALU.mult),
                    reads=["T0_%d" % slot, "T1_%d" % slot], writes=[("G", c)])
    w_down_v = w_down.rearrange("(k p) n -> p k n", p=128)
    x1_v = x1.rearrange("(t p) d -> p t d", p=128)
    y_v = y.rearrange("(t p) d -> p t d", p=128)
    Gkeys = [("G", c) for c in range(44)]
    li = 0
    for half in range(2):
        P.dma("sp", Y[:], x1_v[:, half * 4:(half + 1) * 4, :], writes=["XY"])
        for cb in range(8):
            wb = WB[li % 2]
            wkey = "WB%d" % (li % 2)
            li += 1
            wv = wb[:].rearrange("p (k n) -> p k n", k=44)
            P.dma("pool", wv[:, 0:22, :], w_down_v[:, 0:22, cb * 256:(cb + 1) * 256], writes=[wkey])
            P.dma("pool", wv[:, 22:44, :], w_down_v[:, 22:44, cb * 256:(cb + 1) * 256], writes=[wkey + "b"])
            for t4 in range(4):
                tt = half * 4 + t4
                pi = 6 + (ps_i % 2)
                ps_i += 1
                pm = PS[pi]
                pmk = "ps%d" % pi
                for k in range(44):
                    P.add("pe", lambda e, k=k, pm=pm, tt=tt, wv=wv: e.matmul(
                        pm[:, 0:256], lhsT=G[:, k, tt * 128:(tt + 1) * 128], rhs=wv[:, k, :],
                        start=(k == 0), stop=(k == 43)), reads=[wkey, wkey + "b", ("G", k)], writes=[pmk])
                ysl = Y[:, t4, cb * 256:(cb + 1) * 256]
                P.add("dve", lambda e, ysl=ysl, pm=pm: e.scalar_tensor_tensor(
                    out=ysl, in0=ysl, scalar=ALPHA, in1=pm[:, 0:256], op0=ALU.mult, op1=ALU.add),
                    reads=[pmk, "XY"], writes=["XY"])
        for t4 in range(4):
            tt = half * 4 + t4
            ln_tile(P, Y[:, t4, :], "XY", LG[:], LB[:], Y[:, t4, :], "XY", ST, MV, "ln")
            P.dma("sp", y_v[:, tt, :], Y[:, t4, :], reads=["XY"], final=True)
    P.emit()
    return nc, P


def _halo_T(xf, c):
    b, j = divmod(c, 4)
    t0 = j * TOK
    out = np.zeros((xf.shape[2], TOK + 2), np.float32)
    lo = max(t0 - 1, 0)
    hi = min(t0 + TOK + 1, T)
    out[:, lo - (t0 - 1):hi - (t0 - 1)] = xf[b, lo:hi, :].T
    return out


def _bc(v):
    return np.ascontiguousarray(np.broadcast_to(np.asarray(v, np.float32)[None, :], (128, v.shape[0])))


_NC_CACHE = {}


def _get(name, builder):
    if name not in _NC_CACHE:
        _NC_CACHE[name] = builder()[0]
    return _NC_CACHE[name]


def run_ffn(xf, w_up, conv_w, conv_b, w_down, g, b):
    nc = _get("ffn", build_ffn)
    cwb = np.concatenate([conv_w, conv_b[None, :]], axis=0)
    cwb = np.ascontiguousarray(cwb.reshape(4, 88, 128).transpose(2, 1, 0))
    w_up = np.ascontiguousarray(w_up)
    w_down = np.ascontiguousarray(w_down)
    lg, lb = _bc(g), _bc(b)
    in_maps = []
    for c in range(NCORES):
        bb, j = divmod(c, 4)
        in_maps.append({"x1": np.ascontiguousarray(xf[bb, j * TOK:(j + 1) * TOK, :]), "x1T": _halo_T(xf, c),
                        "w_up": w_up, "cwb": cwb, "w_down": w_down, "lng": lg, "lnb": lb})
    res = run_bass_kernel_spmd(nc, in_maps, core_ids=list(range(NCORES)))
    out = np.empty_like(xf)
    for c in range(NCORES):
        bb, j = divmod(c, 4)
        out[bb, j * TOK:(j + 1) * TOK, :] = res.results[c]["y"]
    return out


def build_proj_ln(KC):
    nc = bass.Bass("TRN2", target_bir_lowering=False)
    x1 = nc.dram_tensor("x1", [TOK, D], F32, kind="ExternalInput").ap()
    aT = nc.dram_tensor("aT", [KC * 128, TOK], F32, kind="ExternalInput").ap()
    w = nc.dram_tensor("w", [KC * 128, D], F32, kind="ExternalInput").ap()
    lng = nc.dram_tensor("lng", [128, D], F32, kind="ExternalInput").ap()
    lnb = nc.dram_tensor("lnb", [128, D], F32, kind="ExternalInput").ap()
    y = nc.dram_tensor("y", [TOK, D], F32, kind="ExternalOutput").ap()
    P = Prog(nc)
    G = P.sbuf("G", [128, KC, TOK], BF16)
    Y = P.sbuf("Y", [128, 8, D], F32)
    WB = [P.sbuf("WB%d" % i, [128, KC, 512], BF16) for i in range(2)]
    LG = P.sbuf("LG", [128, D], F32)
    LB = P.sbuf("LB", [128, D], F32)
    ST = P.sbuf("ST", [128, 4, 6], F32)
    MV = P.sbuf("MV", [128, 4], F32)
    PS = [P.psum("ps%d" % i, [128, 512]) for i in range(4)]
    P.dma("sp", LG[:], lng, writes=["lnp"])
    P.dma("sp", LB[:], lnb, writes=["lnp2"])
    aT_v = aT.rearrange("(k p) t -> p k t", p=128)
    for k in range(KC):
        P.dma("pool", G[:, k, :], aT_v[:, k, :], writes=[("G", k)])
    w_v = w.rearrange("(k p) n -> p k n", p=128)
    x1_v = x1.rearrange("(t p) d -> p t d", p=128)
    y_v = y.rearrange("(t p) d -> p t d", p=128)
    for tt in range(8):
        P.dma("sp", Y[:, tt, :], x1_v[:, tt, :], writes=[("Y", tt)])
    ps_i = 0
    for cb in range(4):
        wb = WB[cb % 2]
        wkey = "WB%d" % (cb % 2)
        P.dma("pool", wb[:], w_v[:, :, cb * 512:(cb + 1) * 512], writes=[wkey])
        for tt in range(8):
            pi = ps_i % 4
            ps_i += 1
            pm = PS[pi]
            pmk = "ps%d" % pi
            for k in range(KC):
                P.add("pe", lambda e, k=k, pm=pm, tt=tt, wb=wb: e.matmul(
                    pm[:], lhsT=G[:, k, tt * 128:(tt + 1) * 128], rhs=wb[:, k, :],
                    start=(k == 0), stop=(k == KC - 1)), reads=[wkey, ("G", k)], writes=[pmk])
            ysl = Y[:, tt, cb * 512:(cb + 1) * 512]
            P.add("dve", lambda e, ysl=ysl, pm=pm: e.scalar_tensor_tensor(
                out=ysl, in0=ysl, scalar=ALPHA, in1=pm[:], op0=ALU.mult, op1=ALU.add),
                reads=[pmk, ("Y", tt)], writes=[("Y", tt)])
    for tt in range(8):
        ln_tile(P, Y[:, tt, :], ("Y", tt), LG[:], LB[:], Y[:, tt, :], ("Y", tt), ST, MV, "ln")
        P.dma("sp", y_v[:, tt, :], Y[:, tt, :], reads=[("Y", tt)], final=True)
    P.emit()
    return nc, P


def _shard_tok(xf, c):
    b, j = divmod(c, 4)
    return np.ascontiguousarray(xf[b, j * TOK:(j + 1) * TOK, :])


def _gather_tok(res, name, width):
    out = np.empty((NB, T, width), np.float32)
    for c in range(NCORES):
        b, j = divmod(c, 4)
        out[b, j * TOK:(j + 1) * TOK, :] = res.results[c][name]
    return out


def run_proj_ln(xf, af, w, g, b):
    KC = af.shape[2] // 128
    nc = _get("proj_ln%d" % KC, lambda: build_proj_ln(KC))
    w = np.ascontiguousarray(w)
    lg, lb = _bc(g), _bc(b)
    in_maps = []
    for c in range(NCORES):
        in_maps.append({"x1": _shard_tok(xf, c), "aT": np.ascontiguousarray(_shard_tok(af, c).T),
                        "w": w, "lng": lg, "lnb": lb})
    res = run_bass_kernel_spmd(nc, in_maps, core_ids=list(range(NCORES)))
    return _gather_tok(res, "y", D)


def emit_proj_tm(P, XT, xkey, w_v, NCOLS, WB, PS, sink):
    ps_i = 0
    for cb in range(NCOLS // 512):
        wb = WB[cb % 2]
        wkey = "WB%d" % (cb % 2)
        P.dma("pool", wb[:], w_v[:, :, cb * 512:(cb + 1) * 512], writes=[wkey])
        for tt in range(8):
            pi = ps_i % len(PS)
            ps_i += 1
            pm = PS[pi]
            pmk = "ps%d" % pi
            for k in range(16):
                P.add("pe", lambda e, k=k, pm=pm, tt=tt, wb=wb: e.matmul(
                    pm[:], lhsT=XT[:, k, tt * 128:(tt + 1) * 128], rhs=wb[:, k, :],
                    start=(k == 0), stop=(k == 15)), reads=[wkey, xkey], writes=[pmk])
            sink(P, cb, tt, pm, pmk)


def build_proj(NCOLS):
    nc = bass.Bass("TRN2", target_bir_lowering=False)
    xT = nc.dram_tensor("xT", [D, TOK], F32, kind="ExternalInput").ap()
    w = nc.dram_tensor("w", [D, NCOLS], F32, kind="ExternalInput").ap()
    h = nc.dram_tensor("h", [TOK, NCOLS], F32, kind="ExternalOutput").ap()
    P = Prog(nc)
    XT = P.sbuf("XT", [128, 16, TOK], BF16)
    WB = [P.sbuf("WB%d" % i, [128, 16, 512], BF16) for i in range(2)]
    OB = [P.sbuf("OB%d" % i, [128, 512], F32) for i in range(4)]
    PS = [P.psum("ps%d" % i, [128, 512]) for i in range(4)]
    xT_v = xT.rearrange("(k p) t -> p k t", p=128)
    for k in range(16):
        P.dma("pool", XT[:, k, :], xT_v[:, k, :], writes=["XT"])
    w_v = w.rearrange("(k p) n -> p k n", p=128)
    h_v = h.rearrange("(t p) n -> p t n", p=128)
    cnt = [0]

    def sink(P, cb, tt, pm, pmk):
        i = cnt[0] % 4
        cnt[0] += 1
        ob = OB[i]
        P.add("act", lambda e: e.activation(out=ob[:], in_=pm[:], func=AF.Copy), reads=[pmk], writes=["OB%d" % i])
        P.dma("sp", h_v[:, tt, cb * 512:(cb + 1) * 512], ob[:], reads=["OB%d" % i], final=True)

    emit_proj_tm(P, XT, "XT", w_v, NCOLS, WB, PS, sink)
    P.emit()
    return nc, P


def run_proj(xf, w):
    NCOLS = w.shape[1]
    nc = _get("proj%d" % NCOLS, lambda: build_proj(NCOLS))
    w = np.ascontiguousarray(w)
    in_maps = [{"xT": np.ascontiguousarray(_shard_tok(xf, c).T), "w": w} for c in range(NCORES)]
    res = run_bass_kernel_spmd(nc, in_maps, core_ids=list(range(NCORES)))
    return _gather_tok(res, "h", NCOLS)


NQH, NKH, HD = 16, 4, 128
QKV = (NQH + 2 * NKH) * HD
NRM = (NQH + NKH) * HD


def build_qkv():
    nc = bass.Bass("TRN2", target_bir_lowering=False)
    xT = nc.dram_tensor("xT", [D, TOK], F32, kind="ExternalInput").ap()
    w = nc.dram_tensor("w", [D, QKV], F32, kind="ExternalInput").ap()
    cos = nc.dram_tensor("cos", [TOK, HD], F32, kind="ExternalInput").ap()
    sinp = nc.dram_tensor("sinp", [TOK, HD], F32, kind="ExternalInput").ap()
    gains = nc.dram_tensor("gains", [128, 4, HD], F32, kind="ExternalInput").ap()
    h = nc.dram_tensor("h", [TOK, QKV], F32, kind="ExternalOutput").ap()
    P = Prog(nc)
    XT = P.sbuf("XT", [128, 16, TOK], BF16)
    WB = [P.sbuf("WB%d" % i, [128, 16, 512], BF16) for i in range(2)]
    H = P.sbuf("H", [128, 8, QKV], F32)
    TMP = P.sbuf("TMP", [128, NRM], F32)
    TMP2 = P.sbuf("TMP2", [128, NRM], F32)
    CS = P.sbuf("CS", [128, 8, HD], F32)
    SN = P.sbuf("SN", [128, 8, HD], F32)
    GN = P.sbuf("GN", [128, 4, HD], F32)
    TAB = P.sbuf("TAB", [128, 4, 8, HD], F32)
    SS = P.sbuf("SS", [128, 24], F32)
    PS = [P.psum("ps%d" % i, [128, 512]) for i in range(4)]
    xT_v = xT.rearrange("(k p) t -> p k t", p=128)
    for k in range(16):
        P.dma("pool", XT[:, k, :], xT_v[:, k, :], writes=["XT"])
    P.dma("sp", CS[:], cos.rearrange("(t p) f -> p t f", p=128), writes=["CS"])
    P.dma("sp", SN[:], sinp.rearrange("(t p) f -> p t f", p=128), writes=["SN"])
    P.dma("sp", GN[:], gains, writes=["GN"])
    for i in range(4):
        src = CS if i % 2 == 0 else SN
        P.add("dve", lambda e, i=i, src=src: e.tensor_tensor(
            out=TAB[:, i, :, :], in0=src[:], in1=GN[:, i, :].unsqueeze(1).to_broadcast([128, 8, HD]), op=ALU.mult),
            reads=["CS", "SN", "GN"], writes=["TAB"])
    w_v = w.rearrange("(k p) n -> p k n", p=128)
    h_v = h.rearrange("(t p) n -> p t n", p=128)

    def sink(P, cb, tt, pm, pmk):
        P.add("act", lambda e: e.activation(out=H[:, tt, cb * 512:(cb + 1) * 512], in_=pm[:], func=AF.Copy),
              reads=[pmk], writes=[("H", tt)])

    emit_proj_tm(P, XT, "XT", w_v, QKV, WB, PS, sink)
    NH = NQH + NKH
    for tt in range(8):
        X = H[:, tt, 0:NRM]
        hk = ("H", tt)
        P.add("dve", lambda e, X=X: e.tensor_tensor(out=TMP[:], in0=X, in1=X, op=ALU.mult), reads=[hk], writes=["TMP"])
        P.add("dve", lambda e: e.tensor_reduce(out=SS[:, 0:NH], in_=TMP[:].rearrange("p (h f) -> p h f", f=HD),
                                               axis=AX.X, op=ALU.add), reads=["TMP"], writes=["SS"])
        P.add("dve", lambda e: e.tensor_scalar(out=SS[:, 0:NH], in0=SS[:, 0:NH], scalar1=1.0 / HD, scalar2=RMS_EPS,
                                               op0=ALU.mult, op1=ALU.add), reads=["SS"], writes=["SS"])
        P.add("act", lambda e: e.activation(out=SS[:, 0:NH], in_=SS[:, 0:NH], func=AF.Sqrt), reads=["SS"], writes=["SS"])
        P.add("dve", lambda e: e.reciprocal(out=SS[:, 0:NH], in_=SS[:, 0:NH]), reads=["SS"], writes=["SS"])
        P.add("dve", lambda e, X=X: e.tensor_tensor(
            out=TMP[:].rearrange("p (h f) -> p h f", f=HD), in0=X.rearrange("p (h f) -> p h f", f=HD),
            in1=SS[:, 0:NH].unsqueeze(2).to_broadcast([128, NH, HD]), op=ALU.mult), reads=[hk, "SS"], writes=["TMP"])
        for (h0, nh, ti) in ((0, NQH, 0), (NQH, NKH, 2)):
            xn = TMP[:, h0 * HD:(h0 + nh) * HD]
            xo = X[:, h0 * HD:(h0 + nh) * HD]
            t2 = TMP2[:, h0 * HD:(h0 + nh) * HD]
            P.add("dve", lambda e, xn=xn, xo=xo, nh=nh, ti=ti, tt=tt: e.tensor_tensor(
                out=xo.rearrange("p (h f) -> p h f", f=HD), in0=xn.rearrange("p (h f) -> p h f", f=HD),
                in1=TAB[:, ti, tt, :].unsqueeze(1).to_broadcast([128, nh, HD]), op=ALU.mult),
                reads=["TMP", "TAB"], writes=[hk])
            for s in range(2):
                P.add("dve", lambda e, xn=xn, t2=t2, nh=nh, ti=ti, tt=tt, s=s: e.tensor_tensor(
                    out=t2.rearrange("p (h a s f) -> p h a s f", a=2, s=2, f=32)[:, :, :, s, :],
                    in0=xn.rearrange("p (h a s f) -> p h a s f", a=2, s=2, f=32)[:, :, :, 1 - s, :],
                    in1=TAB[:, ti + 1, tt, :].rearrange("p (a s f) -> p a s f", a=2, s=2)[:, :, s, :]
                        .unsqueeze(1).to_broadcast([128, nh, 2, 32]), op=ALU.mult),
                    reads=["TMP", "TAB"], writes=["TMP2"])
            P.add("dve", lambda e, xo=xo, t2=t2: e.tensor_tensor(out=xo, in0=xo, in1=t2, op=ALU.add),
                  reads=[hk, "TMP2"], writes=[hk])
        P.dma("sp", h_v[:, tt, :], H[:, tt, :], reads=[hk], final=True)
    P.emit()
    return nc, P


def _rope_tables():
    rows = T // 64
    r = np.repeat(np.arange(rows, dtype=np.float32), 64)
    cc = np.tile(np.arange(64, dtype=np.float32), rows)
    half = HD // 2
    inv = np.exp(-np.log(np.float32(10000.0)) * np.arange(0, half, 2, dtype=np.float32) / np.float32(half)).astype(np.float32)
    ar = (r[:, None] * inv).astype(np.float32)
    ac = (cc[:, None] * inv).astype(np.float32)
    cos = np.concatenate([np.cos(ar), np.cos(ar), np.cos(ac), np.cos(ac)], axis=1).astype(np.float32)
    sinp = np.concatenate([-np.sin(ar), np.sin(ar), -np.sin(ac), np.sin(ac)], axis=1).astype(np.float32)
    return cos, sinp


def _swap32(g):
    return np.concatenate([g[32:64], g[0:32], g[96:128], g[64:96]])


def run_qkv(xf, w, gq, gk):
    nc = _get("qkv", build_qkv)
    w = np.ascontiguousarray(w)
    cos, sinp = _rope_tables()
    gains = np.stack([gq, _swap32(gq), gk, _swap32(gk)], axis=0).astype(np.float32)
    gains = np.ascontiguousarray(np.broadcast_to(gains[None], (128, 4, HD)))
    in_maps = []
    for c in range(NCORES):
        b, j = divmod(c, 4)
        in_maps.append({"xT": np.ascontiguousarray(_shard_tok(xf, c).T), "w": w,
                        "cos": np.ascontiguousarray(cos[j * TOK:(j + 1) * TOK]),
                        "sinp": np.ascontiguousarray(sinp[j * TOK:(j + 1) * TOK]), "gains": gains})
    res = run_bass_kernel_spmd(nc, in_maps, core_ids=list(range(NCORES)))
    return _gather_tok(res, "h", QKV)


VW = 130


def build_attn():
    nc = bass.Bass("TRN2", target_bir_lowering=False)
    qT = nc.dram_tensor("qT", [NQH * HD, TOK], F32, kind="ExternalInput").ap()
    kT = nc.dram_tensor("kT", [NKH * HD, T], F32, kind="ExternalInput").ap()
    v = nc.dram_tensor("v", [T, NKH * HD], F32, kind="ExternalInput").ap()
    o = nc.dram_tensor("o", [TOK, NQH * HD], F32, kind="ExternalOutput").ap()
    P = Prog(nc)
    QT = P.sbuf("QT", [128, NQH, TOK], BF16)
    KT = P.sbuf("KT", [128, NKH, T], BF16)
    VE = P.sbuf("VE", [128, 32, NKH, VW], BF16)
    PT = [P.sbuf("PT%d" % i, [128, 512], BF16) for i in range(3)]
    OB = [P.sbuf("OB%d" % i, [128, NQH * HD], F32) for i in range(2)]
    RC = P.sbuf("RC", [128, 8], F32)
    PS = [P.psum("ps%d" % i, [128, 512]) for i in range(8)]
    qT_v = qT.rearrange("(h p) t -> p h t", p=128)
    for h in range(NQH):
        P.dma("pool", QT[:, h, :], qT_v[:, h, :], writes=["QT"])
    kT_v = kT.rearrange("(h p) t -> p h t", p=128)
    for h in range(NKH):
        for half in range(2):
            P.dma("pool", KT[:, h, half * 2048:(half + 1) * 2048], kT_v[:, h, half * 2048:(half + 1) * 2048], writes=["KT"])
    P.add("dve", lambda e: e.memset(VE[:].rearrange("p c k w -> p (c k w)"), 1.0), writes=["VE"])
    v_v = v.rearrange("(c p) (k f) -> p c k f", p=128, k=NKH)
    for kv in range(NKH):
        P.dma("pool", VE[:, :, kv, 0:HD], v_v[:, :, kv, :], writes=["VE"])
    o_v = o.rearrange("(t p) n -> p t n", p=128)
    scale = float(HD) ** -0.5
    si = 0
    for qt in range(8):
        ob = OB[qt % 2]
        obk = "OB%d" % (qt % 2)
        for kv in range(NKH):
            for sc in range(32):
                sb = 4 + si % 3
                st = PS[sb]
                stk = "ps%d" % sb
                pt = PT[si % 3]
                ptk = "PT%d" % (si % 3)
                si += 1
                P.add("pe", lambda e, st=st, kv=kv, sc=sc, qt=qt: e.matmul(
                    st[:].rearrange("p (h q) -> p h q", h=4), lhsT=KT[:, kv, sc * 128:(sc + 1) * 128],
                    rhs=QT[:, 4 * kv:4 * kv + 4, qt * 128:(qt + 1) * 128], start=True, stop=True),
                    reads=["QT", "KT"], writes=[stk])
                P.add("act", lambda e, st=st, pt=pt: e.activation(out=pt[:], in_=st[:], func=AF.Exp, scale=scale),
                      reads=[stk], writes=[ptk])
                for hh in range(4):
                    P.add("pe", lambda e, hh=hh, pt=pt, sc=sc, kv=kv: e.matmul(
                        PS[hh][:, 0:VW], lhsT=pt[:, hh * 128:(hh + 1) * 128], rhs=VE[:, sc, kv, :],
                        start=(sc == 0), stop=(sc == 31)), reads=[ptk, "VE"], writes=["ps%d" % hh])
            for hh in range(4):
                hq = 4 * kv + hh
                P.add("dve", lambda e, hh=hh: e.reciprocal(out=RC[:, hh:hh + 1], in_=PS[hh][:, HD:HD + 1]),
                      reads=["ps%d" % hh], writes=[("RC", hh)])
                P.add("dve", lambda e, hh=hh, hq=hq, ob=ob: e.tensor_scalar(
                    out=ob[:, hq * HD:(hq + 1) * HD], in0=PS[hh][:, 0:HD], scalar1=RC[:, hh:hh + 1], scalar2=None,
                    op0=ALU.mult), reads=["ps%d" % hh, ("RC", hh)], writes=[obk])
        P.dma("sp", o_v[:, qt, :], ob[:], reads=[obk], final=True)
    P.emit()
    return nc, P


def run_attn(qkv):
    nc = _get("attn", build_attn)
    in_maps = []
    for c in range(NCORES):
        b, j = divmod(c, 4)
        in_maps.append({"qT": np.ascontiguousarray(qkv[b, j * TOK:(j + 1) * TOK, 0:2048].T),
                        "kT": np.ascontiguousarray(qkv[b, :, 2048:2560].T),
                        "v": np.ascontiguousarray(qkv[b, :, 2560:3072])})
    res = run_bass_kernel_spmd(nc, in_maps, core_ids=list(range(NCORES)))
    return _gather_tok(res, "o", NQH * HD)


GW = 1024


def build_gmlp():
    nc = bass.Bass("TRN2", target_bir_lowering=False)
    u = nc.dram_tensor("u", [TOK, GW], F32, kind="ExternalInput").ap()
    v = nc.dram_tensor("v", [TOK, GW], F32, kind="ExternalInput").ap()
    wsT = nc.dram_tensor("wsT", [8, 128, 128], F32, kind="ExternalInput").ap()
    biasT = nc.dram_tensor("biasT", [128, 8], F32, kind="ExternalInput").ap()
    lng = nc.dram_tensor("lng", [128, GW], F32, kind="ExternalInput").ap()
    lnb = nc.dram_tensor("lnb", [128, GW], F32, kind="ExternalInput").ap()
    ob = nc.dram_tensor("ob", [TOK, GW], F32, kind="ExternalOutput").ap()
    P = Prog(nc)
    U = P.sbuf("U", [128, 8, GW], F32)
    V = P.sbuf("V", [128, 8, GW], F32)
    VN = P.sbuf("VN", [128, 8, GW], BF16)
    WS = P.sbuf("WS", [128, 8, 128], BF16)
    BI = P.sbuf("BI", [128, 8], F32)
    LG = P.sbuf("LG", [128, GW], F32)
    LB = P.sbuf("LB", [128, GW], F32)
    ST = P.sbuf("ST", [128, 2, 6], F32)
    MV = P.sbuf("MV", [128, 4], F32)
    PS = [P.psum("ps%d" % i, [128, 512]) for i in range(4)]
    P.dma("sp", U[:], u.rearrange("(t p) n -> p t n", p=128), writes=["U"])
    P.dma("sp", V[:], v.rearrange("(t p) n -> p t n", p=128), writes=["V"])
    P.dma("pool", WS[:], wsT.rearrange("g s t -> s g t"), writes=["WS"])
    P.dma("sp", BI[:], biasT, writes=["BI"])
    P.dma("sp", LG[:], lng, writes=["lnp"])
    P.dma("sp", LB[:], lnb, writes=["lnp2"])
    ob_v = ob.rearrange("(t p) n -> p t n", p=128)
    ps_i = 0
    for n in range(8):
        y = V[:, n, :]
        yk = ("V", n)
        for c in range(2):
            P.add("dve", lambda e, c=c, y=y: e.bn_stats(out=ST[:, c, :], in_=y[:, c * 512:(c + 1) * 512]),
                  reads=["V"], writes=["st"])
        P.add("dve", lambda e: e.bn_aggr(out=MV[:, 0:2], in_=ST[:].rearrange("p a b -> p (a b)")), reads=["st"], writes=["mv"])
        P.add("dve", lambda e: e.tensor_scalar(out=MV[:, 2:3], in0=MV[:, 1:2], scalar1=LN_EPS, scalar2=None, op0=ALU.add),
              reads=["mv"], writes=["mv"])
        P.add("act", lambda e: e.activation(out=MV[:, 2:3], in_=MV[:, 2:3], func=AF.Sqrt), reads=["mv"], writes=["mv"])
        P.add("dve", lambda e: e.reciprocal(out=MV[:, 3:4], in_=MV[:, 2:3]), reads=["mv"], writes=["mv"])
        P.add("dve", lambda e, y=y: e.tensor_scalar(out=y, in0=y, scalar1=MV[:, 0:1], scalar2=MV[:, 3:4],
                                                    op0=ALU.subtract, op1=ALU.mult), reads=["V", "mv"], writes=["V"])
        P.add("dve", lambda e, y=y: e.tensor_tensor(out=y, in0=y, in1=LG[:], op=ALU.mult), reads=["V", "lnp"], writes=["V"])
        P.add("dve", lambda e, y=y, n=n: e.tensor_tensor(out=VN[:, n, :], in0=y, in1=LB[:], op=ALU.add),
              reads=["V", "lnp2"], writes=[("VN", n)])
        for g4 in range(2):
            pi = ps_i % 4
            ps_i += 1
            pm = PS[pi]
            pmk = "ps%d" % pi
            for gg in range(4):
                g = g4 * 4 + gg
                P.add("pe", lambda e, pm=pm, g=g, gg=gg, n=n: e.matmul(
                    pm[:, gg * 128:(gg + 1) * 128], lhsT=WS[:, g, :], rhs=VN[:, n, g * 128:(g + 1) * 128],
                    start=True, stop=True), reads=["WS", ("VN", n)], writes=[pmk])
            for gg in range(4):
                g = g4 * 4 + gg
                usl = U[:, n, g * 128:(g + 1) * 128]
                P.add("dve", lambda e, pm=pm, g=g, gg=gg, usl=usl: e.scalar_tensor_tensor(
                    out=usl, in0=pm[:, gg * 128:(gg + 1) * 128], scalar=BI[:, g:g + 1], in1=usl,
                    op0=ALU.add, op1=ALU.mult), reads=[pmk, "BI", ("U", n)], writes=[("U", n)])
        P.dma("sp", ob_v[:, n, :], U[:, n, :], reads=[("U", n), "U"], final=True)
    P.emit()
    return nc, P


def run_gmlp(h0, ws, bias, g, b):
    nc = _get("gmlp", build_gmlp)
    wsT = np.ascontiguousarray(ws.transpose(0, 2, 1))
    biasT = np.ascontiguousarray(bias.T)
    lg, lb = _bc(g), _bc(b)
    in_maps = []
    for c in range(NCORES):
        hc = _shard_tok(h0, c)
        in_maps.append({"u": np.ascontiguousarray(hc[:, 5120:6144]), "v": np.ascontiguousarray(hc[:, 6144:7168]),
                        "wsT": wsT, "biasT": biasT, "lng": lg, "lnb": lb})
    res = run_bass_kernel_spmd(nc, in_maps, core_ids=list(range(NCORES)))
    return _gather_tok(res, "ob", GW)


HD_ORDER = (0, 1)


def build_hgrn():
    nc = bass.Bass("TRN2", target_bir_lowering=False)
    qT = nc.dram_tensor("qT", [2, 128, T], F32, kind="ExternalInput").ap()
    ffT = nc.dram_tensor("ffT", [2, 128, T], F32, kind="ExternalInput").ap()
    fbT = nc.dram_tensor("fbT", [2, 128, T], F32, kind="ExternalInput").ap()
    iv = nc.dram_tensor("iv", [2, T, 128], F32, kind="ExternalInput").ap()
    gg = nc.dram_tensor("gg", [2, T, 128], F32, kind="ExternalInput").ap()
    lbt = nc.dram_tensor("lbt", [2, 128, 3], F32, kind="ExternalInput").ap()
    gn = nc.dram_tensor("gn", [2, 64, 128], F32, kind="ExternalInput").ap()
    masks = nc.dram_tensor("masks", [2, 128, 128], F32, kind="ExternalInput").ap()
    rmask = nc.dram_tensor("rmask", [128, 512], F32, kind="ExternalInput").ap()
    ident = nc.dram_tensor("ident", [128, 128], F32, kind="ExternalInput").ap()
    oa = nc.dram_tensor("oa", [2, T, 128], F32, kind="ExternalOutput").ap()
    P = Prog(nc)
    SEG = 512
    MK = P.sbuf("MK", [128, 2, 128], F32)
    RM = P.sbuf("RM", [128, SEG], F32)
    ID = P.sbuf("ID", [128, 128], BF16)
    GN = P.sbuf("GN", [64, 2, 128], F32)
    LBT = P.sbuf("LBT", [128, 2, 3], F32)
    LBV = P.sbuf("LBV", [128, 2, 4], F32)
    Qs = [P.sbuf("Qs%d" % i, [128, SEG], F32) for i in range(2)]
    Fs = [P.sbuf("Fs%d" % i, [128, SEG], F32) for i in range(2)]
    Vs = [P.sbuf("Vs%d" % i, [128, 4, 128], BF16) for i in range(2)]
    Gs = [P.sbuf("Gs%d" % i, [64, 8, 128], F32) for i in range(2)]
    Fg = P.sbuf("Fg", [128, SEG], F32)
    Kt = P.sbuf("Kt", [128, SEG], F32)
    Bt = P.sbuf("Bt", [128, SEG], F32)
    Tm = P.sbuf("Tm", [128, SEG], F32)
    Et = P.sbuf("Et", [128, SEG], F32)
    QS = [P.sbuf("QS%d" % i, [128, SEG], BF16) for i in range(2)]
    KS = [P.sbuf("KS%d" % i, [128, SEG], BF16) for i in range(2)]
    KP = [P.sbuf("KP%d" % i, [128, SEG], BF16) for i in range(2)]
    QB = [P.sbuf("QB%d" % i, [128, SEG], BF16) for i in range(2)]
    DEC = [P.sbuf("DEC%d" % i, [128, 8], F32) for i in range(2)]
    AT = [P.sbuf("AT%d" % i, [128, 128], BF16) for i in range(2)]
    KPT = [P.sbuf("KPT%d" % i, [128, 128], BF16) for i in range(2)]
    S32 = P.sbuf("S32", [128, 128], F32)
    SBF = P.sbuf("SBF", [128, 128], BF16)
    OF = P.sbuf("OF", [64, 64, 128], F32)
    OS = [P.sbuf("OS%d" % i, [64, 8, 128], F32) for i in range(2)]
    SQ = P.sbuf("SQ", [64, 8, 128], F32)
    SSQ = P.sbuf("SSQ", [64, 8], F32)
    PSC = [P.psum("psc%d" % i, [128, 512]) for i in range(2)]
    PTP = [P.psum("ptp%d" % i, [128, 1024], BF16) for i in range(2)]
    PO = [P.psum("po%d" % i, [128, 512]) for i in range(2)]
    PKV = [P.psum("pkv%d" % i, [128, 512]) for i in range(2)]

    P.dma("sp", MK[:], masks.rearrange("m s t -> s m t"), writes=["MK"])
    P.dma("sp", RM[:], rmask, writes=["RM"])
    P.dma("pool", ID[:], ident, writes=["ID"])
    P.dma("sp", GN[:], gn.rearrange("h p f -> p h f"), writes=["GN"])
    P.dma("sp", LBT[:], lbt.rearrange("h p f -> p h f"), writes=["LBT"])
    P.add("act", lambda e: e.activation(out=LBT[:], in_=LBT[:], func=AF.Exp), reads=["LBT"], writes=["LBT"])
    P.add("dve", lambda e: e.tensor_reduce(out=LBV[:, :, 0], in_=LBT[:], axis=AX.X, op=ALU.add), reads=["LBT"], writes=["LBV"])
    P.add("dve", lambda e: e.reciprocal(out=LBV[:, :, 1], in_=LBV[:, :, 0]), reads=["LBV"], writes=["LBV"])
    P.add("dve", lambda e: e.tensor_tensor(out=LBV[:, :, 2], in0=LBT[:, :, 0], in1=LBV[:, :, 1], op=ALU.mult),
          reads=["LBV", "LBT"], writes=["LBV"])
    P.add("dve", lambda e: e.tensor_scalar(out=LBV[:, :, 3], in0=LBV[:, :, 2], scalar1=-1.0, scalar2=1.0,
                                           op0=ALU.mult, op1=ALU.add), reads=["LBV"], writes=["LBV"])
    si = 0
    bi = 0
    ci = 0
    for hd in HD_ORDER:
        lb = LBV[:, hd, 2:3]
        oml = LBV[:, hd, 3:4]
        for dr in range(2):
            P.add("dve", lambda e: e.memset(S32[:], 0.0), writes=["S32"])
            P.add("dve", lambda e: e.memset(SBF[:], 0.0), writes=["SBF"])
            fsrc = ffT if dr == 0 else fbT
            mid = 31 if dr == 0 else 32
            last = 63 if dr == 0 else 0
            for seg in (range(8) if dr == 0 else range(7, -1, -1)):
                p = si % 2
                si += 1
                t0 = seg * SEG
                q_s, f_s, v_s, g_s = Qs[p], Fs[p], Vs[p], Gs[p]
                qk, fk, vk, gk = "Qs%d" % p, "Fs%d" % p, "Vs%d" % p, "Gs%d" % p
                P.dma("sp", q_s[:], qT[hd, :, t0:t0 + SEG], writes=[qk])
                P.dma("sp", f_s[:], fsrc[hd, :, t0:t0 + SEG], writes=[fk])
                P.dma("pool", v_s[:], iv[hd, t0:t0 + SEG, :].rearrange("(b p) v -> p b v", p=128), writes=[vk])
                if dr == 1:
                    P.dma("sp", g_s[:], gg[hd, t0:t0 + SEG, :].rearrange("(c p) v -> p c v", p=64), writes=[gk])
                qs, ks, kp, qb, dec = QS[p], KS[p], KP[p], QB[p], DEC[p]
                qsk, ksk, kpk, qbk, deck = "QS%d" % p, "KS%d" % p, "KP%d" % p, "QB%d" % p, "DEC%d" % p
                P.add("act", lambda e, f_s=f_s: e.activation(out=Fg[:], in_=f_s[:], func=AF.Sigmoid), reads=[fk], writes=["Fg"])
                P.add("dve", lambda e, oml=oml, lb=lb: e.tensor_scalar(out=Fg[:], in0=Fg[:], scalar1=oml, scalar2=lb, op0=ALU.mult, op1=ALU.add),
                      reads=["Fg", "LBV"], writes=["Fg"])
                P.add("dve", lambda e: e.tensor_scalar(out=Kt[:], in0=Fg[:], scalar1=-1.0, scalar2=1.0, op0=ALU.mult, op1=ALU.add),
                      reads=["Fg"], writes=["Kt"])
                P.add("act", lambda e: e.activation(out=Fg[:], in_=Fg[:], func=AF.Ln), reads=["Fg"], writes=["Fg"])
                P.add("dve", lambda e: e.tensor_tensor_scan(out=Bt[:], data0=RM[:], data1=Fg[:], initial=0.0,
                                                            op0=ALU.mult, op1=ALU.add), reads=["RM", "Fg"], writes=["Bt"])
                B3 = Bt[:].rearrange("p (c t) -> p c t", t=64)
                T3 = Tm[:].rearrange("p (c t) -> p c t", t=64)
                F3 = Fg[:].rearrange("p (c t) -> p c t", t=64)
                if dr == 1:
                    P.add("dve", lambda e, B3=B3, T3=T3: e.tensor_tensor(
                        out=T3, in0=B3[:, :, 63:64].to_broadcast([128, 8, 64]), in1=B3, op=ALU.subtract),
                        reads=["Bt"], writes=["Tm"])
                    P.add("dve", lambda e: e.tensor_tensor(out=Bt[:], in0=Tm[:], in1=Fg[:], op=ALU.add),
                          reads=["Tm", "Fg"], writes=["Bt"])
                P.add("dve", lambda e, B3=B3, T3=T3, mid=mid: e.tensor_tensor(
                    out=T3, in0=B3, in1=B3[:, :, mid:mid + 1].to_broadcast([128, 8, 64]), op=ALU.subtract),
                    reads=["Bt"], writes=["Tm"])
                P.add("act", lambda e: e.activation(out=Et[:], in_=Tm[:], func=AF.Exp), reads=["Tm"], writes=["Et"])
                P.add("dve", lambda e, qs=qs, q_s=q_s: e.tensor_tensor(out=qs[:], in0=q_s[:], in1=Et[:], op=ALU.mult),
                      reads=[qk, "Et"], writes=[qsk])
                P.add("act", lambda e: e.activation(out=Et[:], in_=Tm[:], func=AF.Exp, scale=-1.0), reads=["Tm"], writes=["Et"])
                P.add("dve", lambda e, ks=ks: e.tensor_tensor(out=ks[:], in0=Kt[:], in1=Et[:], op=ALU.mult),
                      reads=["Kt", "Et"], writes=[ksk])
                P.add("dve", lambda e, B3=B3, T3=T3, last=last: e.tensor_tensor(
                    out=T3, in0=B3[:, :, last:last + 1].to_broadcast([128, 8, 64]), in1=B3, op=ALU.subtract),
                    reads=["Bt"], writes=["Tm"])
                P.add("act", lambda e: e.activation(out=Et[:], in_=Tm[:], func=AF.Exp), reads=["Tm"], writes=["Et"])
                P.add("dve", lambda e, kp=kp: e.tensor_tensor(out=kp[:], in0=Kt[:], in1=Et[:], op=ALU.mult),
                      reads=["Kt", "Et"], writes=[kpk])
                P.add("act", lambda e: e.activation(out=Et[:], in_=Bt[:], func=AF.Exp), reads=["Bt"], writes=["Et"])
                P.add("dve", lambda e, qb=qb, q_s=q_s: e.tensor_tensor(out=qb[:], in0=q_s[:], in1=Et[:], op=ALU.mult),
                      reads=[qk, "Et"], writes=[qbk])
                E3v = Et[:].rearrange("p (c t) -> p c t", t=64)
                P.add("dve", lambda e, dec=dec, E3v=E3v, last=last: e.tensor_copy(out=dec[:], in_=E3v[:, :, last]),
                      reads=["Et"], writes=[deck])
                os_ = OS[p]
                osk = "OS%d" % p
                for blk in (range(4) if dr == 0 else range(3, -1, -1)):
                    b2 = bi % 2
                    bi += 1
                    psc, ptp = PSC[b2], PTP[b2]
                    at, kpt = AT[b2], KPT[b2]
                    atk, kptk = "AT%d" % b2, "KPT%d" % b2
                    cs = slice(blk * 128, (blk + 1) * 128)
                    P.add("pe", lambda e, psc=psc, ks=ks, qs=qs, cs=cs: e.matmul(
                        psc[:, 0:128], lhsT=ks[:, cs], rhs=qs[:, cs], start=True, stop=True),
                        reads=[ksk, qsk], writes=["psc%d" % b2])
                    P.add("dve", lambda e, at=at, psc=psc, dr=dr: e.tensor_tensor(
                        out=at[:], in0=psc[:, 0:128], in1=MK[:, dr, :], op=ALU.mult),
                        reads=["psc%d" % b2, "MK"], writes=[atk])
                    P.add("pe", lambda e, ptp=ptp, kp=kp, cs=cs: e.transpose(ptp[:, 0:128], kp[:, cs], ID[:]),
                          reads=[kpk, "ID"], writes=["ptp%d" % b2])
                    P.add("act", lambda e, kpt=kpt, ptp=ptp: e.activation(out=kpt[:], in_=ptp[:, 0:128], func=AF.Copy),
                          reads=["ptp%d" % b2], writes=[kptk])
                    for c in ((0, 1) if dr == 0 else (1, 0)):
                        c2 = ci % 2
                        ci += 1
                        po, pkv = PO[c2], PKV[c2]
                        cl = blk * 2 + c
                        n = seg * 8 + cl
                        rs = slice(64 * c, 64 * c + 64)
                        P.add("pe", lambda e, po=po, at=at, v_s=v_s, rs=rs, blk=blk: e.matmul(
                            po[0:64, 0:128], lhsT=at[rs, rs], rhs=v_s[rs, blk, :], start=True, stop=False),
                            reads=[atk, vk], writes=["po%d" % c2])
                        P.add("pe", lambda e, po=po, qb=qb, cl=cl: e.matmul(
                            po[0:64, 0:128], lhsT=qb[:, cl * 64:(cl + 1) * 64], rhs=SBF[:], start=False, stop=True),
                            reads=[qbk, "SBF"], writes=["po%d" % c2])
                        P.add("pe", lambda e, pkv=pkv, kpt=kpt, v_s=v_s, rs=rs, blk=blk: e.matmul(
                            pkv[:, 0:128], lhsT=kpt[rs, :], rhs=v_s[rs, blk, :], start=True, stop=True),
                            reads=[kptk, vk], writes=["pkv%d" % c2])
                        P.add("dve", lambda e, pkv=pkv, dec=dec, cl=cl: e.scalar_tensor_tensor(
                            out=S32[:], in0=S32[:], scalar=dec[:, cl:cl + 1], in1=pkv[:, 0:128], op0=ALU.mult, op1=ALU.add),
                            reads=["S32", deck, "pkv%d" % c2], writes=["S32"])
                        P.add("act", lambda e: e.activation(out=SBF[:], in_=S32[:], func=AF.Copy), reads=["S32"], writes=["SBF"])
                        if dr == 0:
                            P.add("act", lambda e, po=po, n=n: e.activation(out=OF[:, n, :], in_=po[0:64, 0:128], func=AF.Copy),
                                  reads=["po%d" % c2], writes=[("OF", n)])
                        else:
                            P.add("dve", lambda e, po=po, n=n, cl=cl, os_=os_: e.tensor_tensor(
                                out=os_[:, cl, :], in0=po[0:64, 0:128], in1=OF[:, n, :], op=ALU.add),
                                reads=["po%d" % c2, ("OF", n)], writes=[osk])
                if dr == 1:
                    P.add("dve", lambda e, os_=os_: e.tensor_tensor(out=SQ[:], in0=os_[:], in1=os_[:], op=ALU.mult),
                          reads=[osk], writes=["SQ"])
                    P.add("dve", lambda e: e.tensor_reduce(out=SSQ[:], in_=SQ[:], axis=AX.X, op=ALU.add), reads=["SQ"], writes=["SSQ"])
                    P.add("dve", lambda e: e.tensor_scalar(out=SSQ[:], in0=SSQ[:], scalar1=1.0 / 128, scalar2=RMS_EPS,
                                                           op0=ALU.mult, op1=ALU.add), reads=["SSQ"], writes=["SSQ"])
                    P.add("act", lambda e: e.activation(out=SSQ[:], in_=SSQ[:], func=AF.Sqrt), reads=["SSQ"], writes=["SSQ"])
                    P.add("dve", lambda e: e.reciprocal(out=SSQ[:], in_=SSQ[:]), reads=["SSQ"], writes=["SSQ"])
                    P.add("dve", lambda e, os_=os_: e.tensor_tensor(
                        out=os_[:], in0=os_[:], in1=SSQ[:].unsqueeze(2).to_broadcast([64, 8, 128]), op=ALU.mult),
                        reads=[osk, "SSQ"], writes=[osk])
                    P.add("dve", lambda e, os_=os_, hd=hd: e.tensor_tensor(
                        out=os_[:], in0=os_[:], in1=GN[:, hd, :].unsqueeze(1).to_broadcast([64, 8, 128]), op=ALU.mult),
                        reads=[osk, "GN"], writes=[osk])
                    P.add("act", lambda e, g_s=g_s: e.activation(out=g_s[:], in_=g_s[:], func=AF.Silu), reads=[gk], writes=[gk])
                    P.add("dve", lambda e, os_=os_, g_s=g_s: e.tensor_tensor(out=os_[:], in0=os_[:], in1=g_s[:], op=ALU.mult),
                          reads=[osk, gk], writes=[osk])
                    P.dma("sp", oa[hd, t0:t0 + SEG, :].rearrange("(c p) v -> p c v", p=64), os_[:], reads=[osk], final=True)
    P.emit()
    return nc, P


def run_hgrn(h0, lb_table, norm_g):
    nc = _get("hgrn", build_hgrn)
    idx = np.arange(128)
    same = (idx[:, None] // 64) == (idx[None, :] // 64)
    mfw = (same & (idx[:, None] <= idx[None, :])).astype(np.float32)
    mbw = (same & (idx[:, None] >= idx[None, :])).astype(np.float32)
    masks = np.stack([mfw, mbw], axis=0)
    rmask = np.ones((128, 512), np.float32)
    rmask[:, ::64] = 0.0
    ident = np.eye(128, dtype=np.float32)
    in_maps = []
    for c in range(NCORES):
        b, j = divmod(c, 4)
        hs = [2 * j, 2 * j + 1]

        def colsT(base):
            return np.ascontiguousarray(np.stack([h0[b, :, base + H * 128: base + (H + 1) * 128].T for H in hs], axis=0))

        def cols(base):
            return np.ascontiguousarray(np.stack([h0[b, :, base + H * 128: base + (H + 1) * 128] for H in hs], axis=0))

        in_maps.append({
            "qT": colsT(0), "ffT": colsT(1024), "fbT": colsT(2048), "iv": cols(3072), "gg": cols(4096),
            "lbt": np.ascontiguousarray(np.stack([lb_table[:, H * 128:(H + 1) * 128].T for H in hs], axis=0)),
            "gn": np.ascontiguousarray(np.stack([np.broadcast_to(norm_g[H * 128:(H + 1) * 128][None, :], (64, 128)) for H in hs], axis=0)),
            "masks": masks, "rmask": rmask, "ident": ident})
    res = run_bass_kernel_spmd(nc, in_maps, core_ids=list(range(NCORES)))
    out = np.empty((NB, T, GW), np.float32)
    for c in range(NCORES):
        b, j = divmod(c, 4)
        for hd in range(2):
            H = 2 * j + hd
            out[b, :, H * 128:(H + 1) * 128] = res.results[c]["oa"][hd]
    return out


def kernel(x, w_in_ab, hgrn_lb_table, hgrn_norm_g, gmlp_ln_g, gmlp_ln_b, gmlp_ws, gmlp_bias,
           w_out_ab, w_in_attn, q_norm_g, k_norm_g, w_out_attn, ffn_up, ffn_conv_w, ffn_conv_b,
           ffn_down, ln1_g, ln1_b, ln2_g, ln2_b):
    f = lambda a: np.asarray(a, dtype=np.float32)
    x = f(x)
    h0 = run_proj(x, f(w_in_ab)[0])
    oa = run_hgrn(h0, f(hgrn_lb_table), f(hgrn_norm_g)[0])
    ob = run_gmlp(h0, f(gmlp_ws)[0], f(gmlp_bias)[0], f(gmlp_ln_g)[0], f(gmlp_ln_b)[0])
    ocat = np.concatenate([oa, ob], axis=2)
    x1 = run_proj_ln(x, ocat, f(w_out_ab)[0], f(ln1_g)[0], f(ln1_b)[0])
    x2 = run_ffn(x1, f(ffn_up)[0], f(ffn_conv_w)[0], f(ffn_conv_b)[0], f(ffn_down)[0], f(ln2_g)[0], f(ln2_b)[0])
    qkv = run_qkv(x2, f(w_in_attn)[0], f(q_norm_g)[0], f(k_norm_g)[0])
    o = run_attn(qkv)
    x3 = run_proj_ln(x2, o, f(w_out_attn)[0], f(ln1_g)[1], f(ln1_b)[1])
    x4 = run_ffn(x3, f(ffn_up)[1], f(ffn_conv_w)[1], f(ffn_conv_b)[1], f(ffn_down)[1], f(ln2_g)[1], f(ln2_b)[1])
    return x4
```

```python
import numpy as np
from contextlib import ExitStack
import concourse.bass as bass
import concourse.mybir as mybir
from concourse.bass_utils import run_bass_kernel_spmd

F32 = mybir.dt.float32
BF16 = mybir.dt.bfloat16
AF = mybir.ActivationFunctionType
ALU = mybir.AluOpType
AX = mybir.AxisListType

SEM_EPOCH = 24000
DMA_LANES = 6


class Op:
    pass


class Prog:
    ENGS = ("pe", "act", "dve", "pool", "sp")

    def __init__(self, nc, same_engine_sync=True):
        self.nc = nc
        self.ops = []
        self.last_w = {}
        self.readers = {}
        self.same_engine_sync = same_engine_sync
        self.stack = ExitStack()
        self.psum_banks = []

    def _uname(self, name):
        self._ucnt = getattr(self, "_ucnt", 0) + 1
        return "%s_u%d" % (name, self._ucnt)

    def sbuf(self, name, shape, dtype):
        return self.stack.enter_context(self.nc.sbuf_tensor(self._uname(name), list(shape), dtype))

    def psum(self, name, shape, dtype=F32):
        return self.stack.enter_context(self.nc.psum_tensor(self._uname(name), list(shape), dtype))

    def add(self, eng, fn, reads=(), writes=(), dma=False, final=False):
        op = Op()
        op.eng = eng
        op.fn = fn
        op.is_dma = dma
        op.sig = False
        op.sem = None
        op.val = 0
        op.final = final
        op.idx = len(self.ops)
        deps = set()
        mykey = ("dma", op.idx) if dma else eng
        for k in reads:
            deps.update(self.last_w.get(k, {}).values())
        for k in writes:
            gen = self.last_w.get(k)
            if gen is None:
                gen = self.last_w[k] = {}
            rd = self.readers.get(k)
            if rd:
                deps.update(rd.values())
                deps.update(gen.values())
                gen.clear()
                rd.clear()
            elif dma:
                deps.update(v for kk, v in gen.items() if not isinstance(kk, tuple))
            else:
                deps.update(gen.values())
        deps.discard(op.idx)
        op.deps = deps
        for k in reads:
            self.readers.setdefault(k, {})[mykey] = op.idx
        for k in writes:
            self.last_w[k][mykey] = op.idx
        self.ops.append(op)
        return op

    def dma(self, eng, out, in_, reads=(), writes=(), final=False, **kw):
        return self.add(eng, lambda e: e.dma_start(out=out, in_=in_, **kw), reads, writes, dma=True, final=final)

    def _needs_wait(self, op, d):
        if d.is_dma:
            return True
        if d.eng != op.eng:
            return True
        if op.is_dma:
            return True
        if op.eng == "pe":
            return False
        return self.same_engine_sync

    def cc(self, fn, reads=(), writes=()):
        op = self.add("pool", fn, reads, writes, dma=True)
        op.is_cc = True
        return op

    def begin_phase(self):
        self.pstack = ExitStack()
        self.main_stack = self.stack
        self.stack = self.pstack
        self.phase_start = len(self.ops)

    def end_phase(self):
        self.emit_ops(self.phase_start, len(self.ops), barrier=True)
        self.pstack.close()
        self.stack = self.main_stack
        self.last_w = {}
        self.readers = {}

    def emit(self):
        self.emit_ops(0, len(self.ops), barrier=False)
        self.stack.close()

    def _sem(self, name):
        if name not in self.sems:
            self.sems[name] = self.main_stack.enter_context(self.nc.semaphore(name)) if hasattr(self, "main_stack") and self.main_stack is not None \
                else self.stack.enter_context(self.nc.semaphore(name))
        return self.sems[name]

    def emit_ops(self, lo, hi, barrier):
        nc = self.nc
        ops = self.ops
        if not hasattr(self, "sems"):
            self.sems = {}
            self.cnt = {}
            self.dma_n = {}
            self.lane_prev = {}
            self.known = {e: {} for e in self.ENGS}
            self.nbar = 0
            self.nsig = 0
            self.nwaits = 0
        cur = ops[lo:hi]
        for op in cur:
            for di in op.deps:
                d = ops[di]
                if self._needs_wait(op, d):
                    d.sig = True
            if op.final:
                op.sig = True
        cnt, dma_n, lane_prev = self.cnt, self.dma_n, self.lane_prev
        for op in cur:
            if getattr(op, "is_cc", False):
                c = cnt.get("cc", 0) + 1
                cnt["cc"] = c
                op.sem = "cc"
                op.val = c
                op.prev_lane = None
                op.sig = True
            elif op.is_dma:
                n = dma_n.get(op.eng, 0)
                dma_n[op.eng] = n + 1
                lane = n % DMA_LANES
                key = (op.eng, lane)
                c = cnt.get(key, 0) + 1
                cnt[key] = c
                ep = (c * 16) // SEM_EPOCH
                kk = (op.eng, lane, ep)
                c2 = cnt.get(kk, 0) + 1
                cnt[kk] = c2
                op.sem = "d_%s_%d_%d" % (op.eng, lane, ep)
                op.val = 16 * c2
                op.prev_lane = lane_prev.get(key)
                lane_prev[key] = op.idx
                op.sig = True
            elif op.sig:
                c = cnt.get(op.eng, 0) + 1
                cnt[op.eng] = c
                ep = c // SEM_EPOCH
                kk = (op.eng, "e", ep)
                c2 = cnt.get(kk, 0) + 1
                cnt[kk] = c2
                op.sem = "c_%s_%d" % (op.eng, ep)
                op.val = c2
        for op in cur:
            if op.sem is not None:
                self._sem(op.sem)
        if barrier:
            self._sem("bar")
            self._sem("bar_sp")
        self.nsig += sum(1 for o in cur if o.sig)
        by_eng = {e: [o for o in cur if o.eng == e] for e in self.ENGS}
        finals = [o for o in cur if o.final]
        prog = self
        sems = self.sems
        nbar = self.nbar
        BSC = self.bar_scratch if barrier else None

        def emit_engine(ename, e):
            known = prog.known[ename]

            def wait(d):
                if known.get(d.sem, 0) >= d.val:
                    return
                known[d.sem] = d.val
                e.wait_ge(sems[d.sem], d.val)
                prog.nwaits += 1

            if nbar > 0:
                e.wait_ge(sems["bar"], 4 * nbar)
                e.wait_ge(sems["bar_sp"], 16 * nbar)
            last_dma = {}
            for op in by_eng[ename]:
                for di in sorted(op.deps):
                    d = ops[di]
                    if prog._needs_wait(op, d):
                        wait(d)
                if op.is_dma and getattr(op, "prev_lane", None) is not None:
                    wait(ops[op.prev_lane])
                ins = op.fn(e)
                if op.sig:
                    if getattr(op, "is_cc", False):
                        ins.then_inc(sems[op.sem])
                    else:
                        ins.then_inc(sems[op.sem], 16 if op.is_dma else 1)
                if op.is_dma:
                    last_dma[op.sem] = op
            if ename == "sp":
                for o in finals:
                    wait(o)
            if barrier:
                for o in last_dma.values():
                    wait(o)
                if ename == "pe":
                    e.matmul(BSC["ps"][0:1, 0:2], lhsT=BSC["bf"][:, 0:1], rhs=BSC["bf"][:, 0:2], start=True, stop=True).then_inc(sems["bar"], 1)
                elif ename == "act":
                    e.activation(out=BSC["a"][:, 0:1], in_=BSC["a"][:, 1:2], func=AF.Copy).then_inc(sems["bar"], 1)
                elif ename == "dve":
                    e.memset(BSC["d"][:, 0:1], 0.0).then_inc(sems["bar"], 1)
                elif ename == "pool":
                    e.memset(BSC["p"][:, 0:1], 0.0).then_inc(sems["bar"], 1)
                elif ename == "sp":
                    e.dma_start(out=BSC["s"][:, 0:1], in_=BSC["s"][:, 1:2]).then_inc(sems["bar_sp"], 16)

        with nc.Block() as block:
            @block.tensor
            def _(e):
                emit_engine("pe", e)

            @block.scalar
            def _(e):
                emit_engine("act", e)

            @block.vector
            def _(e):
                emit_engine("dve", e)

            @block.gpsimd
            def _(e):
                emit_engine("pool", e)

            @block.sync
            def _(e):
                emit_engine("sp", e)
        if barrier:
            self.nbar += 1

    def setup_barrier(self):
        self.main_stack = None
        st = self.stack
        self.bar_scratch = {
            "bf": st.enter_context(self.nc.sbuf_tensor("bar_bf", [128, 2], BF16)),
            "a": st.enter_context(self.nc.sbuf_tensor("bar_a", [128, 2], F32)),
            "d": st.enter_context(self.nc.sbuf_tensor("bar_d", [128, 2], F32)),
            "p": st.enter_context(self.nc.sbuf_tensor("bar_p", [128, 2], F32)),
            "s": st.enter_context(self.nc.sbuf_tensor("bar_s", [128, 2], F32)),
            "ps": st.enter_context(self.nc.psum_tensor("bar_ps", [128, 512], F32)),
        }
        self.main_stack = st


D = 2048
T = 4096
NB = 2
TOK = 1024
DFF = 5632
ALPHA = (2.0 * 2) ** 0.25
LN_EPS = 1e-5
RMS_EPS = 1e-6
NCORES = 8


def ln_tile(P, y_ap, ykey, g_bc, b_bc, out_ap, okey, st, mv, tagkey, eps=LN_EPS):
    for c in range(4):
        P.add("dve", lambda e, c=c: e.bn_stats(out=st[:, c, :], in_=y_ap[:, c * 512:(c + 1) * 512]),
              reads=[ykey], writes=[tagkey + "st"])
    P.add("dve", lambda e: e.bn_aggr(out=mv[:, 0:2], in_=st[:].rearrange("p a b -> p (a b)")),
          reads=[tagkey + "st"], writes=[tagkey + "mv"])
    P.add("dve", lambda e: e.tensor_scalar(out=mv[:, 2:3], in0=mv[:, 1:2], scalar1=eps, scalar2=None, op0=ALU.add),
          reads=[tagkey + "mv"], writes=[tagkey + "mv"])
    P.add("act", lambda e: e.activation(out=mv[:, 2:3], in_=mv[:, 2:3], func=AF.Sqrt),
          reads=[tagkey + "mv"], writes=[tagkey + "mv"])
    P.add("dve", lambda e: e.reciprocal(out=mv[:, 3:4], in_=mv[:, 2:3]),
          reads=[tagkey + "mv"], writes=[tagkey + "mv"])
    P.add("dve", lambda e: e.tensor_scalar(out=y_ap, in0=y_ap, scalar1=mv[:, 0:1], scalar2=mv[:, 3:4],
                                           op0=ALU.subtract, op1=ALU.mult),
          reads=[ykey, tagkey + "mv"], writes=[ykey])
    P.add("dve", lambda e: e.tensor_tensor(out=y_ap, in0=y_ap, in1=g_bc, op=ALU.mult),
          reads=[ykey, "lnp"], writes=[ykey])
    P.add("dve", lambda e: e.tensor_tensor(out=out_ap, in0=y_ap, in1=b_bc, op=ALU.add),
          reads=[ykey, "lnp"], writes=[okey])


def build_ffn():
    nc = bass.Bass("TRN2", target_bir_lowering=False)
    x1 = nc.dram_tensor("x1", [TOK, D], F32, kind="ExternalInput").ap()
    x1T = nc.dram_tensor("x1T", [D, TOK + 2], F32, kind="ExternalInput").ap()
    w_up = nc.dram_tensor("w_up", [D, 2 * DFF], F32, kind="ExternalInput").ap()
    cwb = nc.dram_tensor("cwb", [128, 88, 4], F32, kind="ExternalInput").ap()
    w_down = nc.dram_tensor("w_down", [DFF, D], F32, kind="ExternalInput").ap()
    lng = nc.dram_tensor("lng", [128, D], F32, kind="ExternalInput").ap()
    lnb = nc.dram_tensor("lnb", [128, D], F32, kind="ExternalInput").ap()
    y = nc.dram_tensor("y", [TOK, D], F32, kind="ExternalOutput").ap()
    P = Prog(nc)
    G = P.sbuf("G", [128, 44, TOK], BF16)
    XY = P.sbuf("XY", [128, 16 * (TOK + 2)], BF16)
    XT = XY[:].rearrange("p (k t) -> p k t", k=16)
    Y = XY[:, 0:16384].bitcast(F32).rearrange("p (t d) -> p t d", t=4)
    WB = [P.sbuf("WB%d" % i, [128, 44 * 256], BF16) for i in range(2)]
    H = [[P.sbuf("H%d_%d" % (gv, s), [128, 514], F32) for s in range(2)] for gv in range(2)]
    TT = [[P.sbuf("T%d_%d" % (gv, s), [128, 512], F32) for s in range(2)] for gv in range(2)]
    CW = P.sbuf("CW", [128, 88, 4], F32)
    LG = P.sbuf("LG", [128, D], F32)
    LB = P.sbuf("LB", [128, D], F32)
    ST = P.sbuf("ST", [128, 4, 6], F32)
    MV = P.sbuf("MV", [128, 4], F32)
    PS = [P.psum("ps%d" % i, [128, 512]) for i in range(8)]

    P.dma("sp", CW[:], cwb, writes=["CW"])
    P.dma("sp", LG[:], lng, writes=["lnp"])
    P.dma("sp", LB[:], lnb, writes=["lnp2"])
    x1T_v = x1T.rearrange("(k p) t -> p k t", p=128)
    for k in range(16):
        P.dma("pool", XT[:, k, :], x1T_v[:, k, :], writes=["XY"])
    w_up_v = w_up.rearrange("(k p) n -> p k n", p=128)
    NG = 22
    ps_i = 0
    for grp in range(NG):
        wb = WB[grp % 2]
        wv = wb[:, 0:2 * 16 * 256].rearrange("p (g k n) -> p g k n", g=2, k=16)
        wkey = "WB%d" % (grp % 2)
        for gv in range(2):
            c0 = gv * DFF + grp * 256
            P.dma("pool", wv[:, gv, :, :], w_up_v[:, :, c0:c0 + 256], writes=[wkey])
        for cc in range(2):
            c = grp * 2 + cc
            for blk in range(2):
                slot = (c * 2 + blk) % 2
                for gv in range(2):
                    pm = PS[ps_i % 3]
                    ph = PS[3 + ps_i % 3]
                    pmk = "ps%d" % (ps_i % 3)
                    phk = "ps%d" % (3 + ps_i % 3)
                    ps_i += 1
                    t0 = 1 + blk * 512
                    for k in range(16):
                        P.add("pe", lambda e, k=k, pm=pm, gv=gv, cc=cc, t0=t0, wv=wv: e.matmul(
                            pm[:], lhsT=wv[:, gv, k, cc * 128:(cc + 1) * 128], rhs=XT[:, k, t0:t0 + 512],
                            start=(k == 0), stop=(k == 15)), reads=[wkey, "XY"], writes=[pmk])
                    for k in range(16):
                        P.add("pe", lambda e, k=k, ph=ph, gv=gv, cc=cc, t0=t0, wv=wv: e.matmul(
                            ph[:, 0:2], lhsT=wv[:, gv, k, cc * 128:(cc + 1) * 128], rhs=XT[:, k, t0 - 1:t0 + 513:513],
                            start=(k == 0), stop=(k == 15)), reads=[wkey, "XY"], writes=[phk])
                    h = H[gv][slot]
                    hk = "H%d_%d" % (gv, slot)
                    P.add("act", lambda e, h=h, pm=pm: e.activation(out=h[:, 1:513], in_=pm[:], func=AF.Copy),
                          reads=[pmk], writes=[hk])
                    P.add("act", lambda e, h=h, ph=ph: e.activation(out=h[:, 0:514:513], in_=ph[:, 0:2], func=AF.Copy),
                          reads=[phk], writes=[hk])
                    tt = TT[gv][slot]
                    tk = "T%d_%d" % (gv, slot)
                    ch = gv * 44 + c
                    P.add("dve", lambda e, tt=tt, h=h, ch=ch: e.tensor_scalar(
                        out=tt[:], in0=h[:, 0:512], scalar1=CW[:, ch, 0:1], scalar2=CW[:, ch, 3:4],
                        op0=ALU.mult, op1=ALU.add), reads=[hk, "CW"], writes=[tk])
                    for j in (1, 2):
                        P.add("dve", lambda e, tt=tt, h=h, ch=ch, j=j: e.scalar_tensor_tensor(
                            out=tt[:], in0=h[:, j:j + 512], scalar=CW[:, ch, j:j + 1], in1=tt[:],
                            op0=ALU.mult, op1=ALU.add), reads=[hk, "CW", tk], writes=[tk])
                tg = TT[0][slot]
                tv = TT[1][slot]
                P.add("act", lambda e, tg=tg: e.activation(out=tg[:], in_=tg[:], func=AF.Silu),
                      reads=["T0_%d" % slot], writes=["T0_%d" % slot])
                P.add("pool", lambda e, tg=tg, tv=tv, c=c, blk=blk: e.tensor_tensor(
                    out=G[:, c, blk * 512:(blk + 1) * 512], in0=tg[:], in1=tv[:], op=ALU.mult),
                    reads=["T0_%d" % slot, "T1_%d" % slot], writes=[("G", c)])
    w_down_v = w_down.rearrange("(k p) n -> p k n", p=128)
    x1_v = x1.rearrange("(t p) d -> p t d", p=128)
    y_v = y.rearrange("(t p) d -> p t d", p=128)
    Gkeys = [("G", c) for c in range(44)]
    li = 0
    for half in range(2):
        P.dma("sp", Y[:], x1_v[:, half * 4:(half + 1) * 4, :], writes=["XY"])
        for cb in range(8):
            wb = WB[li % 2]
            wkey = "WB%d" % (li % 2)
            li += 1
            wv = wb[:].rearrange("p (k n) -> p k n", k=44)
            P.dma("pool", wv[:, 0:22, :], w_down_v[:, 0:22, cb * 256:(cb + 1) * 256], writes=[wkey])
            P.dma("pool", wv[:, 22:44, :], w_down_v[:, 22:44, cb * 256:(cb + 1) * 256], writes=[wkey + "b"])
            for t4 in range(4):
                tt = half * 4 + t4
                pi = 6 + (ps_i % 2)
                ps_i += 1
                pm = PS[pi]
                pmk = "ps%d" % pi
                for k in range(44):
                    P.add("pe", lambda e, k=k, pm=pm, tt=tt, wv=wv: e.matmul(
                        pm[:, 0:256], lhsT=G[:, k, tt * 128:(tt + 1) * 128], rhs=wv[:, k, :],
                        start=(k == 0), stop=(k == 43)), reads=[wkey, wkey + "b", ("G", k)], writes=[pmk])
                ysl = Y[:, t4, cb * 256:(cb + 1) * 256]
                P.add("dve", lambda e, ysl=ysl, pm=pm: e.scalar_tensor_tensor(
                    out=ysl, in0=ysl, scalar=ALPHA, in1=pm[:, 0:256], op0=ALU.mult, op1=ALU.add),
                    reads=[pmk, "XY"], writes=["XY"])
        for t4 in range(4):
            tt = half * 4 + t4
            ln_tile(P, Y[:, t4, :], "XY", LG[:], LB[:], Y[:, t4, :], "XY", ST, MV, "ln")
            P.dma("sp", y_v[:, tt, :], Y[:, t4, :], reads=["XY"], final=True)
    P.emit()
    return nc, P


def _halo_T(xf, c):
    b, j = divmod(c, 4)
    t0 = j * TOK
    out = np.zeros((xf.shape[2], TOK + 2), np.float32)
    lo = max(t0 - 1, 0)
    hi = min(t0 + TOK + 1, T)
    out[:, lo - (t0 - 1):hi - (t0 - 1)] = xf[b, lo:hi, :].T
    return out


def _bc(v):
    return np.ascontiguousarray(np.broadcast_to(np.asarray(v, np.float32)[None, :], (128, v.shape[0])))


_NC_CACHE = {}


def _get(name, builder):
    if name not in _NC_CACHE:
        _NC_CACHE[name] = builder()[0]
    return _NC_CACHE[name]


def run_ffn(xf, w_up, conv_w, conv_b, w_down, g, b):
    nc = _get("ffn", build_ffn)
    cwb = np.concatenate([conv_w, conv_b[None, :]], axis=0)
    cwb = np.ascontiguousarray(cwb.reshape(4, 88, 128).transpose(2, 1, 0))
    w_up = np.ascontiguousarray(w_up)
    w_down = np.ascontiguousarray(w_down)
    lg, lb = _bc(g), _bc(b)
    in_maps = []
    for c in range(NCORES):
        bb, j = divmod(c, 4)
        in_maps.append({"x1": np.ascontiguousarray(xf[bb, j * TOK:(j + 1) * TOK, :]), "x1T": _halo_T(xf, c),
                        "w_up": w_up, "cwb": cwb, "w_down": w_down, "lng": lg, "lnb": lb})
    res = run_bass_kernel_spmd(nc, in_maps, core_ids=list(range(NCORES)))
    out = np.empty_like(xf)
    for c in range(NCORES):
        bb, j = divmod(c, 4)
        out[bb, j * TOK:(j + 1) * TOK, :] = res.results[c]["y"]
    return out


def build_proj_ln(KC):
    nc = bass.Bass("TRN2", target_bir_lowering=False)
    x1 = nc.dram_tensor("x1", [TOK, D], F32, kind="ExternalInput").ap()
    aT = nc.dram_tensor("aT", [KC * 128, TOK], F32, kind="ExternalInput").ap()
    w = nc.dram_tensor("w", [KC * 128, D], F32, kind="ExternalInput").ap()
    lng = nc.dram_tensor("lng", [128, D], F32, kind="ExternalInput").ap()
    lnb = nc.dram_tensor("lnb", [128, D], F32, kind="ExternalInput").ap()
    y = nc.dram_tensor("y", [TOK, D], F32, kind="ExternalOutput").ap()
    P = Prog(nc)
    G = P.sbuf("G", [128, KC, TOK], BF16)
    Y = P.sbuf("Y", [128, 8, D], F32)
    WB = [P.sbuf("WB%d" % i, [128, KC, 512], BF16) for i in range(2)]
    LG = P.sbuf("LG", [128, D], F32)
    LB = P.sbuf("LB", [128, D], F32)
    ST = P.sbuf("ST", [128, 4, 6], F32)
    MV = P.sbuf("MV", [128, 4], F32)
    PS = [P.psum("ps%d" % i, [128, 512]) for i in range(4)]
    P.dma("sp", LG[:], lng, writes=["lnp"])
    P.dma("sp", LB[:], lnb, writes=["lnp2"])
    aT_v = aT.rearrange("(k p) t -> p k t", p=128)
    for k in range(KC):
        P.dma("pool", G[:, k, :], aT_v[:, k, :], writes=[("G", k)])
    w_v = w.rearrange("(k p) n -> p k n", p=128)
    x1_v = x1.rearrange("(t p) d -> p t d", p=128)
    y_v = y.rearrange("(t p) d -> p t d", p=128)
    for tt in range(8):
        P.dma("sp", Y[:, tt, :], x1_v[:, tt, :], writes=[("Y", tt)])
    ps_i = 0
    for cb in range(4):
        wb = WB[cb % 2]
        wkey = "WB%d" % (cb % 2)
        P.dma("pool", wb[:], w_v[:, :, cb * 512:(cb + 1) * 512], writes=[wkey])
        for tt in range(8):
            pi = ps_i % 4
            ps_i += 1
            pm = PS[pi]
            pmk = "ps%d" % pi
            for k in range(KC):
                P.add("pe", lambda e, k=k, pm=pm, tt=tt, wb=wb: e.matmul(
                    pm[:], lhsT=G[:, k, tt * 128:(tt + 1) * 128], rhs=wb[:, k, :],
                    start=(k == 0), stop=(k == KC - 1)), reads=[wkey, ("G", k)], writes=[pmk])
            ysl = Y[:, tt, cb * 512:(cb + 1) * 512]
            P.add("dve", lambda e, ysl=ysl, pm=pm: e.scalar_tensor_tensor(
                out=ysl, in0=ysl, scalar=ALPHA, in1=pm[:], op0=ALU.mult, op1=ALU.add),
                reads=[pmk, ("Y", tt)], writes=[("Y", tt)])
    for tt in range(8):
        ln_tile(P, Y[:, tt, :], ("Y", tt), LG[:], LB[:], Y[:, tt, :], ("Y", tt), ST, MV, "ln")
        P.dma("sp", y_v[:, tt, :], Y[:, tt, :], reads=[("Y", tt)], final=True)
    P.emit()
    return nc, P


def _shard_tok(xf, c):
    b, j = divmod(c, 4)
    return np.ascontiguousarray(xf[b, j * TOK:(j + 1) * TOK, :])


def _gather_tok(res, name, width):
    out = np.empty((NB, T, width), np.float32)
    for c in range(NCORES):
        b, j = divmod(c, 4)
        out[b, j * TOK:(j + 1) * TOK, :] = res.results[c][name]
    return out


def run_proj_ln(xf, af, w, g, b):
    KC = af.shape[2] // 128
    nc = _get("proj_ln%d" % KC, lambda: build_proj_ln(KC))
    w = np.ascontiguousarray(w)
    lg, lb = _bc(g), _bc(b)
    in_maps = []
    for c in range(NCORES):
        in_maps.append({"x1": _shard_tok(xf, c), "aT": np.ascontiguousarray(_shard_tok(af, c).T),
                        "w": w, "lng": lg, "lnb": lb})
    res = run_bass_kernel_spmd(nc, in_maps, core_ids=list(range(NCORES)))
    return _gather_tok(res, "y", D)


def emit_proj_tm(P, XT, xkey, w_v, NCOLS, WB, PS, sink):
    ps_i = 0
    for cb in range(NCOLS // 512):
        wb = WB[cb % 2]
        wkey = "WB%d" % (cb % 2)
        P.dma("pool", wb[:], w_v[:, :, cb * 512:(cb + 1) * 512], writes=[wkey])
        for tt in range(8):
            pi = ps_i % len(PS)
            ps_i += 1
            pm = PS[pi]
            pmk = "ps%d" % pi
            for k in range(16):
                P.add("pe", lambda e, k=k, pm=pm, tt=tt, wb=wb: e.matmul(
                    pm[:], lhsT=XT[:, k, tt * 128:(tt + 1) * 128], rhs=wb[:, k, :],
                    start=(k == 0), stop=(k == 15)), reads=[wkey, xkey], writes=[pmk])
            sink(P, cb, tt, pm, pmk)


def build_proj(NCOLS):
    nc = bass.Bass("TRN2", target_bir_lowering=False)
    xT = nc.dram_tensor("xT", [D, TOK], F32, kind="ExternalInput").ap()
    w = nc.dram_tensor("w", [D, NCOLS], F32, kind="ExternalInput").ap()
    h = nc.dram_tensor("h", [TOK, NCOLS], F32, kind="ExternalOutput").ap()
    P = Prog(nc)
    XT = P.sbuf("XT", [128, 16, TOK], BF16)
    WB = [P.sbuf("WB%d" % i, [128, 16, 512], BF16) for i in range(2)]
    OB = [P.sbuf("OB%d" % i, [128, 512], F32) for i in range(4)]
    PS = [P.psum("ps%d" % i, [128, 512]) for i in range(4)]
    xT_v = xT.rearrange("(k p) t -> p k t", p=128)
    for k in range(16):
        P.dma("pool", XT[:, k, :], xT_v[:, k, :], writes=["XT"])
    w_v = w.rearrange("(k p) n -> p k n", p=128)
    h_v = h.rearrange("(t p) n -> p t n", p=128)
    cnt = [0]

    def sink(P, cb, tt, pm, pmk):
        i = cnt[0] % 4
        cnt[0] += 1
        ob = OB[i]
        P.add("act", lambda e: e.activation(out=ob[:], in_=pm[:], func=AF.Copy), reads=[pmk], writes=["OB%d" % i])
        P.dma("sp", h_v[:, tt, cb * 512:(cb + 1) * 512], ob[:], reads=["OB%d" % i], final=True)

    emit_proj_tm(P, XT, "XT", w_v, NCOLS, WB, PS, sink)
    P.emit()
    return nc, P


def run_proj(xf, w):
    NCOLS = w.shape[1]
    nc = _get("proj%d" % NCOLS, lambda: build_proj(NCOLS))
    w = np.ascontiguousarray(w)
    in_maps = [{"xT": np.ascontiguousarray(_shard_tok(xf, c).T), "w": w} for c in range(NCORES)]
    res = run_bass_kernel_spmd(nc, in_maps, core_ids=list(range(NCORES)))
    return _gather_tok(res, "h", NCOLS)


NQH, NKH, HD = 16, 4, 128
QKV = (NQH + 2 * NKH) * HD
NRM = (NQH + NKH) * HD


def build_qkv():
    nc = bass.Bass("TRN2", target_bir_lowering=False)
    xT = nc.dram_tensor("xT", [D, TOK], F32, kind="ExternalInput").ap()
    w = nc.dram_tensor("w", [D, QKV], F32, kind="ExternalInput").ap()
    cos = nc.dram_tensor("cos", [TOK, HD], F32, kind="ExternalInput").ap()
    sinp = nc.dram_tensor("sinp", [TOK, HD], F32, kind="ExternalInput").ap()
    gains = nc.dram_tensor("gains", [128, 4, HD], F32, kind="ExternalInput").ap()
    h = nc.dram_tensor("h", [TOK, QKV], F32, kind="ExternalOutput").ap()
    P = Prog(nc)
    XT = P.sbuf("XT", [128, 16, TOK], BF16)
    WB = [P.sbuf("WB%d" % i, [128, 16, 512], BF16) for i in range(2)]
    H = P.sbuf("H", [128, 8, QKV], F32)
    TMP = P.sbuf("TMP", [128, NRM], F32)
    TMP2 = P.sbuf("TMP2", [128, NRM], F32)
    CS = P.sbuf("CS", [128, 8, HD], F32)
    SN = P.sbuf("SN", [128, 8, HD], F32)
    GN = P.sbuf("GN", [128, 4, HD], F32)
    TAB = P.sbuf("TAB", [128, 4, 8, HD], F32)
    SS = P.sbuf("SS", [128, 24], F32)
    PS = [P.psum("ps%d" % i, [128, 512]) for i in range(4)]
    xT_v = xT.rearrange("(k p) t -> p k t", p=128)
    for k in range(16):
        P.dma("pool", XT[:, k, :], xT_v[:, k, :], writes=["XT"])
    P.dma("sp", CS[:], cos.rearrange("(t p) f -> p t f", p=128), writes=["CS"])
    P.dma("sp", SN[:], sinp.rearrange("(t p) f -> p t f", p=128), writes=["SN"])
    P.dma("sp", GN[:], gains, writes=["GN"])
    for i in range(4):
        src = CS if i % 2 == 0 else SN
        P.add("dve", lambda e, i=i, src=src: e.tensor_tensor(
            out=TAB[:, i, :, :], in0=src[:], in1=GN[:, i, :].unsqueeze(1).to_broadcast([128, 8, HD]), op=ALU.mult),
            reads=["CS", "SN", "GN"], writes=["TAB"])
    w_v = w.rearrange("(k p) n -> p k n", p=128)
    h_v = h.rearrange("(t p) n -> p t n", p=128)

    def sink(P, cb, tt, pm, pmk):
        P.add("act", lambda e: e.activation(out=H[:, tt, cb * 512:(cb + 1) * 512], in_=pm[:], func=AF.Copy),
              reads=[pmk], writes=[("H", tt)])

    emit_proj_tm(P, XT, "XT", w_v, QKV, WB, PS, sink)
    NH = NQH + NKH
    for tt in range(8):
        X = H[:, tt, 0:NRM]
        hk = ("H", tt)
        P.add("dve", lambda e, X=X: e.tensor_tensor(out=TMP[:], in0=X, in1=X, op=ALU.mult), reads=[hk], writes=["TMP"])
        P.add("dve", lambda e: e.tensor_reduce(out=SS[:, 0:NH], in_=TMP[:].rearrange("p (h f) -> p h f", f=HD),
                                               axis=AX.X, op=ALU.add), reads=["TMP"], writes=["SS"])
        P.add("dve", lambda e: e.tensor_scalar(out=SS[:, 0:NH], in0=SS[:, 0:NH], scalar1=1.0 / HD, scalar2=RMS_EPS,
                                               op0=ALU.mult, op1=ALU.add), reads=["SS"], writes=["SS"])
        P.add("act", lambda e: e.activation(out=SS[:, 0:NH], in_=SS[:, 0:NH], func=AF.Sqrt), reads=["SS"], writes=["SS"])
        P.add("dve", lambda e: e.reciprocal(out=SS[:, 0:NH], in_=SS[:, 0:NH]), reads=["SS"], writes=["SS"])
        P.add("dve", lambda e, X=X: e.tensor_tensor(
            out=TMP[:].rearrange("p (h f) -> p h f", f=HD), in0=X.rearrange("p (h f) -> p h f", f=HD),
            in1=SS[:, 0:NH].unsqueeze(2).to_broadcast([128, NH, HD]), op=ALU.mult), reads=[hk, "SS"], writes=["TMP"])
        for (h0, nh, ti) in ((0, NQH, 0), (NQH, NKH, 2)):
            xn = TMP[:, h0 * HD:(h0 + nh) * HD]
            xo = X[:, h0 * HD:(h0 + nh) * HD]
            t2 = TMP2[:, h0 * HD:(h0 + nh) * HD]
            P.add("dve", lambda e, xn=xn, xo=xo, nh=nh, ti=ti, tt=tt: e.tensor_tensor(
                out=xo.rearrange("p (h f) -> p h f", f=HD), in0=xn.rearrange("p (h f) -> p h f", f=HD),
                in1=TAB[:, ti, tt, :].unsqueeze(1).to_broadcast([128, nh, HD]), op=ALU.mult),
                reads=["TMP", "TAB"], writes=[hk])
            for s in range(2):
                P.add("dve", lambda e, xn=xn, t2=t2, nh=nh, ti=ti, tt=tt, s=s: e.tensor_tensor(
                    out=t2.rearrange("p (h a s f) -> p h a s f", a=2, s=2, f=32)[:, :, :, s, :],
                    in0=xn.rearrange("p (h a s f) -> p h a s f", a=2, s=2, f=32)[:, :, :, 1 - s, :],
                    in1=TAB[:, ti + 1, tt, :].rearrange("p (a s f) -> p a s f", a=2, s=2)[:, :, s, :]
                        .unsqueeze(1).to_broadcast([128, nh, 2, 32]), op=ALU.mult),
                    reads=["TMP", "TAB"], writes=["TMP2"])
            P.add("dve", lambda e, xo=xo, t2=t2: e.tensor_tensor(out=xo, in0=xo, in1=t2, op=ALU.add),
                  reads=[hk, "TMP2"], writes=[hk])
        P.dma("sp", h_v[:, tt, :], H[:, tt, :], reads=[hk], final=True)
    P.emit()
    return nc, P


def _rope_tables():
    rows = T // 64
    r = np.repeat(np.arange(rows, dtype=np.float32), 64)
    cc = np.tile(np.arange(64, dtype=np.float32), rows)
    half = HD // 2
    inv = np.exp(-np.log(np.float32(10000.0)) * np.arange(0, half, 2, dtype=np.float32) / np.float32(half)).astype(np.float32)
    ar = (r[:, None] * inv).astype(np.float32)
    ac = (cc[:, None] * inv).astype(np.float32)
    cos = np.concatenate([np.cos(ar), np.cos(ar), np.cos(ac), np.cos(ac)], axis=1).astype(np.float32)
    sinp = np.concatenate([-np.sin(ar), np.sin(ar), -np.sin(ac), np.sin(ac)], axis=1).astype(np.float32)
    return cos, sinp


def _swap32(g):
    return np.concatenate([g[32:64], g[0:32], g[96:128], g[64:96]])


def run_qkv(xf, w, gq, gk):
    nc = _get("qkv", build_qkv)
    w = np.ascontiguousarray(w)
    cos, sinp = _rope_tables()
    gains = np.stack([gq, _swap32(gq), gk, _swap32(gk)], axis=0).astype(np.float32)
    gains = np.ascontiguousarray(np.broadcast_to(gains[None], (128, 4, HD)))
    in_maps = []
    for c in range(NCORES):
        b, j = divmod(c, 4)
        in_maps.append({"xT": np.ascontiguousarray(_shard_tok(xf, c).T), "w": w,
                        "cos": np.ascontiguousarray(cos[j * TOK:(j + 1) * TOK]),
                        "sinp": np.ascontiguousarray(sinp[j * TOK:(j + 1) * TOK]), "gains": gains})
    res = run_bass_kernel_spmd(nc, in_maps, core_ids=list(range(NCORES)))
    return _gather_tok(res, "h", QKV)


VW = 130


def build_attn():
    nc = bass.Bass("TRN2", target_bir_lowering=False)
    qT = nc.dram_tensor("qT", [NQH * HD, TOK], F32, kind="ExternalInput").ap()
    kT = nc.dram_tensor("kT", [NKH * HD, T], F32, kind="ExternalInput").ap()
    v = nc.dram_tensor("v", [T, NKH * HD], F32, kind="ExternalInput").ap()
    o = nc.dram_tensor("o", [TOK, NQH * HD], F32, kind="ExternalOutput").ap()
    P = Prog(nc)
    QT = P.sbuf("QT", [128, NQH, TOK], BF16)
    KT = P.sbuf("KT", [128, NKH, T], BF16)
    VE = P.sbuf("VE", [128, 32, NKH, VW], BF16)
    PT = [P.sbuf("PT%d" % i, [128, 512], BF16) for i in range(3)]
    OB = [P.sbuf("OB%d" % i, [128, NQH * HD], F32) for i in range(2)]
    RC = P.sbuf("RC", [128, 8], F32)
    PS = [P.psum("ps%d" % i, [128, 512]) for i in range(8)]
    qT_v = qT.rearrange("(h p) t -> p h t", p=128)
    for h in range(NQH):
        P.dma("pool", QT[:, h, :], qT_v[:, h, :], writes=["QT"])
    kT_v = kT.rearrange("(h p) t -> p h t", p=128)
    for h in range(NKH):
        for half in range(2):
            P.dma("pool", KT[:, h, half * 2048:(half + 1) * 2048], kT_v[:, h, half * 2048:(half + 1) * 2048], writes=["KT"])
    P.add("dve", lambda e: e.memset(VE[:].rearrange("p c k w -> p (c k w)"), 1.0), writes=["VE"])
    v_v = v.rearrange("(c p) (k f) -> p c k f", p=128, k=NKH)
    for kv in range(NKH):
        P.dma("pool", VE[:, :, kv, 0:HD], v_v[:, :, kv, :], writes=["VE"])
    o_v = o.rearrange("(t p) n -> p t n", p=128)
    scale = float(HD) ** -0.5
    si = 0
    for qt in range(8):
        ob = OB[qt % 2]
        obk = "OB%d" % (qt % 2)
        for kv in range(NKH):
            for sc in range(32):
                sb = 4 + si % 3
                st = PS[sb]
                stk = "ps%d" % sb
                pt = PT[si % 3]
                ptk = "PT%d" % (si % 3)
                si += 1
                P.add("pe", lambda e, st=st, kv=kv, sc=sc, qt=qt: e.matmul(
                    st[:].rearrange("p (h q) -> p h q", h=4), lhsT=KT[:, kv, sc * 128:(sc + 1) * 128],
                    rhs=QT[:, 4 * kv:4 * kv + 4, qt * 128:(qt + 1) * 128], start=True, stop=True),
                    reads=["QT", "KT"], writes=[stk])
                P.add("act", lambda e, st=st, pt=pt: e.activation(out=pt[:], in_=st[:], func=AF.Exp, scale=scale),
                      reads=[stk], writes=[ptk])
                for hh in range(4):
                    P.add("pe", lambda e, hh=hh, pt=pt, sc=sc, kv=kv: e.matmul(
                        PS[hh][:, 0:VW], lhsT=pt[:, hh * 128:(hh + 1) * 128], rhs=VE[:, sc, kv, :],
                        start=(sc == 0), stop=(sc == 31)), reads=[ptk, "VE"], writes=["ps%d" % hh])
            for hh in range(4):
                hq = 4 * kv + hh
                P.add("dve", lambda e, hh=hh: e.reciprocal(out=RC[:, hh:hh + 1], in_=PS[hh][:, HD:HD + 1]),
                      reads=["ps%d" % hh], writes=[("RC", hh)])
                P.add("dve", lambda e, hh=hh, hq=hq, ob=ob: e.tensor_scalar(
                    out=ob[:, hq * HD:(hq + 1) * HD], in0=PS[hh][:, 0:HD], scalar1=RC[:, hh:hh + 1], scalar2=None,
                    op0=ALU.mult), reads=["ps%d" % hh, ("RC", hh)], writes=[obk])
        P.dma("sp", o_v[:, qt, :], ob[:], reads=[obk], final=True)
    P.emit()
    return nc, P


def run_attn(qkv):
    nc = _get("attn", build_attn)
    in_maps = []
    for c in range(NCORES):
        b, j = divmod(c, 4)
        in_maps.append({"qT": np.ascontiguousarray(qkv[b, j * TOK:(j + 1) * TOK, 0:2048].T),
                        "kT": np.ascontiguousarray(qkv[b, :, 2048:2560].T),
                        "v": np.ascontiguousarray(qkv[b, :, 2560:3072])})
    res = run_bass_kernel_spmd(nc, in_maps, core_ids=list(range(NCORES)))
    return _gather_tok(res, "o", NQH * HD)


GW = 1024


def build_gmlp():
    nc = bass.Bass("TRN2", target_bir_lowering=False)
    u = nc.dram_tensor("u", [TOK, GW], F32, kind="ExternalInput").ap()
    v = nc.dram_tensor("v", [TOK, GW], F32, kind="ExternalInput").ap()
    wsT = nc.dram_tensor("wsT", [8, 128, 128], F32, kind="ExternalInput").ap()
    biasT = nc.dram_tensor("biasT", [128, 8], F32, kind="ExternalInput").ap()
    lng = nc.dram_tensor("lng", [128, GW], F32, kind="ExternalInput").ap()
    lnb = nc.dram_tensor("lnb", [128, GW], F32, kind="ExternalInput").ap()
    ob = nc.dram_tensor("ob", [TOK, GW], F32, kind="ExternalOutput").ap()
    P = Prog(nc)
    U = P.sbuf("U", [128, 8, GW], F32)
    V = P.sbuf("V", [128, 8, GW], F32)
    VN = P.sbuf("VN", [128, 8, GW], BF16)
    WS = P.sbuf("WS", [128, 8, 128], BF16)
    BI = P.sbuf("BI", [128, 8], F32)
    LG = P.sbuf("LG", [128, GW], F32)
    LB = P.sbuf("LB", [128, GW], F32)
    ST = P.sbuf("ST", [128, 2, 6], F32)
    MV = P.sbuf("MV", [128, 4], F32)
    PS = [P.psum("ps%d" % i, [128, 512]) for i in range(4)]
    P.dma("sp", U[:], u.rearrange("(t p) n -> p t n", p=128), writes=["U"])
    P.dma("sp", V[:], v.rearrange("(t p) n -> p t n", p=128), writes=["V"])
    P.dma("pool", WS[:], wsT.rearrange("g s t -> s g t"), writes=["WS"])
    P.dma("sp", BI[:], biasT, writes=["BI"])
    P.dma("sp", LG[:], lng, writes=["lnp"])
    P.dma("sp", LB[:], lnb, writes=["lnp2"])
    ob_v = ob.rearrange("(t p) n -> p t n", p=128)
    ps_i = 0
    for n in range(8):
        y = V[:, n, :]
        yk = ("V", n)
        for c in range(2):
            P.add("dve", lambda e, c=c, y=y: e.bn_stats(out=ST[:, c, :], in_=y[:, c * 512:(c + 1) * 512]),
                  reads=["V"], writes=["st"])
        P.add("dve", lambda e: e.bn_aggr(out=MV[:, 0:2], in_=ST[:].rearrange("p a b -> p (a b)")), reads=["st"], writes=["mv"])
        P.add("dve", lambda e: e.tensor_scalar(out=MV[:, 2:3], in0=MV[:, 1:2], scalar1=LN_EPS, scalar2=None, op0=ALU.add),
              reads=["mv"], writes=["mv"])
        P.add("act", lambda e: e.activation(out=MV[:, 2:3], in_=MV[:, 2:3], func=AF.Sqrt), reads=["mv"], writes=["mv"])
        P.add("dve", lambda e: e.reciprocal(out=MV[:, 3:4], in_=MV[:, 2:3]), reads=["mv"], writes=["mv"])
        P.add("dve", lambda e, y=y: e.tensor_scalar(out=y, in0=y, scalar1=MV[:, 0:1], scalar2=MV[:, 3:4],
                                                    op0=ALU.subtract, op1=ALU.mult), reads=["V", "mv"], writes=["V"])
        P.add("dve", lambda e, y=y: e.tensor_tensor(out=y, in0=y, in1=LG[:], op=ALU.mult), reads=["V", "lnp"], writes=["V"])
        P.add("dve", lambda e, y=y, n=n: e.tensor_tensor(out=VN[:, n, :], in0=y, in1=LB[:], op=ALU.add),
              reads=["V", "lnp2"], writes=[("VN", n)])
        for g4 in range(2):
            pi = ps_i % 4
            ps_i += 1
            pm = PS[pi]
            pmk = "ps%d" % pi
            for gg in range(4):
                g = g4 * 4 + gg
                P.add("pe", lambda e, pm=pm, g=g, gg=gg, n=n: e.matmul(
                    pm[:, gg * 128:(gg + 1) * 128], lhsT=WS[:, g, :], rhs=VN[:, n, g * 128:(g + 1) * 128],
                    start=True, stop=True), reads=["WS", ("VN", n)], writes=[pmk])
            for gg in range(4):
                g = g4 * 4 + gg
                usl = U[:, n, g * 128:(g + 1) * 128]
                P.add("dve", lambda e, pm=pm, g=g, gg=gg, usl=usl: e.scalar_tensor_tensor(
                    out=usl, in0=pm[:, gg * 128:(gg + 1) * 128], scalar=BI[:, g:g + 1], in1=usl,
                    op0=ALU.add, op1=ALU.mult), reads=[pmk, "BI", ("U", n)], writes=[("U", n)])
        P.dma("sp", ob_v[:, n, :], U[:, n, :], reads=[("U", n), "U"], final=True)
    P.emit()
    return nc, P


def run_gmlp(h0, ws, bias, g, b):
    nc = _get("gmlp", build_gmlp)
    wsT = np.ascontiguousarray(ws.transpose(0, 2, 1))
    biasT = np.ascontiguousarray(bias.T)
    lg, lb = _bc(g), _bc(b)
    in_maps = []
    for c in range(NCORES):
        hc = _shard_tok(h0, c)
        in_maps.append({"u": np.ascontiguousarray(hc[:, 5120:6144]), "v": np.ascontiguousarray(hc[:, 6144:7168]),
                        "wsT": wsT, "biasT": biasT, "lng": lg, "lnb": lb})
    res = run_bass_kernel_spmd(nc, in_maps, core_ids=list(range(NCORES)))
    return _gather_tok(res, "ob", GW)


HD_ORDER = (0, 1)


def build_hgrn():
    nc = bass.Bass("TRN2", target_bir_lowering=False)
    qT = nc.dram_tensor("qT", [2, 128, T], F32, kind="ExternalInput").ap()
    ffT = nc.dram_tensor("ffT", [2, 128, T], F32, kind="ExternalInput").ap()
    fbT = nc.dram_tensor("fbT", [2, 128, T], F32, kind="ExternalInput").ap()
    iv = nc.dram_tensor("iv", [2, T, 128], F32, kind="ExternalInput").ap()
    gg = nc.dram_tensor("gg", [2, T, 128], F32, kind="ExternalInput").ap()
    lbt = nc.dram_tensor("lbt", [2, 128, 3], F32, kind="ExternalInput").ap()
    gn = nc.dram_tensor("gn", [2, 64, 128], F32, kind="ExternalInput").ap()
    masks = nc.dram_tensor("masks", [2, 128, 128], F32, kind="ExternalInput").ap()
    rmask = nc.dram_tensor("rmask", [128, 512], F32, kind="ExternalInput").ap()
    ident = nc.dram_tensor("ident", [128, 128], F32, kind="ExternalInput").ap()
    oa = nc.dram_tensor("oa", [2, T, 128], F32, kind="ExternalOutput").ap()
    P = Prog(nc)
    SEG = 512
    MK = P.sbuf("MK", [128, 2, 128], F32)
    RM = P.sbuf("RM", [128, SEG], F32)
    ID = P.sbuf("ID", [128, 128], BF16)
    GN = P.sbuf("GN", [64, 2, 128], F32)
    LBT = P.sbuf("LBT", [128, 2, 3], F32)
    LBV = P.sbuf("LBV", [128, 2, 4], F32)
    Qs = [P.sbuf("Qs%d" % i, [128, SEG], F32) for i in range(2)]
    Fs = [P.sbuf("Fs%d" % i, [128, SEG], F32) for i in range(2)]
    Vs = [P.sbuf("Vs%d" % i, [128, 4, 128], BF16) for i in range(2)]
    Gs = [P.sbuf("Gs%d" % i, [64, 8, 128], F32) for i in range(2)]
    Fg = P.sbuf("Fg", [128, SEG], F32)
    Kt = P.sbuf("Kt", [128, SEG], F32)
    Bt = P.sbuf("Bt", [128, SEG], F32)
    Tm = P.sbuf("Tm", [128, SEG], F32)
    Et = P.sbuf("Et", [128, SEG], F32)
    QS = [P.sbuf("QS%d" % i, [128, SEG], BF16) for i in range(2)]
    KS = [P.sbuf("KS%d" % i, [128, SEG], BF16) for i in range(2)]
    KP = [P.sbuf("KP%d" % i, [128, SEG], BF16) for i in range(2)]
    QB = [P.sbuf("QB%d" % i, [128, SEG], BF16) for i in range(2)]
    DEC = [P.sbuf("DEC%d" % i, [128, 8], F32) for i in range(2)]
    AT = [P.sbuf("AT%d" % i, [128, 128], BF16) for i in range(2)]
    KPT = [P.sbuf("KPT%d" % i, [128, 128], BF16) for i in range(2)]
    S32 = P.sbuf("S32", [128, 128], F32)
    SBF = P.sbuf("SBF", [128, 128], BF16)
    OF = P.sbuf("OF", [64, 64, 128], F32)
    OS = [P.sbuf("OS%d" % i, [64, 8, 128], F32) for i in range(2)]
    SQ = P.sbuf("SQ", [64, 8, 128], F32)
    SSQ = P.sbuf("SSQ", [64, 8], F32)
    PSC = [P.psum("psc%d" % i, [128, 512]) for i in range(2)]
    PTP = [P.psum("ptp%d" % i, [128, 1024], BF16) for i in range(2)]
    PO = [P.psum("po%d" % i, [128, 512]) for i in range(2)]
    PKV = [P.psum("pkv%d" % i, [128, 512]) for i in range(2)]

    P.dma("sp", MK[:], masks.rearrange("m s t -> s m t"), writes=["MK"])
    P.dma("sp", RM[:], rmask, writes=["RM"])
    P.dma("pool", ID[:], ident, writes=["ID"])
    P.dma("sp", GN[:], gn.rearrange("h p f -> p h f"), writes=["GN"])
    P.dma("sp", LBT[:], lbt.rearrange("h p f -> p h f"), writes=["LBT"])
    P.add("act", lambda e: e.activation(out=LBT[:], in_=LBT[:], func=AF.Exp), reads=["LBT"], writes=["LBT"])
    P.add("dve", lambda e: e.tensor_reduce(out=LBV[:, :, 0], in_=LBT[:], axis=AX.X, op=ALU.add), reads=["LBT"], writes=["LBV"])
    P.add("dve", lambda e: e.reciprocal(out=LBV[:, :, 1], in_=LBV[:, :, 0]), reads=["LBV"], writes=["LBV"])
    P.add("dve", lambda e: e.tensor_tensor(out=LBV[:, :, 2], in0=LBT[:, :, 0], in1=LBV[:, :, 1], op=ALU.mult),
          reads=["LBV", "LBT"], writes=["LBV"])
    P.add("dve", lambda e: e.tensor_scalar(out=LBV[:, :, 3], in0=LBV[:, :, 2], scalar1=-1.0, scalar2=1.0,
                                           op0=ALU.mult, op1=ALU.add), reads=["LBV"], writes=["LBV"])
    si = 0
    bi = 0
    ci = 0
    for hd in HD_ORDER:
        lb = LBV[:, hd, 2:3]
        oml = LBV[:, hd, 3:4]
        for dr in range(2):
            P.add("dve", lambda e: e.memset(S32[:], 0.0), writes=["S32"])
            P.add("dve", lambda e: e.memset(SBF[:], 0.0), writes=["SBF"])
            fsrc = ffT if dr == 0 else fbT
            mid = 31 if dr == 0 else 32
            last = 63 if dr == 0 else 0
            for seg in (range(8) if dr == 0 else range(7, -1, -1)):
                p = si % 2
                si += 1
                t0 = seg * SEG
                q_s, f_s, v_s, g_s = Qs[p], Fs[p], Vs[p], Gs[p]
                qk, fk, vk, gk = "Qs%d" % p, "Fs%d" % p, "Vs%d" % p, "Gs%d" % p
                P.dma("sp", q_s[:], qT[hd, :, t0:t0 + SEG], writes=[qk])
                P.dma("sp", f_s[:], fsrc[hd, :, t0:t0 + SEG], writes=[fk])
                P.dma("pool", v_s[:], iv[hd, t0:t0 + SEG, :].rearrange("(b p) v -> p b v", p=128), writes=[vk])
                if dr == 1:
                    P.dma("sp", g_s[:], gg[hd, t0:t0 + SEG, :].rearrange("(c p) v -> p c v", p=64), writes=[gk])
                qs, ks, kp, qb, dec = QS[p], KS[p], KP[p], QB[p], DEC[p]
                qsk, ksk, kpk, qbk, deck = "QS%d" % p, "KS%d" % p, "KP%d" % p, "QB%d" % p, "DEC%d" % p
                P.add("act", lambda e, f_s=f_s: e.activation(out=Fg[:], in_=f_s[:], func=AF.Sigmoid), reads=[fk], writes=["Fg"])
                P.add("dve", lambda e, oml=oml, lb=lb: e.tensor_scalar(out=Fg[:], in0=Fg[:], scalar1=oml, scalar2=lb, op0=ALU.mult, op1=ALU.add),
                      reads=["Fg", "LBV"], writes=["Fg"])
                P.add("dve", lambda e: e.tensor_scalar(out=Kt[:], in0=Fg[:], scalar1=-1.0, scalar2=1.0, op0=ALU.mult, op1=ALU.add),
                      reads=["Fg"], writes=["Kt"])
                P.add("act", lambda e: e.activation(out=Fg[:], in_=Fg[:], func=AF.Ln), reads=["Fg"], writes=["Fg"])
                P.add("dve", lambda e: e.tensor_tensor_scan(out=Bt[:], data0=RM[:], data1=Fg[:], initial=0.0,
                                                            op0=ALU.mult, op1=ALU.add), reads=["RM", "Fg"], writes=["Bt"])
                B3 = Bt[:].rearrange("p (c t) -> p c t", t=64)
                T3 = Tm[:].rearrange("p (c t) -> p c t", t=64)
                F3 = Fg[:].rearrange("p (c t) -> p c t", t=64)
                if dr == 1:
                    P.add("dve", lambda e, B3=B3, T3=T3: e.tensor_tensor(
                        out=T3, in0=B3[:, :, 63:64].to_broadcast([128, 8, 64]), in1=B3, op=ALU.subtract),
                        reads=["Bt"], writes=["Tm"])
                    P.add("dve", lambda e: e.tensor_tensor(out=Bt[:], in0=Tm[:], in1=Fg[:], op=ALU.add),
                          reads=["Tm", "Fg"], writes=["Bt"])
                P.add("dve", lambda e, B3=B3, T3=T3, mid=mid: e.tensor_tensor(
                    out=T3, in0=B3, in1=B3[:, :, mid:mid + 1].to_broadcast([128, 8, 64]), op=ALU.subtract),
                    reads=["Bt"], writes=["Tm"])
                P.add("act", lambda e: e.activation(out=Et[:], in_=Tm[:], func=AF.Exp), reads=["Tm"], writes=["Et"])
                P.add("dve", lambda e, qs=qs, q_s=q_s: e.tensor_tensor(out=qs[:], in0=q_s[:], in1=Et[:], op=ALU.mult),
                      reads=[qk, "Et"], writes=[qsk])
                P.add("act", lambda e: e.activation(out=Et[:], in_=Tm[:], func=AF.Exp, scale=-1.0), reads=["Tm"], writes=["Et"])
                P.add("dve", lambda e, ks=ks: e.tensor_tensor(out=ks[:], in0=Kt[:], in1=Et[:], op=ALU.mult),
                      reads=["Kt", "Et"], writes=[ksk])
                P.add("dve", lambda e, B3=B3, T3=T3, last=last: e.tensor_tensor(
                    out=T3, in0=B3[:, :, last:last + 1].to_broadcast([128, 8, 64]), in1=B3, op=ALU.subtract),
                    reads=["Bt"], writes=["Tm"])
                P.add("act", lambda e: e.activation(out=Et[:], in_=Tm[:], func=AF.Exp), reads=["Tm"], writes=["Et"])
                P.add("dve", lambda e, kp=kp: e.tensor_tensor(out=kp[:], in0=Kt[:], in1=Et[:], op=ALU.mult),
                      reads=["Kt", "Et"], writes=[kpk])
                P.add("act", lambda e: e.activation(out=Et[:], in_=Bt[:], func=AF.Exp), reads=["Bt"], writes=["Et"])
                P.add("dve", lambda e, qb=qb, q_s=q_s: e.tensor_tensor(out=qb[:], in0=q_s[:], in1=Et[:], op=ALU.mult),
                      reads=[qk, "Et"], writes=[qbk])
                E3v = Et[:].rearrange("p (c t) -> p c t", t=64)
                P.add("dve", lambda e, dec=dec, E3v=E3v, last=last: e.tensor_copy(out=dec[:], in_=E3v[:, :, last]),
                      reads=["Et"], writes=[deck])
                os_ = OS[p]
                osk = "OS%d" % p
                for blk in (range(4) if dr == 0 else range(3, -1, -1)):
                    b2 = bi % 2
                    bi += 1
                    psc, ptp = PSC[b2], PTP[b2]
                    at, kpt = AT[b2], KPT[b2]
                    atk, kptk = "AT%d" % b2, "KPT%d" % b2
                    cs = slice(blk * 128, (blk + 1) * 128)
                    P.add("pe", lambda e, psc=psc, ks=ks, qs=qs, cs=cs: e.matmul(
                        psc[:, 0:128], lhsT=ks[:, cs], rhs=qs[:, cs], start=True, stop=True),
                        reads=[ksk, qsk], writes=["psc%d" % b2])
                    P.add("dve", lambda e, at=at, psc=psc, dr=dr: e.tensor_tensor(
                        out=at[:], in0=psc[:, 0:128], in1=MK[:, dr, :], op=ALU.mult),
                        reads=["psc%d" % b2, "MK"], writes=[atk])
                    P.add("pe", lambda e, ptp=ptp, kp=kp, cs=cs: e.transpose(ptp[:, 0:128], kp[:, cs], ID[:]),
                          reads=[kpk, "ID"], writes=["ptp%d" % b2])
                    P.add("act", lambda e, kpt=kpt, ptp=ptp: e.activation(out=kpt[:], in_=ptp[:, 0:128], func=AF.Copy),
                          reads=["ptp%d" % b2], writes=[kptk])
                    for c in ((0, 1) if dr == 0 else (1, 0)):
                        c2 = ci % 2
                        ci += 1
                        po, pkv = PO[c2], PKV[c2]
                        cl = blk * 2 + c
                        n = seg * 8 + cl
                        rs = slice(64 * c, 64 * c + 64)
                        P.add("pe", lambda e, po=po, at=at, v_s=v_s, rs=rs, blk=blk: e.matmul(
                            po[0:64, 0:128], lhsT=at[rs, rs], rhs=v_s[rs, blk, :], start=True, stop=False),
                            reads=[atk, vk], writes=["po%d" % c2])
                        P.add("pe", lambda e, po=po, qb=qb, cl=cl: e.matmul(
                            po[0:64, 0:128], lhsT=qb[:, cl * 64:(cl + 1) * 64], rhs=SBF[:], start=False, stop=True),
                            reads=[qbk, "SBF"], writes=["po%d" % c2])
                        P.add("pe", lambda e, pkv=pkv, kpt=kpt, v_s=v_s, rs=rs, blk=blk: e.matmul(
                            pkv[:, 0:128], lhsT=kpt[rs, :], rhs=v_s[rs, blk, :], start=True, stop=True),
                            reads=[kptk, vk], writes=["pkv%d" % c2])
                        P.add("dve", lambda e, pkv=pkv, dec=dec, cl=cl: e.scalar_tensor_tensor(
                            out=S32[:], in0=S32[:], scalar=dec[:, cl:cl + 1], in1=pkv[:, 0:128], op0=ALU.mult, op1=ALU.add),
                            reads=["S32", deck, "pkv%d" % c2], writes=["S32"])
                        P.add("act", lambda e: e.activation(out=SBF[:], in_=S32[:], func=AF.Copy), reads=["S32"], writes=["SBF"])
                        if dr == 0:
                            P.add("act", lambda e, po=po, n=n: e.activation(out=OF[:, n, :], in_=po[0:64, 0:128], func=AF.Copy),
                                  reads=["po%d" % c2], writes=[("OF", n)])
                        else:
                            P.add("dve", lambda e, po=po, n=n, cl=cl, os_=os_: e.tensor_tensor(
                                out=os_[:, cl, :], in0=po[0:64, 0:128], in1=OF[:, n, :], op=ALU.add),
                                reads=["po%d" % c2, ("OF", n)], writes=[osk])
                if dr == 1:
                    P.add("dve", lambda e, os_=os_: e.tensor_tensor(out=SQ[:], in0=os_[:], in1=os_[:], op=ALU.mult),
                          reads=[osk], writes=["SQ"])
                    P.add("dve", lambda e: e.tensor_reduce(out=SSQ[:], in_=SQ[:], axis=AX.X, op=ALU.add), reads=["SQ"], writes=["SSQ"])
                    P.add("dve", lambda e: e.tensor_scalar(out=SSQ[:], in0=SSQ[:], scalar1=1.0 / 128, scalar2=RMS_EPS,
                                                           op0=ALU.mult, op1=ALU.add), reads=["SSQ"], writes=["SSQ"])
                    P.add("act", lambda e: e.activation(out=SSQ[:], in_=SSQ[:], func=AF.Sqrt), reads=["SSQ"], writes=["SSQ"])
                    P.add("dve", lambda e: e.reciprocal(out=SSQ[:], in_=SSQ[:]), reads=["SSQ"], writes=["SSQ"])
                    P.add("dve", lambda e, os_=os_: e.tensor_tensor(
                        out=os_[:], in0=os_[:], in1=SSQ[:].unsqueeze(2).to_broadcast([64, 8, 128]), op=ALU.mult),
                        reads=[osk, "SSQ"], writes=[osk])
                    P.add("dve", lambda e, os_=os_, hd=hd: e.tensor_tensor(
                        out=os_[:], in0=os_[:], in1=GN[:, hd, :].unsqueeze(1).to_broadcast([64, 8, 128]), op=ALU.mult),
                        reads=[osk, "GN"], writes=[osk])
                    P.add("act", lambda e, g_s=g_s: e.activation(out=g_s[:], in_=g_s[:], func=AF.Silu), reads=[gk], writes=[gk])
                    P.add("dve", lambda e, os_=os_, g_s=g_s: e.tensor_tensor(out=os_[:], in0=os_[:], in1=g_s[:], op=ALU.mult),
                          reads=[osk, gk], writes=[osk])
                    P.dma("sp", oa[hd, t0:t0 + SEG, :].rearrange("(c p) v -> p c v", p=64), os_[:], reads=[osk], final=True)
    P.emit()
    return nc, P


def run_hgrn(h0, lb_table, norm_g):
    nc = _get("hgrn", build_hgrn)
    idx = np.arange(128)
    same = (idx[:, None] // 64) == (idx[None, :] // 64)
    mfw = (same & (idx[:, None] <= idx[None, :])).astype(np.float32)
    mbw = (same & (idx[:, None] >= idx[None, :])).astype(np.float32)
    masks = np.stack([mfw, mbw], axis=0)
    rmask = np.ones((128, 512), np.float32)
    rmask[:, ::64] = 0.0
    ident = np.eye(128, dtype=np.float32)
    in_maps = []
    for c in range(NCORES):
        b, j = divmod(c, 4)
        hs = [2 * j, 2 * j + 1]

        def colsT(base):
            return np.ascontiguousarray(np.stack([h0[b, :, base + H * 128: base + (H + 1) * 128].T for H in hs], axis=0))

        def cols(base):
            return np.ascontiguousarray(np.stack([h0[b, :, base + H * 128: base + (H + 1) * 128] for H in hs], axis=0))

        in_maps.append({
            "qT": colsT(0), "ffT": colsT(1024), "fbT": colsT(2048), "iv": cols(3072), "gg": cols(4096),
            "lbt": np.ascontiguousarray(np.stack([lb_table[:, H * 128:(H + 1) * 128].T for H in hs], axis=0)),
            "gn": np.ascontiguousarray(np.stack([np.broadcast_to(norm_g[H * 128:(H + 1) * 128][None, :], (64, 128)) for H in hs], axis=0)),
            "masks": masks, "rmask": rmask, "ident": ident})
    res = run_bass_kernel_spmd(nc, in_maps, core_ids=list(range(NCORES)))
    out = np.empty((NB, T, GW), np.float32)
    for c in range(NCORES):
        b, j = divmod(c, 4)
        for hd in range(2):
            H = 2 * j + hd
            out[b, :, H * 128:(H + 1) * 128] = res.results[c]["oa"][hd]
    return out


def emit_hgrn(P, qT, ffT, fbT, iv, gg, lbt, gn, masks, rmask, IDB, OH, IB1):
    SEG = 512
    IB1v = IB1.rearrange("(s h t) v -> s h t v", s=8, h=2)
    MK = P.sbuf("MK", [128, 2, 128], F32)
    RM = P.sbuf("RM", [128, SEG], F32)
    GN = P.sbuf("GN", [64, 2, 128], F32)
    LBT = P.sbuf("LBT", [128, 2, 3], F32)
    LBV = P.sbuf("LBV", [128, 2, 4], F32)
    Qs = [P.sbuf("Qs%d" % i, [128, SEG], F32) for i in range(2)]
    Fs = [P.sbuf("Fs%d" % i, [128, SEG], F32) for i in range(2)]
    Vf = [P.sbuf("Vf%d" % i, [128, 4, 128], F32) for i in range(2)]
    Vs = [P.sbuf("Vs%d" % i, [128, 4, 128], BF16) for i in range(2)]
    Gs = [P.sbuf("Gs%d" % i, [64, 8, 128], F32) for i in range(2)]
    Fg = P.sbuf("Fg", [128, SEG], F32)
    Kt = P.sbuf("Kt", [128, SEG], F32)
    Bt = P.sbuf("Bt", [128, SEG], F32)
    Tm = P.sbuf("Tm", [128, SEG], F32)
    Et = P.sbuf("Et", [128, SEG], F32)
    QS = [P.sbuf("QS%d" % i, [128, SEG], BF16) for i in range(2)]
    KS = [P.sbuf("KS%d" % i, [128, SEG], BF16) for i in range(2)]
    KP = [P.sbuf("KP%d" % i, [128, SEG], BF16) for i in range(2)]
    QB = [P.sbuf("QB%d" % i, [128, SEG], BF16) for i in range(2)]
    DEC = [P.sbuf("DEC%d" % i, [128, 8], F32) for i in range(2)]
    AT = [P.sbuf("AT%d" % i, [128, 128], BF16) for i in range(2)]
    KPT = [P.sbuf("KPT%d" % i, [128, 128], BF16) for i in range(2)]
    S32 = P.sbuf("S32", [128, 128], F32)
    SBF = P.sbuf("SBF", [128, 128], BF16)
    OF = P.sbuf("OF", [64, 64, 128], F32)
    OS = [P.sbuf("OS%d" % i, [64, 8, 128], F32) for i in range(2)]
    OM = [P.sbuf("OM%d" % i, [64, 8, 128], BF16) for i in range(4)]
    SQ = P.sbuf("SQ", [64, 8, 128], F32)
    SSQ = P.sbuf("SSQ", [64, 8], F32)
    PSC = [P.psum("psc%d" % i, [128, 512]) for i in range(2)]
    PTP = [P.psum("ptp%d" % i, [128, 1024], BF16) for i in range(1)]
    PO = [P.psum("po%d" % i, [128, 512]) for i in range(2)]
    PKV = [P.psum("pkv%d" % i, [128, 512]) for i in range(2)]

    P.dma("sp", MK[:], masks.rearrange("m s t -> s m t"), writes=["MK"])
    P.dma("sp", RM[:], rmask, writes=["RM"])
    P.dma("sp", GN[:], gn.rearrange("h p f -> p h f"), writes=["GN"])
    P.dma("sp", LBT[:], lbt.rearrange("h p f -> p h f"), writes=["LBT"])
    P.add("act", lambda e: e.activation(out=LBT[:], in_=LBT[:], func=AF.Exp), reads=["LBT"], writes=["LBT"])
    P.add("dve", lambda e: e.tensor_reduce(out=LBV[:, :, 0], in_=LBT[:], axis=AX.X, op=ALU.add), reads=["LBT"], writes=["LBV"])
    P.add("dve", lambda e: e.reciprocal(out=LBV[:, :, 1], in_=LBV[:, :, 0]), reads=["LBV"], writes=["LBV"])
    P.add("dve", lambda e: e.tensor_tensor(out=LBV[:, :, 2], in0=LBT[:, :, 0], in1=LBV[:, :, 1], op=ALU.mult),
          reads=["LBV", "LBT"], writes=["LBV"])
    P.add("dve", lambda e: e.tensor_scalar(out=LBV[:, :, 3], in0=LBV[:, :, 2], scalar1=-1.0, scalar2=1.0,
                                           op0=ALU.mult, op1=ALU.add), reads=["LBV"], writes=["LBV"])
    si = bi = ci = mi = 0
    for hd in range(2):
        lb = LBV[:, hd, 2:3]
        oml = LBV[:, hd, 3:4]
        for dr in range(2):
            P.add("dve", lambda e: e.memset(S32[:], 0.0), writes=["S32"])
            P.add("dve", lambda e: e.memset(SBF[:], 0.0), writes=["SBF"])
            fsrc = ffT if dr == 0 else fbT
            mid = 31 if dr == 0 else 32
            last = 63 if dr == 0 else 0
            for seg in (range(8) if dr == 0 else range(7, -1, -1)):
                p = si % 2
                si += 1
                t0 = seg * SEG
                q_s, f_s, v_f, v_s, g_s = Qs[p], Fs[p], Vf[p], Vs[p], Gs[p]
                qk, fk, vfk, vk, gk = "Qs%d" % p, "Fs%d" % p, "Vf%d" % p, "Vs%d" % p, "Gs%d" % p
                P.dma("sp", q_s[:], qT[hd, :, t0:t0 + SEG], reads=["dram:hq"], writes=[qk])
                P.dma("sp", f_s[:], fsrc[hd, :, t0:t0 + SEG], reads=["dram:hq"], writes=[fk])
                P.dma("sp", v_f[:], iv[hd, t0:t0 + SEG, :].rearrange("(b p) v -> p b v", p=128), reads=["dram:hi"], writes=[vfk])
                P.add("pool", lambda e, v_s=v_s, v_f=v_f: e.tensor_copy(out=v_s[:], in_=v_f[:]), reads=[vfk], writes=[vk])
                if dr == 1:
                    P.dma("sp", g_s[:], gg[hd, t0:t0 + SEG, :].rearrange("(c p) v -> p c v", p=64), reads=["dram:hi"], writes=[gk])
                qs, ks, kp, qb, dec = QS[p], KS[p], KP[p], QB[p], DEC[p]
                qsk, ksk, kpk, qbk, deck = "QS%d" % p, "KS%d" % p, "KP%d" % p, "QB%d" % p, "DEC%d" % p
                P.add("act", lambda e, f_s=f_s: e.activation(out=Fg[:], in_=f_s[:], func=AF.Sigmoid), reads=[fk], writes=["Fg"])
                P.add("dve", lambda e, oml=oml, lb=lb: e.tensor_scalar(out=Fg[:], in0=Fg[:], scalar1=oml, scalar2=lb,
                                                                       op0=ALU.mult, op1=ALU.add), reads=["Fg", "LBV"], writes=["Fg"])
                P.add("dve", lambda e: e.tensor_scalar(out=Kt[:], in0=Fg[:], scalar1=-1.0, scalar2=1.0, op0=ALU.mult, op1=ALU.add),
                      reads=["Fg"], writes=["Kt"])
                P.add("act", lambda e: e.activation(out=Fg[:], in_=Fg[:], func=AF.Ln), reads=["Fg"], writes=["Fg"])
                P.add("dve", lambda e: e.tensor_tensor_scan(out=Bt[:], data0=RM[:], data1=Fg[:], initial=0.0,
                                                            op0=ALU.mult, op1=ALU.add), reads=["RM", "Fg"], writes=["Bt"])
                B3 = Bt[:].rearrange("p (c t) -> p c t", t=64)
                T3 = Tm[:].rearrange("p (c t) -> p c t", t=64)
                if dr == 1:
                    P.add("dve", lambda e, B3=B3, T3=T3: e.tensor_tensor(
                        out=T3, in0=B3[:, :, 63:64].to_broadcast([128, 8, 64]), in1=B3, op=ALU.subtract),
                        reads=["Bt"], writes=["Tm"])
                    P.add("dve", lambda e: e.tensor_tensor(out=Bt[:], in0=Tm[:], in1=Fg[:], op=ALU.add),
                          reads=["Tm", "Fg"], writes=["Bt"])
                P.add("dve", lambda e, B3=B3, T3=T3, mid=mid: e.tensor_tensor(
                    out=T3, in0=B3, in1=B3[:, :, mid:mid + 1].to_broadcast([128, 8, 64]), op=ALU.subtract),
                    reads=["Bt"], writes=["Tm"])
                P.add("act", lambda e: e.activation(out=Et[:], in_=Tm[:], func=AF.Exp), reads=["Tm"], writes=["Et"])
                P.add("dve", lambda e, qs=qs, q_s=q_s: e.tensor_tensor(out=qs[:], in0=q_s[:], in1=Et[:], op=ALU.mult),
                      reads=[qk, "Et"], writes=[qsk])
                P.add("act", lambda e: e.activation(out=Et[:], in_=Tm[:], func=AF.Exp, scale=-1.0), reads=["Tm"], writes=["Et"])
                P.add("dve", lambda e, ks=ks: e.tensor_tensor(out=ks[:], in0=Kt[:], in1=Et[:], op=ALU.mult),
                      reads=["Kt", "Et"], writes=[ksk])
                P.add("dve", lambda e, B3=B3, T3=T3, last=last: e.tensor_tensor(
                    out=T3, in0=B3[:, :, last:last + 1].to_broadcast([128, 8, 64]), in1=B3, op=ALU.subtract),
                    reads=["Bt"], writes=["Tm"])
                P.add("act", lambda e: e.activation(out=Et[:], in_=Tm[:], func=AF.Exp), reads=["Tm"], writes=["Et"])
                P.add("dve", lambda e, kp=kp: e.tensor_tensor(out=kp[:], in0=Kt[:], in1=Et[:], op=ALU.mult),
                      reads=["Kt", "Et"], writes=[kpk])
                P.add("act", lambda e: e.activation(out=Et[:], in_=Bt[:], func=AF.Exp), reads=["Bt"], writes=["Et"])
                P.add("dve", lambda e, qb=qb, q_s=q_s: e.tensor_tensor(out=qb[:], in0=q_s[:], in1=Et[:], op=ALU.mult),
                      reads=[qk, "Et"], writes=[qbk])
                E3v = Et[:].rearrange("p (c t) -> p c t", t=64)
                P.add("dve", lambda e, dec=dec, E3v=E3v, last=last: e.tensor_copy(out=dec[:], in_=E3v[:, :, last]),
                      reads=["Et"], writes=[deck])
                os_ = OS[p]
                osk = "OS%d" % p
                for blk in (range(4) if dr == 0 else range(3, -1, -1)):
                    b2 = bi % 2
                    bi += 1
                    psc, ptp = PSC[b2], PTP[0]
                    at, kpt = AT[b2], KPT[b2]
                    atk, kptk = "AT%d" % b2, "KPT%d" % b2
                    cs = slice(blk * 128, (blk + 1) * 128)
                    P.add("pe", lambda e, psc=psc, ks=ks, qs=qs, cs=cs: e.matmul(
                        psc[:, 0:128], lhsT=ks[:, cs], rhs=qs[:, cs], start=True, stop=True),
                        reads=[ksk, qsk], writes=["psc%d" % b2])
                    P.add("dve", lambda e, at=at, psc=psc, dr=dr: e.tensor_tensor(
                        out=at[:], in0=psc[:, 0:128], in1=MK[:, dr, :], op=ALU.mult),
                        reads=["psc%d" % b2, "MK"], writes=[atk])
                    P.add("pe", lambda e, ptp=ptp, kp=kp, cs=cs: e.transpose(ptp[:, 0:128], kp[:, cs], IDB[:]),
                          reads=[kpk, "IDB"], writes=["ptp0"])
                    P.add("act", lambda e, kpt=kpt, ptp=ptp: e.activation(out=kpt[:], in_=ptp[:, 0:128], func=AF.Copy),
                          reads=["ptp0"], writes=[kptk])
                    for c in ((0, 1) if dr == 0 else (1, 0)):
                        c2 = ci % 2
                        ci += 1
                        po, pkv = PO[c2], PKV[c2]
                        cl = blk * 2 + c
                        n = seg * 8 + cl
                        rs = slice(64 * c, 64 * c + 64)
                        P.add("pe", lambda e, po=po, at=at, v_s=v_s, rs=rs, blk=blk: e.matmul(
                            po[0:64, 0:128], lhsT=at[rs, rs], rhs=v_s[rs, blk, :], start=True, stop=False),
                            reads=[atk, vk], writes=["po%d" % c2])
                        P.add("pe", lambda e, po=po, qb=qb, cl=cl: e.matmul(
                            po[0:64, 0:128], lhsT=qb[:, cl * 64:(cl + 1) * 64], rhs=SBF[:], start=False, stop=True),
                            reads=[qbk, "SBF"], writes=["po%d" % c2])
                        P.add("pe", lambda e, pkv=pkv, kpt=kpt, v_s=v_s, rs=rs, blk=blk: e.matmul(
                            pkv[:, 0:128], lhsT=kpt[rs, :], rhs=v_s[rs, blk, :], start=True, stop=True),
                            reads=[kptk, vk], writes=["pkv%d" % c2])
                        P.add("dve", lambda e, pkv=pkv, dec=dec, cl=cl: e.scalar_tensor_tensor(
                            out=S32[:], in0=S32[:], scalar=dec[:, cl:cl + 1], in1=pkv[:, 0:128], op0=ALU.mult, op1=ALU.add),
                            reads=["S32", deck, "pkv%d" % c2], writes=["S32"])
                        P.add("act", lambda e: e.activation(out=SBF[:], in_=S32[:], func=AF.Copy), reads=["S32"], writes=["SBF"])
                        if dr == 0:
                            P.add("act", lambda e, po=po, n=n: e.activation(out=OF[:, n, :], in_=po[0:64, 0:128], func=AF.Copy),
                                  reads=["po%d" % c2], writes=[("OF", n)])
                        else:
                            P.add("dve", lambda e, po=po, n=n, cl=cl, os_=os_: e.tensor_tensor(
                                out=os_[:, cl, :], in0=po[0:64, 0:128], in1=OF[:, n, :], op=ALU.add),
                                reads=["po%d" % c2, ("OF", n)], writes=[osk])
                if dr == 1:
                    P.add("dve", lambda e, os_=os_: e.tensor_tensor(out=SQ[:], in0=os_[:], in1=os_[:], op=ALU.mult),
                          reads=[osk], writes=["SQ"])
                    P.add("dve", lambda e: e.tensor_reduce(out=SSQ[:], in_=SQ[:], axis=AX.X, op=ALU.add), reads=["SQ"], writes=["SSQ"])
                    P.add("dve", lambda e: e.tensor_scalar(out=SSQ[:], in0=SSQ[:], scalar1=1.0 / 128, scalar2=RMS_EPS,
                                                           op0=ALU.mult, op1=ALU.add), reads=["SSQ"], writes=["SSQ"])
                    P.add("act", lambda e: e.activation(out=SSQ[:], in_=SSQ[:], func=AF.Sqrt), reads=["SSQ"], writes=["SSQ"])
                    P.add("dve", lambda e: e.reciprocal(out=SSQ[:], in_=SSQ[:]), reads=["SSQ"], writes=["SSQ"])
                    P.add("dve", lambda e, os_=os_: e.tensor_tensor(
                        out=os_[:], in0=os_[:], in1=SSQ[:].unsqueeze(2).to_broadcast([64, 8, 128]), op=ALU.mult),
                        reads=[osk, "SSQ"], writes=[osk])
                    P.add("dve", lambda e, os_=os_, hd=hd: e.tensor_tensor(
                        out=os_[:], in0=os_[:], in1=GN[:, hd, :].unsqueeze(1).to_broadcast([64, 8, 128]), op=ALU.mult),
                        reads=[osk, "GN"], writes=[osk])
                    P.add("act", lambda e, g_s=g_s: e.activation(out=g_s[:], in_=g_s[:], func=AF.Silu), reads=[gk], writes=[gk])
                    P.add("dve", lambda e, os_=os_, g_s=g_s: e.tensor_tensor(out=os_[:], in0=os_[:], in1=g_s[:], op=ALU.mult),
                          reads=[osk, gk], writes=[osk])
                    for s_ in range(8):
                        om = OM[mi % 4]
                        omk = "OM%d" % (mi % 4)
                        mi += 1
                        P.add("pool", lambda e, om=om, os_=os_, s_=s_: e.tensor_scalar(
                            out=om[:], in0=os_[:], scalar1=OH[0:64, s_:s_ + 1], scalar2=0.0, op0=ALU.mult, op1=ALU.add),
                            reads=[osk, "OH"], writes=[omk])
                        P.dma("sp", IB1v[s_, hd, t0:t0 + SEG, :].rearrange("(c p) v -> p c v", p=64), om[:],
                              reads=[omk], writes=["dram:IB1"])


def emit_projln(P, fill_G, w, x_res, lng, lnb, x_out, IB, OB, OH, RG, dram_reads, IDF):
    P.begin_phase()
    G = P.sbuf("G", [128, 16, TOK], BF16)
    Y = P.sbuf("Y", [128, 8, D], F32)
    WB = [P.sbuf("WB%d" % i, [128, 16, 512], BF16) for i in range(2)]
    STG = [WB[i][:].rearrange("p k n -> p (k n)")[:, 0:2 * D].bitcast(F32) for i in range(2)]
    LG = P.sbuf("LG", [128, D], F32)
    LB = P.sbuf("LB", [128, D], F32)
    ST = P.sbuf("ST", [128, 4, 6], F32)
    MV = P.sbuf("MV", [128, 4], F32)
    HR = P.sbuf("HR", [2, D], F32)
    HM = [P.sbuf("HM%d" % i, [2, D], F32) for i in range(2)]
    PS = [P.psum("ps%d" % i, [128, 512]) for i in range(6)]
    P.dma("sp", LG[:], lng, writes=["lnp"])
    P.dma("sp", LB[:], lnb, writes=["lnp2"])
    fill_G(P, G, STG, PS[4:6], ["ps4", "ps5"])
    w_v = w.rearrange("(k p) n -> p k n", p=128)
    xr_v = x_res.rearrange("(t p) d -> p t d", p=128)
    xo_v = x_out.rearrange("(t p) d -> p t d", p=128)
    for tt in range(8):
        P.dma("sp", Y[:, tt, :], xr_v[:, tt, :], writes=[("Y", tt)])
    ps_i = 0
    for cb in range(4):
        wb = WB[cb % 2]
        wkey = "WB%d" % (cb % 2)
        P.dma("pool", wb[:], w_v[:, :, cb * 512:(cb + 1) * 512], writes=[wkey])
        for tt in range(8):
            pi = ps_i % 4
            ps_i += 1
            pm = PS[pi]
            pmk = "ps%d" % pi
            for k in range(16):
                P.add("pe", lambda e, k=k, pm=pm, tt=tt, wb=wb: e.matmul(
                    pm[:], lhsT=G[:, k, tt * 128:(tt + 1) * 128], rhs=wb[:, k, :],
                    start=(k == 0), stop=(k == 15)), reads=[wkey, ("G", k), "Gb"], writes=[pmk])
            ysl = Y[:, tt, cb * 512:(cb + 1) * 512]
            P.add("dve", lambda e, ysl=ysl, pm=pm: e.scalar_tensor_tensor(
                out=ysl, in0=ysl, scalar=ALPHA, in1=pm[:], op0=ALU.mult, op1=ALU.add),
                reads=[pmk, ("Y", tt)], writes=[("Y", tt)])
    for tt in range(8):
        ln_tile(P, Y[:, tt, :], ("Y", tt), LG[:], LB[:], Y[:, tt, :], ("Y", tt), ST, MV, "ln")
        P.dma("sp", xo_v[:, tt, :], Y[:, tt, :], reads=[("Y", tt)], writes=["dram:xo%d" % tt])
    P.dma("sp", HR[0:1, :], x_out[TOK - 1:TOK, :], reads=["dram:xo7"], writes=["HR"])
    P.dma("sp", HR[1:2, :], x_out[0:1, :], reads=["dram:xo0"], writes=["HR"])
    MLH = P.sbuf("MLH", [2, 8], F32)
    P.dma("sp", MLH[:], OH, writes=["MLH"])
    for s_ in range(8):
        hm = HM[s_ % 2]
        P.add("dve", lambda e, hm=hm, s_=s_: e.tensor_scalar(out=hm[:], in0=HR[:], scalar1=MLH[:, s_:s_ + 1], scalar2=None,
                                                            op0=ALU.mult), reads=["HR", "MLH"], writes=["HM%d" % (s_ % 2)])
        P.dma("sp", IB[2 * s_:2 * s_ + 2, :], hm[:], reads=["HM%d" % (s_ % 2)], writes=["dram:IBh"])
    P.cc(lambda e: e.collective_compute("AllReduce", ALU.add, replica_groups=RG, ins=[IB.opt()], outs=[OB.opt()]),
         reads=["dram:IBh"], writes=["dram:OBh"])
    P.end_phase()


def emit_ffn(P, x_in, OBh, selm, w_up, cwb, w_down, lng, lnb, dst, IDF, final=False):
    P.begin_phase()
    G = P.sbuf("G", [128, 44, TOK], BF16)
    XY = P.sbuf("XY", [128, 16 * (TOK + 2)], BF16)
    XT = XY[:].rearrange("p (k t) -> p k t", k=16)
    Y = XY[:, 0:16384].bitcast(F32).rearrange("p (t d) -> p t d", t=4)
    WB = [P.sbuf("WB%d" % i, [128, 44 * 256], BF16) for i in range(2)]
    STG = [WB[i][:, 0:4096].bitcast(F32) for i in range(2)]
    HT = P.sbuf("HT", [128, 4 * D], F32)
    H = [[HT[:, (gv * 2 + s) * 514:(gv * 2 + s + 1) * 514] for s in range(2)] for gv in range(2)]
    TT = [[HT[:, 2056 + (gv * 2 + s) * 512:2056 + (gv * 2 + s + 1) * 512] for s in range(2)] for gv in range(2)]
    Y2 = HT[:].rearrange("p (t d) -> p t d", t=4)
    htkeys = ["H%d_%d" % (gv, s) for gv in range(2) for s in range(2)] + ["T%d_%d" % (gv, s) for gv in range(2) for s in range(2)]
    CW = P.sbuf("CW", [128, 88, 4], F32)
    LG = WB[0][:, 0:2 * D].bitcast(F32)
    LB = WB[1][:, 0:2 * D].bitcast(F32)
    ST = P.sbuf("ST", [128, 4, 6], F32)
    MV = P.sbuf("MV", [128, 4], F32)
    PS = [P.psum("ps%d" % i, [128, 512]) for i in range(7)]
    P.dma("sp", CW[:], cwb, writes=["CW"])
    OHt = selm
    STH = G[:, 0:4, :].rearrange("p a t -> p (a t)").bitcast(F32)
    CND = [G[:, 4 + 4 * i:8 + 4 * i, :].rearrange("p a t -> p (a t)").bitcast(F32) for i in range(2)]
    gk_ = [("G", i) for i in range(12)]
    P.add("dve", lambda e: e.memset(STH, 0.0), writes=gk_)
    OBv = OBh.rearrange("(r h) d -> h r d", h=2)
    for r_ in range(8):
        cd = CND[r_ % 2]
        P.dma("sp", cd[0:2, :], OBv[:, r_, :], writes=["cnd%d" % (r_ % 2)])
        P.add("dve", lambda e, cd=cd, r_=r_: e.scalar_tensor_tensor(
            out=STH[0:2, :], in0=cd[0:2, :], scalar=OHt[0:2, r_:r_ + 1], in1=STH[0:2, :], op0=ALU.mult, op1=ALU.add),
            reads=["cnd%d" % (r_ % 2), "OH"] + gk_, writes=gk_)
    n = 0
    for tt in range(8):
        st = STG[tt % 2]
        sk = "WB%d" % (tt % 2)
        P.dma("sp", st, x_in[tt * 128:(tt + 1) * 128, :], writes=[sk])
        for c4 in range(4):
            pi = n % 3
            n += 1
            ps = PS[pi]
            for g in range(4):
                cb = c4 * 4 + g
                P.add("pe", lambda e, ps=ps, g=g, st=st, cb=cb: e.transpose(
                    ps[:, g * 128:(g + 1) * 128], st[:, cb * 128:(cb + 1) * 128], IDF[:]),
                    reads=[sk, "IDF"], writes=["ps%d" % pi])
            P.add("act", lambda e, ps=ps, c4=c4, tt=tt: e.activation(
                out=XT[:, c4 * 4:(c4 + 1) * 4, 1 + tt * 128: 1 + (tt + 1) * 128],
                in_=ps[:].rearrange("p (g t) -> p g t", g=4), func=AF.Copy), reads=["ps%d" % pi], writes=["XY"])
    for c4 in range(4):
        ps = PS[3 + c4 % 2]
        for g in range(4):
            cb = c4 * 4 + g
            P.add("pe", lambda e, ps=ps, g=g, cb=cb: e.transpose(
                ps[:, g * 128:(g + 1) * 128], STH[:, cb * 128:(cb + 1) * 128], IDF[:]),
                reads=gk_ + ["IDF"], writes=["ps%d" % (3 + c4 % 2)])
        P.add("act", lambda e, ps=ps, c4=c4: e.activation(
            out=XT[:, c4 * 4:(c4 + 1) * 4, 0:TOK + 2:TOK + 1],
            in_=ps[:].rearrange("p (g t) -> p g t", g=4)[:, :, 0:2], func=AF.Copy),
            reads=["ps%d" % (3 + c4 % 2)], writes=["XY"])
    w_up_v = w_up.rearrange("(k p) n -> p k n", p=128)
    ps_i = 0
    for grp in range(22):
        wb = WB[grp % 2]
        wv = wb[:, 0:2 * 16 * 256].rearrange("p (g k n) -> p g k n", g=2, k=16)
        wkey = "WB%d" % (grp % 2)
        for gv in range(2):
            c0 = gv * DFF + grp * 256
            P.dma("pool", wv[:, gv, :, :], w_up_v[:, :, c0:c0 + 256], writes=[wkey])
        for cc in range(2):
            c = grp * 2 + cc
            for blk in range(2):
                slot = (c * 2 + blk) % 2
                for gv in range(2):
                    pm = PS[ps_i % 3]
                    ph = PS[3 + ps_i % 2]
                    pmk = "ps%d" % (ps_i % 3)
                    phk = "ps%d" % (3 + ps_i % 2)
                    ps_i += 1
                    t0 = 1 + blk * 512
                    for k in range(16):
                        P.add("pe", lambda e, k=k, pm=pm, gv=gv, cc=cc, t0=t0, wv=wv: e.matmul(
                            pm[:], lhsT=wv[:, gv, k, cc * 128:(cc + 1) * 128], rhs=XT[:, k, t0:t0 + 512],
                            start=(k == 0), stop=(k == 15)), reads=[wkey, "XY"], writes=[pmk])
                    for k in range(16):
                        P.add("pe", lambda e, k=k, ph=ph, gv=gv, cc=cc, t0=t0, wv=wv: e.matmul(
                            ph[:, 0:2], lhsT=wv[:, gv, k, cc * 128:(cc + 1) * 128], rhs=XT[:, k, t0 - 1:t0 + 513:513],
                            start=(k == 0), stop=(k == 15)), reads=[wkey, "XY"], writes=[phk])
                    h = H[gv][slot]
                    hk = "H%d_%d" % (gv, slot)
                    P.add("act", lambda e, h=h, pm=pm: e.activation(out=h[:, 1:513], in_=pm[:], func=AF.Copy),
                          reads=[pmk], writes=[hk])
                    P.add("act", lambda e, h=h, ph=ph: e.activation(out=h[:, 0:514:513], in_=ph[:, 0:2], func=AF.Copy),
                          reads=[phk], writes=[hk])
                    tt_ = TT[gv][slot]
                    tk = "T%d_%d" % (gv, slot)
                    ch = gv * 44 + c
                    P.add("dve", lambda e, tt_=tt_, h=h, ch=ch: e.tensor_scalar(
                        out=tt_[:], in0=h[:, 0:512], scalar1=CW[:, ch, 0:1], scalar2=CW[:, ch, 3:4],
                        op0=ALU.mult, op1=ALU.add), reads=[hk, "CW"], writes=[tk])
                    for j in (1, 2):
                        P.add("dve", lambda e, tt_=tt_, h=h, ch=ch, j=j: e.scalar_tensor_tensor(
                            out=tt_[:], in0=h[:, j:j + 512], scalar=CW[:, ch, j:j + 1], in1=tt_[:],
                            op0=ALU.mult, op1=ALU.add), reads=[hk, "CW", tk], writes=[tk])
                tg = TT[0][slot]
                tv = TT[1][slot]
                P.add("act", lambda e, tg=tg: e.activation(out=tg[:], in_=tg[:], func=AF.Silu),
                      reads=["T0_%d" % slot], writes=["T0_%d" % slot])
                P.add("pool", lambda e, tg=tg, tv=tv, c=c, blk=blk: e.tensor_tensor(
                    out=G[:, c, blk * 512:(blk + 1) * 512], in0=tg[:], in1=tv[:], op=ALU.mult),
                    reads=["T0_%d" % slot, "T1_%d" % slot], writes=[("G", c)])
    w_down_v = w_down.rearrange("(k p) n -> p k n", p=128)
    x1_v = x_in.rearrange("(t p) d -> p t d", p=128)
    y_v = dst.rearrange("(t p) d -> p t d", p=128)
    li = 0
    P.dma("sp", Y[:], x1_v[:, 0:4, :], writes=["XY"])
    P.dma("sp", Y2, x1_v[:, 4:8, :], writes=["HT"] + htkeys)
    Yt = [(Y, "XY"), (Y2, "HT")]
    for cb in range(8):
        wb = WB[li % 2]
        wkey = "WB%d" % (li % 2)
        li += 1
        wv = wb[:].rearrange("p (k n) -> p k n", k=44)
        P.dma("pool", wv[:, 0:22, :], w_down_v[:, 0:22, cb * 256:(cb + 1) * 256], writes=[wkey])
        P.dma("pool", wv[:, 22:44, :], w_down_v[:, 22:44, cb * 256:(cb + 1) * 256], writes=[wkey + "b"])
        for tt in range(8):
            yt, yk = Yt[tt // 4]
            pi = 5 + (ps_i % 2)
            ps_i += 1
            pm = PS[pi]
            pmk = "ps%d" % pi
            for k in range(44):
                P.add("pe", lambda e, k=k, pm=pm, tt=tt, wv=wv: e.matmul(
                    pm[:, 0:256], lhsT=G[:, k, tt * 128:(tt + 1) * 128], rhs=wv[:, k, :],
                    start=(k == 0), stop=(k == 43)), reads=[wkey, wkey + "b", ("G", k)], writes=[pmk])
            ysl = yt[:, tt % 4, cb * 256:(cb + 1) * 256]
            P.add("dve", lambda e, ysl=ysl, pm=pm: e.scalar_tensor_tensor(
                out=ysl, in0=ysl, scalar=ALPHA, in1=pm[:, 0:256], op0=ALU.mult, op1=ALU.add),
                reads=[pmk, yk], writes=[yk])
    P.dma("sp", LG, lng, writes=["WB0", "WB0b", "lnp"])
    P.dma("sp", LB, lnb, writes=["WB1", "WB1b", "lnp"])
    for tt in range(8):
        yt, yk = Yt[tt // 4]
        ln_tile(P, yt[:, tt % 4, :], yk, LG, LB, yt[:, tt % 4, :], yk, ST, MV, "ln")
        P.dma("sp", y_v[:, tt, :], yt[:, tt % 4, :], reads=[yk], final=final)
    P.end_phase()


def emit_qkv(P, x_in, w, cos, sinp, gains, q_s, IB2, OH, IDF):
    P.begin_phase()
    XTt = P.sbuf("XT", [128, 16 * TOK], BF16)
    XT = XTt[:].rearrange("p (k t) -> p k t", k=16)
    TMP = XTt[:, 0:2 * NRM].bitcast(F32)
    TMP2 = XTt[:, 2 * NRM:4 * NRM].bitcast(F32)
    WB = [P.sbuf("WB%d" % i, [128, 16, 512], BF16) for i in range(2)]
    H = P.sbuf("H", [128, 8, QKV], F32)
    GN = P.sbuf("GN", [128, 4, HD], F32)
    TAB = P.sbuf("TAB", [128, 4, 8, HD], F32)
    SS = P.sbuf("SS", [128, 24], F32)
    KVM = [P.sbuf("KVM%d" % i, [128, 1024], BF16) for i in range(4)]
    PS = [P.psum("ps%d" % i, [128, 512]) for i in range(6)]
    STG = [H[:, i, 0:D] for i in range(2)]
    n = 0
    for tt in range(8):
        st = STG[tt % 2]
        sk = ("H", tt % 2)
        P.dma("sp", st, x_in[tt * 128:(tt + 1) * 128, :], writes=[sk])
        for c4 in range(4):
            pi = 4 + n % 2
            n += 1
            ps = PS[pi]
            for g in range(4):
                cb = c4 * 4 + g
                P.add("pe", lambda e, ps=ps, g=g, st=st, cb=cb: e.transpose(
                    ps[:, g * 128:(g + 1) * 128], st[:, cb * 128:(cb + 1) * 128], IDF[:]),
                    reads=[sk, "IDF"], writes=["ps%d" % pi])
            P.add("act", lambda e, ps=ps, c4=c4, tt=tt: e.activation(
                out=XT[:, c4 * 4:(c4 + 1) * 4, tt * 128:(tt + 1) * 128],
                in_=ps[:].rearrange("p (g t) -> p g t", g=4), func=AF.Copy), reads=["ps%d" % pi], writes=["XT"])
    P.dma("sp", GN[:], gains, writes=["GN"])
    for i in range(4):
        src = cos if i % 2 == 0 else sinp
        P.dma("sp", TAB[:, i, :, :], src.rearrange("(t p) f -> p t f", p=128), writes=[("TAB", i)])
        P.add("dve", lambda e, i=i: e.tensor_tensor(
            out=TAB[:, i, :, :], in0=TAB[:, i, :, :], in1=GN[:, i, :].unsqueeze(1).to_broadcast([128, 8, HD]), op=ALU.mult),
            reads=[("TAB", i), "GN"], writes=[("TAB", i)])
    w_v = w.rearrange("(k p) n -> p k n", p=128)

    def sink(P, cb, tt, pm, pmk):
        P.add("act", lambda e: e.activation(out=H[:, tt, cb * 512:(cb + 1) * 512], in_=pm[:], func=AF.Copy),
              reads=[pmk], writes=[("H", tt)])

    emit_proj_tm(P, XT, "XT", w_v, QKV, WB, PS[0:4], sink)
    NH = NQH + NKH
    q_v = q_s.rearrange("(t p) n -> p t n", p=128)
    mi = 0
    for tt in range(8):
        X = H[:, tt, 0:NRM]
        hk = ("H", tt)
        P.add("dve", lambda e, X=X: e.tensor_tensor(out=TMP, in0=X, in1=X, op=ALU.mult), reads=[hk], writes=["XT"])
        P.add("dve", lambda e: e.tensor_reduce(out=SS[:, 0:NH], in_=TMP.rearrange("p (h f) -> p h f", f=HD),
                                               axis=AX.X, op=ALU.add), reads=["XT"], writes=["SS"])
        P.add("dve", lambda e: e.tensor_scalar(out=SS[:, 0:NH], in0=SS[:, 0:NH], scalar1=1.0 / HD, scalar2=RMS_EPS,
                                               op0=ALU.mult, op1=ALU.add), reads=["SS"], writes=["SS"])
        P.add("act", lambda e: e.activation(out=SS[:, 0:NH], in_=SS[:, 0:NH], func=AF.Sqrt), reads=["SS"], writes=["SS"])
        P.add("dve", lambda e: e.reciprocal(out=SS[:, 0:NH], in_=SS[:, 0:NH]), reads=["SS"], writes=["SS"])
        P.add("dve", lambda e, X=X: e.tensor_tensor(
            out=TMP.rearrange("p (h f) -> p h f", f=HD), in0=X.rearrange("p (h f) -> p h f", f=HD),
            in1=SS[:, 0:NH].unsqueeze(2).to_broadcast([128, NH, HD]), op=ALU.mult), reads=[hk, "SS"], writes=["XT"])
        for (h0, nh, ti) in ((0, NQH, 0), (NQH, NKH, 2)):
            xn = TMP[:, h0 * HD:(h0 + nh) * HD]
            xo = X[:, h0 * HD:(h0 + nh) * HD]
            t2 = TMP2[:, h0 * HD:(h0 + nh) * HD]
            P.add("dve", lambda e, xn=xn, xo=xo, nh=nh, ti=ti, tt=tt: e.tensor_tensor(
                out=xo.rearrange("p (h f) -> p h f", f=HD), in0=xn.rearrange("p (h f) -> p h f", f=HD),
                in1=TAB[:, ti, tt, :].unsqueeze(1).to_broadcast([128, nh, HD]), op=ALU.mult),
                reads=["XT", ("TAB", ti)], writes=[hk])
            for s in range(2):
                P.add("dve", lambda e, xn=xn, t2=t2, nh=nh, ti=ti, tt=tt, s=s: e.tensor_tensor(
                    out=t2.rearrange("p (h a s f) -> p h a s f", a=2, s=2, f=32)[:, :, :, s, :],
                    in0=xn.rearrange("p (h a s f) -> p h a s f", a=2, s=2, f=32)[:, :, :, 1 - s, :],
                    in1=TAB[:, ti + 1, tt, :].rearrange("p (a s f) -> p a s f", a=2, s=2)[:, :, s, :]
                        .unsqueeze(1).to_broadcast([128, nh, 2, 32]), op=ALU.mult),
                    reads=["XT", ("TAB", ti + 1)], writes=["XT2"])
            P.add("dve", lambda e, xo=xo, t2=t2: e.tensor_tensor(out=xo, in0=xo, in1=t2, op=ALU.add),
                  reads=[hk, "XT2"], writes=[hk])
        P.dma("sp", q_v[:, tt, :], H[:, tt, 0:NQH * HD], reads=[hk], writes=["dram:q"])
        for s_ in range(8):
            kvm = KVM[mi % 4]
            kk = "KVM%d" % (mi % 4)
            mi += 1
            P.add("pool", lambda e, kvm=kvm, tt=tt, s_=s_: e.tensor_scalar(
                out=kvm[:], in0=H[:, tt, NQH * HD:QKV], scalar1=OH[:, s_:s_ + 1], scalar2=0.0, op0=ALU.mult, op1=ALU.add),
                reads=[hk, "OH"], writes=[kk])
            P.dma("sp", IB2[s_ * TOK + tt * 128: s_ * TOK + (tt + 1) * 128, :], kvm[:], reads=[kk], writes=["dram:IB2"])
    P.end_phase()


def emit_attn(P, q_s, OB2, o_s, IDF, IDB, OHB):
    P.begin_phase()
    QT = P.sbuf("QT", [128, NQH, TOK], BF16)
    KT = P.sbuf("KT", [128, NKH, T], BF16)
    VE = P.sbuf("VE", [128, 32, NKH, VW], BF16)
    PT = [P.sbuf("PT%d" % i, [128, 512], BF16) for i in range(3)]
    OBt = [P.sbuf("OB%d" % i, [128, NQH * HD], F32) for i in range(2)]
    RC = P.sbuf("RC", [128, 8], F32)
    IDM2 = P.sbuf("IDM2", [128, 2, 128], BF16)
    CK = [P.sbuf("CK%d" % i, [128, 2, 8, 512], BF16) for i in range(2)]
    PS = [P.psum("ps%d" % i, [128, 512]) for i in range(7)]
    load_T(P, q_s, TOK, NQH * HD, QT, "QT", OBt, IDF, PS[4:7], ["ps4", "ps5", "ps6"], skeys=["OB0", "OB1"])
    for b_ in range(2):
        P.add("dve", lambda e, b_=b_: e.tensor_scalar(out=IDM2[:, b_, :], in0=IDB[:], scalar1=OHB[:, b_:b_ + 1], scalar2=None,
                                                     op0=ALU.mult), reads=["IDB", "OHB"], writes=["IDM2"])
    P.add("dve", lambda e: e.memset(VE[:].rearrange("p c k w -> p (c k w)"), 1.0), writes=["VE"])
    n = 0
    for jp in range(4):
        ck = CK[0]
        for b_ in range(2):
            P.dma("sp", ck[:, b_, :, :], OB2[(4 * b_ + jp) * TOK:(4 * b_ + jp + 1) * TOK, 0:512].rearrange("(t p) n -> p t n", p=128),
                  writes=["CK0"])
        for kv in range(NKH):
            for t4 in range(2):
                pi = 4 + n % 3
                n += 1
                ps = PS[pi]
                for g in range(4):
                    tt = t4 * 4 + g
                    for b_ in range(2):
                        P.add("pe", lambda e, ps=ps, g=g, ck=ck, b_=b_, tt=tt, kv=kv: e.matmul(
                            ps[:, g * 128:(g + 1) * 128], lhsT=ck[:, b_, tt, kv * 128:(kv + 1) * 128], rhs=IDM2[:, b_, :],
                            start=(b_ == 0), stop=(b_ == 1)), reads=["CK0", "IDM2"], writes=["ps%d" % pi])
                P.add("act", lambda e, ps=ps, kv=kv, jp=jp, t4=t4: e.activation(
                    out=KT[:, kv, jp * TOK + t4 * 512: jp * TOK + (t4 + 1) * 512], in_=ps[:], func=AF.Copy),
                    reads=["ps%d" % pi], writes=["KT"])
        cv = CK[1]
        for b_ in range(2):
            P.dma("sp", cv[:, b_, :, :], OB2[(4 * b_ + jp) * TOK:(4 * b_ + jp + 1) * TOK, 512:1024].rearrange("(t p) n -> p t n", p=128),
                  writes=["CK1"])
        vev = VE[:, jp * 8:(jp + 1) * 8, :, 0:HD]
        P.add("dve", lambda e, vev=vev, cv=cv: e.tensor_scalar(
            out=vev, in0=cv[:, 0, :, :].rearrange("p t (k f) -> p t k f", k=NKH), scalar1=OHB[:, 0:1], scalar2=None, op0=ALU.mult),
            reads=["CK1", "OHB", "VE"], writes=["VE"])
        P.add("dve", lambda e, vev=vev, cv=cv: e.scalar_tensor_tensor(
            out=vev, in0=cv[:, 1, :, :].rearrange("p t (k f) -> p t k f", k=NKH), scalar=OHB[:, 1:2], in1=vev,
            op0=ALU.mult, op1=ALU.add), reads=["CK1", "OHB", "VE"], writes=["VE"])
    o_v = o_s.rearrange("(t p) n -> p t n", p=128)
    scale = float(HD) ** -0.5
    iters = [(qt, kv, sc) for qt in range(8) for kv in range(NKH) for sc in range(32)]

    def emit_S(i):
        qt, kv, sc = iters[i]
        sb = 4 + i % 3
        st = PS[sb]
        stk = "ps%d" % sb
        pt = PT[i % 3]
        ptk = "PT%d" % (i % 3)
        P.add("pe", lambda e, st=st, kv=kv, sc=sc, qt=qt: e.matmul(
            st[:].rearrange("p (h q) -> p h q", h=4), lhsT=KT[:, kv, sc * 128:(sc + 1) * 128],
            rhs=QT[:, 4 * kv:4 * kv + 4, qt * 128:(qt + 1) * 128], start=True, stop=True),
            reads=["QT", "KT"], writes=[stk])
        P.add("act", lambda e, st=st, pt=pt: e.activation(out=pt[:], in_=st[:], func=AF.Exp, scale=scale),
              reads=[stk], writes=[ptk])

    emit_S(0)
    emit_S(1)
    for i, (qt, kv, sc) in enumerate(iters):
        pt = PT[i % 3]
        ptk = "PT%d" % (i % 3)
        ob = OBt[qt % 2]
        obk = "OB%d" % (qt % 2)
        for hh in range(4):
            P.add("pe", lambda e, hh=hh, pt=pt, sc=sc, kv=kv: e.matmul(
                PS[hh][:, 0:VW], lhsT=pt[:, hh * 128:(hh + 1) * 128], rhs=VE[:, sc, kv, :],
                start=(sc == 0), stop=(sc == 31)), reads=[ptk, "VE"], writes=["ps%d" % hh])
        if i + 2 < len(iters):
            emit_S(i + 2)
        if sc == 31:
            for hh in range(4):
                hq = 4 * kv + hh
                P.add("dve", lambda e, hh=hh: e.reciprocal(out=RC[:, hh:hh + 1], in_=PS[hh][:, HD:HD + 1]),
                      reads=["ps%d" % hh], writes=[("RC", hh)])
                P.add("dve", lambda e, hh=hh, hq=hq, ob=ob: e.tensor_scalar(
                    out=ob[:, hq * HD:(hq + 1) * HD], in0=PS[hh][:, 0:HD], scalar1=RC[:, hh:hh + 1], scalar2=None,
                    op0=ALU.mult), reads=["ps%d" % hh, ("RC", hh)], writes=[obk])
            if kv == NKH - 1:
                P.dma("sp", o_v[:, qt, :], ob[:], reads=[obk], writes=["dram:o"])
    P.end_phase()


def load_T(P, src, ntok, W, dst, dkey, ST, IDF, PS, pskeys, tok_off=0, skeys=("ldT_st0", "ldT_st1")):
    n = 0
    for tt in range(ntok // 128):
        st = ST[tt % 2]
        sk = skeys[tt % 2]
        P.dma("sp", st[:, 0:W], src[tt * 128:(tt + 1) * 128, :], writes=[sk])
        for c4 in range(W // 512):
            pi = n % len(PS)
            n += 1
            ps = PS[pi]
            for g in range(4):
                cb = c4 * 4 + g
                P.add("pe", lambda e, ps=ps, g=g, st=st, cb=cb: e.transpose(
                    ps[:, g * 128:(g + 1) * 128], st[:, cb * 128:(cb + 1) * 128], IDF[:]),
                    reads=[sk, "IDF"], writes=[pskeys[pi]])
            P.add("act", lambda e, ps=ps, c4=c4, tt=tt: e.activation(
                out=dst[:, c4 * 4:(c4 + 1) * 4, tok_off + tt * 128: tok_off + (tt + 1) * 128],
                in_=ps[:].rearrange("p (g t) -> p g t", g=4), func=AF.Copy),
                reads=[pskeys[pi]], writes=[dkey])


_EI_NAMES = []


def build_fused(dbg=False, stop_after=99):
    nc = bass.Bass("TRN2", target_bir_lowering=False)

    def EI(name, shape, dt=F32):
        return nc.dram_tensor(name, list(shape), dt, kind="ExternalInput").ap()

    def SC(name, shape, dt=F32, cc=False):
        if dbg and not cc:
            return nc.dram_tensor(name, list(shape), dt, kind="ExternalOutput").ap()
        return nc.dram_tensor(name, list(shape), dt).ap()

    names = []

    def EIs(stage, name, shape, dt=F32):
        if stage > stop_after:
            return None
        names.append(name)
        return nc.dram_tensor(name, list(shape), dt, kind="ExternalInput").ap()

    xT_b = EIs(1, "xT_b", [D, T]); xT_own = EIs(2, "xT_own", [D, TOK]); x_own = EIs(4, "x_own", [TOK, D])
    w_hq = EIs(1, "w_hq", [D, 768]); w_hi = EIs(1, "w_hi", [D, 512]); w_uv = EIs(2, "w_uv", [D, 2048])
    lbt = EIs(3, "lbt", [2, 128, 3]); gn = EIs(3, "gn", [2, 64, 128]); masks = EIs(3, "masks", [2, 128, 128])
    rmask = EIs(3, "rmask", [128, 512]); ident = EIs(1, "ident", [128, 128])
    wsT = EIs(2, "wsT", [8, 128, 128]); biasT = EIs(2, "biasT", [128, 8]); glng = EIs(2, "glng", [128, GW]); glnb = EIs(2, "glnb", [128, GW])
    w_out = [EIs(4 + 4 * l, "w_out%d" % l, [D, D]) for l in range(2)]
    ln1g = [EIs(4 + 4 * l, "ln1g%d" % l, [128, D]) for l in range(2)]; ln1b = [EIs(4 + 4 * l, "ln1b%d" % l, [128, D]) for l in range(2)]
    w_up = [EIs(5 + 4 * l, "w_up%d" % l, [D, 2 * DFF]) for l in range(2)]; cwb = [EIs(5 + 4 * l, "cwb%d" % l, [128, 88, 4]) for l in range(2)]
    w_down = [EIs(5 + 4 * l, "w_down%d" % l, [DFF, D]) for l in range(2)]
    ln2g = [EIs(5 + 4 * l, "ln2g%d" % l, [128, D]) for l in range(2)]; ln2b = [EIs(5 + 4 * l, "ln2b%d" % l, [128, D]) for l in range(2)]
    w_qkv = EIs(6, "w_qkv", [D, QKV]); cos = EIs(6, "cos", [TOK, HD]); sinp = EIs(6, "sinp", [TOK, HD]); gains = EIs(6, "gains", [128, 4, HD])
    oh8 = EIs(1, "oh8", [128, 8]); ohb = EIs(1, "ohb", [128, 2]); mlh = EIs(4, "mlh", [2, 8])
    _EI_NAMES[:] = names
    y = nc.dram_tensor("y", [TOK, D], F32, kind="ExternalOutput").ap()
    qT_s = SC("qT_s", [2, 128, T]); ffT_s = SC("ffT_s", [2, 128, T]); fbT_s = SC("fbT_s", [2, 128, T])
    iv_s = SC("iv_s", [2, T, 128]); gg_s = SC("gg_s", [2, T, 128])
    ob_s = SC("ob_s", [TOK, GW])
    IB1 = SC("IB1", [8 * 2 * T, 128], BF16, cc=True); OB1 = SC("OB1", [8 * 2 * T, 128], BF16, cc=True)
    x1_s = SC("x1_s", [TOK, D]); x2_s = SC("x2_s", [TOK, D]); q_s = SC("q_s", [TOK, NQH * HD]); o_s = SC("o_s", [TOK, D])
    x3_s = SC("x3_s", [TOK, D])
    IB2 = SC("IB2", [8 * TOK, 1024], BF16, cc=True); OB2 = SC("OB2", [8 * TOK, 1024], BF16, cc=True)
    IB3 = SC("IB3", [16, D], F32, cc=True); OB3 = SC("OB3", [16, D], F32, cc=True)
    IB4 = SC("IB4", [16, D], F32, cc=True); OB4 = SC("OB4", [16, D], F32, cc=True)
    RG = [list(range(NCORES))]

    P = Prog(nc)
    P.setup_barrier()
    IDF = P.sbuf("IDF", [128, 128], F32)
    IDB = P.sbuf("IDB", [128, 128], BF16)
    OH = P.sbuf("OH", [128, 8], F32)
    OHB = P.sbuf("OHB", [128, 2], F32)

    def done():
        P.stack.close()
        return nc, P

    P.begin_phase()
    P.dma("sp", IDF[:], ident, writes=["IDF"])
    P.dma("pool", IDB[:], ident, writes=["IDB"])
    P.dma("sp", OH[:], oh8, writes=["OH"])
    P.dma("sp", OHB[:], ohb, writes=["OHB"])
    XTb = [P.sbuf("XTb%d" % i, [128, 16, 512], BF16) for i in range(2)]
    WQ = P.sbuf("WQ", [128, 16, 768], BF16)
    WI = P.sbuf("WI", [128, 16, 512], BF16)
    OBF = [P.sbuf("OBF%d" % i, [128, 512], F32) for i in range(4)]
    PS = [P.psum("ps%d" % i, [128, 512]) for i in range(4)]
    P.dma("pool", WQ[:], w_hq.rearrange("(k p) n -> p k n", p=128), writes=["WQ"])
    P.dma("pool", WI[:], w_hi.rearrange("(k p) n -> p k n", p=128), writes=["WI"])
    xTb_v = xT_b.rearrange("(k p) t -> p k t", p=128)
    dstT = [qT_s, ffT_s, fbT_s]
    n = 0
    for tb in range(8):
        xt = XTb[tb % 2]
        xk = "XTb%d" % (tb % 2)
        for k4 in range(4):
            P.dma("pool", xt[:, k4 * 4:(k4 + 1) * 4, :], xTb_v[:, k4 * 4:(k4 + 1) * 4, tb * 512:(tb + 1) * 512], writes=[xk])
        for ch in range(6):
            pi = n % 4
            n += 1
            ps, ob = PS[pi], OBF[pi]
            for k in range(16):
                P.add("pe", lambda e, ps=ps, k=k, ch=ch, xt=xt: e.matmul(
                    ps[:], lhsT=WQ[:, k, ch * 128:(ch + 1) * 128], rhs=xt[:, k, :], start=(k == 0), stop=(k == 15)),
                    reads=["WQ", xk], writes=["ps%d" % pi])
            P.add("act", lambda e, ps=ps, ob=ob: e.activation(out=ob[:], in_=ps[:], func=AF.Copy),
                  reads=["ps%d" % pi], writes=["OBF%d" % pi])
            P.dma("sp", dstT[ch // 2][ch % 2, :, tb * 512:(tb + 1) * 512], ob[:], reads=["OBF%d" % pi], writes=["dram:hq"])
        for tt in range(4):
            pi = n % 4
            n += 1
            ps, ob = PS[pi], OBF[pi]
            for k in range(16):
                P.add("pe", lambda e, ps=ps, k=k, tt=tt, xt=xt: e.matmul(
                    ps[:], lhsT=xt[:, k, tt * 128:(tt + 1) * 128], rhs=WI[:, k, :], start=(k == 0), stop=(k == 15)),
                    reads=["WI", xk], writes=["ps%d" % pi])
            P.add("act", lambda e, ps=ps, ob=ob: e.activation(out=ob[:], in_=ps[:], func=AF.Copy),
                  reads=["ps%d" % pi], writes=["OBF%d" % pi])
            r0 = tb * 512 + tt * 128
            for q4 in range(4):
                dst = (iv_s if q4 < 2 else gg_s)[q4 % 2, r0:r0 + 128, :]
                P.dma("sp", dst, ob[:, q4 * 128:(q4 + 1) * 128], reads=["OBF%d" % pi], writes=["dram:hi"])
    P.end_phase()
    if stop_after <= 1:
        return done()

    P.begin_phase()
    XT = P.sbuf("XT", [128, 16, TOK], BF16)
    WB = [P.sbuf("WB%d" % i, [128, 16, 512], BF16) for i in range(2)]
    UV = P.sbuf("UV", [128, 8, 2 * GW], F32)
    VN = P.sbuf("VN", [128, 8, GW], BF16)
    WS = P.sbuf("WS", [128, 8, 128], BF16)
    BI = P.sbuf("BI", [128, 8], F32)
    LG = P.sbuf("LG", [128, GW], F32)
    LB = P.sbuf("LB", [128, GW], F32)
    ST2 = P.sbuf("ST2", [128, 2, 6], F32)
    MV = P.sbuf("MV", [128, 4], F32)
    PS = [P.psum("ps%d" % i, [128, 512]) for i in range(6)]
    xTo_v = xT_own.rearrange("(k p) t -> p k t", p=128)
    for k in range(16):
        P.dma("pool", XT[:, k, :], xTo_v[:, k, :], writes=["XT"])
    P.dma("pool", WS[:], wsT.rearrange("g s t -> s g t"), writes=["WS"])
    P.dma("sp", BI[:], biasT, writes=["BI"])
    P.dma("sp", LG[:], glng, writes=["lnp"])
    P.dma("sp", LB[:], glnb, writes=["lnp2"])

    def sink_uv(P, cb, tt, pm, pmk):
        P.add("act", lambda e: e.activation(out=UV[:, tt, cb * 512:(cb + 1) * 512], in_=pm[:], func=AF.Copy),
              reads=[pmk], writes=[("UV", tt)])

    emit_proj_tm(P, XT, "XT", w_uv.rearrange("(k p) n -> p k n", p=128), 2048, WB, PS[0:4], sink_uv)
    ob_v = ob_s.rearrange("(t p) n -> p t n", p=128)
    ps_i = 0
    for n in range(8):
        yv = UV[:, n, GW:2 * GW]
        uk = ("UV", n)
        for c in range(2):
            P.add("dve", lambda e, c=c, yv=yv: e.bn_stats(out=ST2[:, c, :], in_=yv[:, c * 512:(c + 1) * 512]),
                  reads=[uk], writes=["st"])
        P.add("dve", lambda e: e.bn_aggr(out=MV[:, 0:2], in_=ST2[:].rearrange("p a b -> p (a b)")), reads=["st"], writes=["mv"])
        P.add("dve", lambda e: e.tensor_scalar(out=MV[:, 2:3], in0=MV[:, 1:2], scalar1=LN_EPS, scalar2=None, op0=ALU.add),
              reads=["mv"], writes=["mv"])
        P.add("act", lambda e: e.activation(out=MV[:, 2:3], in_=MV[:, 2:3], func=AF.Sqrt), reads=["mv"], writes=["mv"])
        P.add("dve", lambda e: e.reciprocal(out=MV[:, 3:4], in_=MV[:, 2:3]), reads=["mv"], writes=["mv"])
        P.add("dve", lambda e, yv=yv: e.tensor_scalar(out=yv, in0=yv, scalar1=MV[:, 0:1], scalar2=MV[:, 3:4],
                                                      op0=ALU.subtract, op1=ALU.mult), reads=[uk, "mv"], writes=[uk])
        P.add("dve", lambda e, yv=yv: e.tensor_tensor(out=yv, in0=yv, in1=LG[:], op=ALU.mult), reads=[uk, "lnp"], writes=[uk])
        P.add("dve", lambda e, yv=yv, n=n: e.tensor_tensor(out=VN[:, n, :], in0=yv, in1=LB[:], op=ALU.add),
              reads=[uk, "lnp2"], writes=[("VN", n)])
        for g4 in range(2):
            pi = 4 + ps_i % 2
            ps_i += 1
            pm = PS[pi]
            pmk = "ps%d" % pi
            for gg_ in range(4):
                g = g4 * 4 + gg_
                P.add("pe", lambda e, pm=pm, g=g, gg_=gg_, n=n: e.matmul(
                    pm[:, gg_ * 128:(gg_ + 1) * 128], lhsT=WS[:, g, :], rhs=VN[:, n, g * 128:(g + 1) * 128],
                    start=True, stop=True), reads=["WS", ("VN", n)], writes=[pmk])
            for gg_ in range(4):
                g = g4 * 4 + gg_
                usl = UV[:, n, g * 128:(g + 1) * 128]
                P.add("dve", lambda e, pm=pm, g=g, gg_=gg_, usl=usl: e.scalar_tensor_tensor(
                    out=usl, in0=pm[:, gg_ * 128:(gg_ + 1) * 128], scalar=BI[:, g:g + 1], in1=usl,
                    op0=ALU.add, op1=ALU.mult), reads=[pmk, "BI", uk], writes=[uk])
        P.dma("sp", ob_v[:, n, :], UV[:, n, 0:GW], reads=[uk], writes=["dram:ob"])
    P.end_phase()
    if stop_after <= 2:
        return done()

    P.begin_phase()
    emit_hgrn(P, qT_s, ffT_s, fbT_s, iv_s, gg_s, lbt, gn, masks, rmask, IDB, OH, IB1)
    P.cc(lambda e: e.collective_compute("AllReduce", ALU.add, replica_groups=RG, ins=[IB1.opt()], outs=[OB1.opt()]),
         reads=["dram:IB1"], writes=["dram:OB1"])
    P.end_phase()
    if stop_after <= 3:
        return done()

    OB1v = OB1.rearrange("(s h t) v -> s h t v", s=8, h=2)

    def fill_G0(P, G, ST, PS, pskeys):
        IDM = P.sbuf("IDM", [128, 8, 128], BF16)
        CT = [P.sbuf("CT%d" % i, [128, 8, 8, 128], BF16) for i in range(2)]
        for c in range(8):
            P.add("dve", lambda e, c=c: e.tensor_scalar(out=IDM[:, c, :], in0=IDB[:], scalar1=OH[:, c:c + 1], scalar2=None,
                                                        op0=ALU.mult), reads=["IDB", "OH"], writes=["IDM"])
        n = 0
        for jp in range(4):
            for hd in range(2):
                H = 2 * jp + hd
                ct = CT[H % 2]
                ck = "CT%d" % (H % 2)
                for c in range(8):
                    bp, jj = divmod(c, 4)
                    P.dma("sp", ct[:, c, :, :], OB1v[4 * bp + jp, hd, jj * TOK:(jj + 1) * TOK, :].rearrange("(t p) v -> p t v", p=128),
                          writes=[ck])
                for t4 in range(2):
                    pi = n % len(PS)
                    n += 1
                    ps = PS[pi]
                    for g in range(4):
                        tt = t4 * 4 + g
                        for c in range(8):
                            P.add("pe", lambda e, ps=ps, g=g, ct=ct, c=c, tt=tt: e.matmul(
                                ps[:, g * 128:(g + 1) * 128], lhsT=ct[:, c, tt, :], rhs=IDM[:, c, :],
                                start=(c == 0), stop=(c == 7)), reads=[ck, "IDM"], writes=[pskeys[pi]])
                    P.add("act", lambda e, ps=ps, H=H, t4=t4: e.activation(
                        out=G[:, H, t4 * 512:(t4 + 1) * 512], in_=ps[:], func=AF.Copy), reads=[pskeys[pi]], writes=[("G", H)])
        load_T(P, ob_s, TOK, GW, G[:, 8:16, :], "Gb", ST, IDF, PS, pskeys, skeys=("WB0", "WB1"))

    emit_projln(P, fill_G0, w_out[0], x_own, ln1g[0], ln1b[0], x1_s, IB3, OB3, mlh, RG, ["dram:OB1", "dram:ob"], IDF)
    if stop_after <= 4:
        return done()

    emit_ffn(P, x1_s, OB3, OH, w_up[0], cwb[0], w_down[0], ln2g[0], ln2b[0], x2_s, IDF)
    if stop_after <= 5:
        return done()

    emit_qkv(P, x2_s, w_qkv, cos, sinp, gains, q_s, IB2, OH, IDF)
    P.begin_phase()
    P.cc(lambda e: e.collective_compute("AllReduce", ALU.add, replica_groups=RG, ins=[IB2.opt()], outs=[OB2.opt()]),
         reads=[], writes=[])
    P.end_phase()
    if stop_after <= 6:
        return done()

    emit_attn(P, q_s, OB2, o_s, IDF, IDB, OHB)
    if stop_after <= 7:
        return done()

    def fill_G1(P, G, ST, PS, pskeys):
        load_T(P, o_s, TOK, D, G, "Gb", ST, IDF, PS, pskeys, skeys=("WB0", "WB1"))

    emit_projln(P, fill_G1, w_out[1], x2_s, ln1g[1], ln1b[1], x3_s, IB4, OB4, mlh, RG, [], IDF)
    if stop_after <= 8:
        return done()

    emit_ffn(P, x3_s, OB4, OH, w_up[1], cwb[1], w_down[1], ln2g[1], ln2b[1], y, IDF, final=True)
    return done()


def _fused_inputs(inp):
    f = lambda a: np.ascontiguousarray(np.asarray(a, dtype=np.float32))
    x = f(inp["x"])
    w_in = f(inp["w_in_ab"])[0]
    lb_table = f(inp["hgrn_lb_table"])
    norm_g = f(inp["hgrn_norm_g"])[0]
    idx = np.arange(128)
    same = (idx[:, None] // 64) == (idx[None, :] // 64)
    mfw = (same & (idx[:, None] <= idx[None, :])).astype(np.float32)
    mbw = (same & (idx[:, None] >= idx[None, :])).astype(np.float32)
    masks = np.stack([mfw, mbw], axis=0)
    rmask = np.ones((128, 512), np.float32)
    rmask[:, ::64] = 0.0
    ident = np.eye(128, dtype=np.float32)
    cos, sinp = _rope_tables()
    gq, gk = f(inp["q_norm_g"])[0], f(inp["k_norm_g"])[0]
    gains = np.stack([gq, _swap32(gq), gk, _swap32(gk)], axis=0).astype(np.float32)
    gains = np.ascontiguousarray(np.broadcast_to(gains[None], (128, 4, HD)))
    shared = {
        "w_uv": f(w_in[:, 5120:7168]), "masks": masks, "rmask": rmask, "ident": ident,
        "wsT": f(f(inp["gmlp_ws"])[0].transpose(0, 2, 1)), "biasT": f(f(inp["gmlp_bias"])[0].T),
        "glng": _bc(f(inp["gmlp_ln_g"])[0]), "glnb": _bc(f(inp["gmlp_ln_b"])[0]),
        "w_out0": f(inp["w_out_ab"])[0], "w_out1": f(inp["w_out_attn"])[0],
        "w_qkv": f(inp["w_in_attn"])[0], "gains": gains,
    }
    for l in range(2):
        cw = np.concatenate([f(inp["ffn_conv_w"])[l], f(inp["ffn_conv_b"])[l][None, :]], axis=0)
        shared["cwb%d" % l] = f(cw.reshape(4, 88, 128).transpose(2, 1, 0))
        shared["w_up%d" % l] = f(inp["ffn_up"])[l]
        shared["w_down%d" % l] = f(inp["ffn_down"])[l]
        shared["ln1g%d" % l] = _bc(f(inp["ln1_g"])[l]); shared["ln1b%d" % l] = _bc(f(inp["ln1_b"])[l])
        shared["ln2g%d" % l] = _bc(f(inp["ln2_g"])[l]); shared["ln2b%d" % l] = _bc(f(inp["ln2_b"])[l])
    xT = [f(x[b].T) for b in range(NB)]
    in_maps = []
    for c in range(NCORES):
        b, j = divmod(c, 4)
        hs = [2 * j, 2 * j + 1]
        m = dict(shared)
        m["xT_b"] = xT[b]
        m["xT_own"] = f(xT[b][:, j * TOK:(j + 1) * TOK])
        m["x_own"] = f(x[b, j * TOK:(j + 1) * TOK, :])
        m["w_hq"] = f(np.concatenate([w_in[:, base + H * 128: base + (H + 1) * 128] for base in (0, 1024, 2048) for H in hs], axis=1))
        m["w_hi"] = f(np.concatenate([w_in[:, base + H * 128: base + (H + 1) * 128] for base in (3072, 4096) for H in hs], axis=1))
        m["lbt"] = f(np.stack([lb_table[:, H * 128:(H + 1) * 128].T for H in hs], axis=0))
        m["gn"] = f(np.stack([np.broadcast_to(norm_g[H * 128:(H + 1) * 128][None, :], (64, 128)) for H in hs], axis=0))
        m["cos"] = f(cos[j * TOK:(j + 1) * TOK]); m["sinp"] = f(sinp[j * TOK:(j + 1) * TOK])
        oh = np.zeros((128, 8), np.float32); oh[:, c] = 1.0
        ob_ = np.zeros((128, 2), np.float32); ob_[:, b] = 1.0
        mlh_ = np.zeros((2, 8), np.float32)
        if j < 3:
            mlh_[0, c + 1] = 1.0
        if j > 0:
            mlh_[1, c - 1] = 1.0
        m["oh8"] = oh; m["ohb"] = ob_; m["mlh"] = mlh_
        in_maps.append(m)
    return in_maps


_FUSED = {}


def kernel(**inp):
    if "nc" not in _FUSED:
        _FUSED["nc"] = build_fused(dbg=False)[0]
    in_maps = _fused_inputs(inp)
    in_maps = [{k: v for k, v in m.items() if k in _EI_NAMES} for m in in_maps]
    res = run_bass_kernel_spmd(_FUSED["nc"], in_maps, core_ids=list(range(NCORES)))
    return _gather_tok(res, "y", D)
```

```python
import numpy as np
from contextlib import ExitStack
import concourse.bass as bass
import concourse.mybir as mybir
from concourse.bass_utils import run_bass_kernel_spmd

F32 = mybir.dt.float32
BF16 = mybir.dt.bfloat16
AF = mybir.ActivationFunctionType
ALU = mybir.AluOpType
AX = mybir.AxisListType

SEM_EPOCH = 24000
DMA_LANES = 6


class Op:
    pass


class Prog:
    ENGS = ("pe", "act", "dve", "pool", "sp")

    def __init__(self, nc, same_engine_sync=True):
        self.nc = nc
        self.ops = []
        self.last_w = {}
        self.readers = {}
        self.same_engine_sync = same_engine_sync
        self.stack = ExitStack()
        self.psum_banks = []

    def _uname(self, name):
        self._ucnt = getattr(self, "_ucnt", 0) + 1
        return "%s_u%d" % (name, self._ucnt)

    def sbuf(self, name, shape, dtype):
        return self.stack.enter_context(self.nc.sbuf_tensor(self._uname(name), list(shape), dtype))

    def psum(self, name, shape, dtype=F32):
        return self.stack.enter_context(self.nc.psum_tensor(self._uname(name), list(shape), dtype))

    def add(self, eng, fn, reads=(), writes=(), dma=False, final=False):
        op = Op()
        op.eng = eng
        op.fn = fn
        op.is_dma = dma
        op.sig = False
        op.sem = None
        op.val = 0
        op.final = final
        op.idx = len(self.ops)
        deps = set()
        mykey = ("dma", op.idx) if dma else eng
        for k in reads:
            deps.update(self.last_w.get(k, {}).values())
        for k in writes:
            gen = self.last_w.get(k)
            if gen is None:
                gen = self.last_w[k] = {}
            rd = self.readers.get(k)
            if rd:
                deps.update(rd.values())
                deps.update(gen.values())
                gen.clear()
                rd.clear()
            elif dma:
                deps.update(v for kk, v in gen.items() if not isinstance(kk, tuple))
            else:
                deps.update(gen.values())
        deps.discard(op.idx)
        op.deps = deps
        for k in reads:
            self.readers.setdefault(k, {})[mykey] = op.idx
        for k in writes:
            self.last_w[k][mykey] = op.idx
        self.ops.append(op)
        return op

    def dma(self, eng, out, in_, reads=(), writes=(), final=False, **kw):
        return self.add(eng, lambda e: e.dma_start(out=out, in_=in_, **kw), reads, writes, dma=True, final=final)

    def _needs_wait(self, op, d):
        if d.is_dma:
            return True
        if d.eng != op.eng:
            return True
        if op.is_dma:
            return True
        if op.eng == "pe":
            return False
        return self.same_engine_sync

    def cc(self, fn, reads=(), writes=()):
        op = self.add("pool", fn, reads, writes, dma=True)
        op.is_cc = True
        return op

    def begin_phase(self):
        self.pstack = ExitStack()
        self.main_stack = self.stack
        self.stack = self.pstack
        self.phase_start = len(self.ops)

    def end_phase(self):
        self.emit_ops(self.phase_start, len(self.ops), barrier=True)
        self.pstack.close()
        self.stack = self.main_stack
        self.last_w = {}
        self.readers = {}

    def emit(self):
        self.emit_ops(0, len(self.ops), barrier=False)
        self.stack.close()

    def _sem(self, name):
        if name not in self.sems:
            self.sems[name] = self.main_stack.enter_context(self.nc.semaphore(name)) if hasattr(self, "main_stack") and self.main_stack is not None \
                else self.stack.enter_context(self.nc.semaphore(name))
        return self.sems[name]

    def emit_ops(self, lo, hi, barrier):
        nc = self.nc
        ops = self.ops
        if not hasattr(self, "sems"):
            self.sems = {}
            self.cnt = {}
            self.dma_n = {}
            self.lane_prev = {}
            self.known = {e: {} for e in self.ENGS}
            self.nbar = 0
            self.nsig = 0
            self.nwaits = 0
        cur = ops[lo:hi]
        for op in cur:
            for di in op.deps:
                d = ops[di]
                if self._needs_wait(op, d):
                    d.sig = True
            if op.final:
                op.sig = True
        cnt, dma_n, lane_prev = self.cnt, self.dma_n, self.lane_prev
        for op in cur:
            if getattr(op, "is_cc", False):
                c = cnt.get("cc", 0) + 1
                cnt["cc"] = c
                op.sem = "cc"
                op.val = c
                op.prev_lane = None
                op.sig = True
            elif op.is_dma:
                n = dma_n.get(op.eng, 0)
                dma_n[op.eng] = n + 1
                lane = n % DMA_LANES
                key = (op.eng, lane)
                c = cnt.get(key, 0) + 1
                cnt[key] = c
                ep = (c * 16) // SEM_EPOCH
                kk = (op.eng, lane, ep)
                c2 = cnt.get(kk, 0) + 1
                cnt[kk] = c2
                op.sem = "d_%s_%d_%d" % (op.eng, lane, ep)
                op.val = 16 * c2
                op.prev_lane = lane_prev.get(key)
                lane_prev[key] = op.idx
                op.sig = True
            elif op.sig:
                c = cnt.get(op.eng, 0) + 1
                cnt[op.eng] = c
                ep = c // SEM_EPOCH
                kk = (op.eng, "e", ep)
                c2 = cnt.get(kk, 0) + 1
                cnt[kk] = c2
                op.sem = "c_%s_%d" % (op.eng, ep)
                op.val = c2
        for op in cur:
            if op.sem is not None:
                self._sem(op.sem)
        if barrier:
            self._sem("bar")
            self._sem("bar_sp")
        self.nsig += sum(1 for o in cur if o.sig)
        by_eng = {e: [o for o in cur if o.eng == e] for e in self.ENGS}
        finals = [o for o in cur if o.final]
        prog = self
        sems = self.sems
        nbar = self.nbar
        BSC = self.bar_scratch if barrier else None

        def emit_engine(ename, e):
            known = prog.known[ename]

            def wait(d):
                if known.get(d.sem, 0) >= d.val:
                    return
                known[d.sem] = d.val
                e.wait_ge(sems[d.sem], d.val)
                prog.nwaits += 1

            if nbar > 0:
                e.wait_ge(sems["bar"], 4 * nbar)
                e.wait_ge(sems["bar_sp"], 16 * nbar)
            last_dma = {}
            for op in by_eng[ename]:
                for di in sorted(op.deps):
                    d = ops[di]
                    if prog._needs_wait(op, d):
                        wait(d)
                if op.is_dma and getattr(op, "prev_lane", None) is not None:
                    wait(ops[op.prev_lane])
                ins = op.fn(e)
                if op.sig:
                    if getattr(op, "is_cc", False):
                        ins.then_inc(sems[op.sem])
                    else:
                        ins.then_inc(sems[op.sem], 16 if op.is_dma else 1)
                if op.is_dma and not getattr(op, "defer", False):
                    last_dma[op.sem] = op
            if ename == "sp":
                for o in finals:
                    wait(o)
            if barrier:
                for o in last_dma.values():
                    wait(o)
                if ename == "pe":
                    e.matmul(BSC["ps"][0:1, 0:2], lhsT=BSC["bf"][:, 0:1], rhs=BSC["bf"][:, 0:2], start=True, stop=True).then_inc(sems["bar"], 1)
                elif ename == "act":
                    e.activation(out=BSC["a"][:, 0:1], in_=BSC["a"][:, 1:2], func=AF.Copy).then_inc(sems["bar"], 1)
                elif ename == "dve":
                    e.memset(BSC["d"][:, 0:1], 0.0).then_inc(sems["bar"], 1)
                elif ename == "pool":
                    e.memset(BSC["p"][:, 0:1], 0.0).then_inc(sems["bar"], 1)
                elif ename == "sp":
                    e.dma_start(out=BSC["s"][:, 0:1], in_=BSC["s"][:, 1:2]).then_inc(sems["bar_sp"], 16)

        with nc.Block() as block:
            @block.tensor
            def _(e):
                emit_engine("pe", e)

            @block.scalar
            def _(e):
                emit_engine("act", e)

            @block.vector
            def _(e):
                emit_engine("dve", e)

            @block.gpsimd
            def _(e):
                emit_engine("pool", e)

            @block.sync
            def _(e):
                emit_engine("sp", e)
        if barrier:
            self.nbar += 1

    def setup_barrier(self):
        self.main_stack = None
        st = self.stack
        self.bar_scratch = {
            "bf": st.enter_context(self.nc.sbuf_tensor("bar_bf", [128, 2], BF16)),
            "a": st.enter_context(self.nc.sbuf_tensor("bar_a", [128, 2], F32)),
            "d": st.enter_context(self.nc.sbuf_tensor("bar_d", [128, 2], F32)),
            "p": st.enter_context(self.nc.sbuf_tensor("bar_p", [128, 2], F32)),
            "s": st.enter_context(self.nc.sbuf_tensor("bar_s", [128, 2], F32)),
            "ps": st.enter_context(self.nc.psum_tensor("bar_ps", [128, 512], F32)),
        }
        self.main_stack = st


D = 2048
T = 4096
NB = 2
TOK = 1024
DFF = 5632
ALPHA = (2.0 * 2) ** 0.25
LN_EPS = 1e-5
RMS_EPS = 1e-6
NCORES = 8


def ln_tile(P, y_ap, ykey, g_bc, b_bc, out_ap, okey, st, mv, tagkey, eps=LN_EPS):
    for c in range(4):
        P.add("dve", lambda e, c=c: e.bn_stats(out=st[:, c, :], in_=y_ap[:, c * 512:(c + 1) * 512]),
              reads=[ykey], writes=[tagkey + "st"])
    P.add("dve", lambda e: e.bn_aggr(out=mv[:, 0:2], in_=st[:].rearrange("p a b -> p (a b)")),
          reads=[tagkey + "st"], writes=[tagkey + "mv"])
    P.add("dve", lambda e: e.tensor_scalar(out=mv[:, 2:3], in0=mv[:, 1:2], scalar1=eps, scalar2=None, op0=ALU.add),
          reads=[tagkey + "mv"], writes=[tagkey + "mv"])
    P.add("act", lambda e: e.activation(out=mv[:, 2:3], in_=mv[:, 2:3], func=AF.Sqrt),
          reads=[tagkey + "mv"], writes=[tagkey + "mv"])
    P.add("dve", lambda e: e.reciprocal(out=mv[:, 3:4], in_=mv[:, 2:3]),
          reads=[tagkey + "mv"], writes=[tagkey + "mv"])
    P.add("dve", lambda e: e.tensor_scalar(out=y_ap, in0=y_ap, scalar1=mv[:, 0:1], scalar2=mv[:, 3:4],
                                           op0=ALU.subtract, op1=ALU.mult),
          reads=[ykey, tagkey + "mv"], writes=[ykey])
    P.add("dve", lambda e: e.tensor_tensor(out=y_ap, in0=y_ap, in1=g_bc, op=ALU.mult),
          reads=[ykey, "lnp"], writes=[ykey])
    P.add("dve", lambda e: e.tensor_tensor(out=out_ap, in0=y_ap, in1=b_bc, op=ALU.add),
          reads=[ykey, "lnp"], writes=[okey])


def build_ffn():
    nc = bass.Bass("TRN2", target_bir_lowering=False)
    x1 = nc.dram_tensor("x1", [TOK, D], F32, kind="ExternalInput").ap()
    x1T = nc.dram_tensor("x1T", [D, TOK + 2], F32, kind="ExternalInput").ap()
    w_up = nc.dram_tensor("w_up", [D, 2 * DFF], F32, kind="ExternalInput").ap()
    cwb = nc.dram_tensor("cwb", [128, 88, 4], F32, kind="ExternalInput").ap()
    w_down = nc.dram_tensor("w_down", [DFF, D], F32, kind="ExternalInput").ap()
    lng = nc.dram_tensor("lng", [128, D], F32, kind="ExternalInput").ap()
    lnb = nc.dram_tensor("lnb", [128, D], F32, kind="ExternalInput").ap()
    y = nc.dram_tensor("y", [TOK, D], F32, kind="ExternalOutput").ap()
    P = Prog(nc)
    G = P.sbuf("G", [128, 44, TOK], BF16)
    XY = P.sbuf("XY", [128, 16 * (TOK + 2)], BF16)
    XT = XY[:].rearrange("p (k t) -> p k t", k=16)
    Y = XY[:, 0:16384].bitcast(F32).rearrange("p (t d) -> p t d", t=4)
    WB = [P.sbuf("WB%d" % i, [128, 44 * 256], BF16) for i in range(2)]
    H = [[P.sbuf("H%d_%d" % (gv, s), [128, 514], F32) for s in range(2)] for gv in range(2)]
    TT = [[P.sbuf("T%d_%d" % (gv, s), [128, 512], F32) for s in range(2)] for gv in range(2)]
    CW = P.sbuf("CW", [128, 88, 4], F32)
    LG = P.sbuf("LG", [128, D], F32)
    LB = P.sbuf("LB", [128, D], F32)
    ST = P.sbuf("ST", [128, 4, 6], F32)
    MV = P.sbuf("MV", [128, 4], F32)
    PS = [P.psum("ps%d" % i, [128, 512]) for i in range(8)]

    P.dma("sp", CW[:], cwb, writes=["CW"])
    P.dma("sp", LG[:], lng, writes=["lnp"])
    P.dma("sp", LB[:], lnb, writes=["lnp2"])
    x1T_v = x1T.rearrange("(k p) t -> p k t", p=128)
    for k in range(16):
        P.dma("pool", XT[:, k, :], x1T_v[:, k, :], writes=["XY"])
    w_up_v = w_up.rearrange("(k p) n -> p k n", p=128)
    NG = 22
    ps_i = 0
    for grp in range(NG):
        wb = WB[grp % 2]
        wv = wb[:, 0:2 * 16 * 256].rearrange("p (g k n) -> p g k n", g=2, k=16)
        wkey = "WB%d" % (grp % 2)
        for gv in range(2):
            c0 = gv * DFF + grp * 256
            P.dma("pool", wv[:, gv, :, :], w_up_v[:, :, c0:c0 + 256], writes=[wkey])
        for cc in range(2):
            c = grp * 2 + cc
            for blk in range(2):
                slot = (c * 2 + blk) % 2
                for gv in range(2):
                    pm = PS[ps_i % 3]
                    ph = PS[3 + ps_i % 3]
                    pmk = "ps%d" % (ps_i % 3)
                    phk = "ps%d" % (3 + ps_i % 3)
                    ps_i += 1
                    t0 = 1 + blk * 512
                    for k in range(16):
                        P.add("pe", lambda e, k=k, pm=pm, gv=gv, cc=cc, t0=t0, wv=wv: e.matmul(
                            pm[:], lhsT=wv[:, gv, k, cc * 128:(cc + 1) * 128], rhs=XT[:, k, t0:t0 + 512],
                            start=(k == 0), stop=(k == 15)), reads=[wkey, "XY"], writes=[pmk])
                    for k in range(16):
                        P.add("pe", lambda e, k=k, ph=ph, gv=gv, cc=cc, t0=t0, wv=wv: e.matmul(
                            ph[:, 0:2], lhsT=wv[:, gv, k, cc * 128:(cc + 1) * 128], rhs=XT[:, k, t0 - 1:t0 + 513:513],
                            start=(k == 0), stop=(k == 15)), reads=[wkey, "XY"], writes=[phk])
                    h = H[gv][slot]
                    hk = "H%d_%d" % (gv, slot)
                    P.add("act", lambda e, h=h, pm=pm: e.activation(out=h[:, 1:513], in_=pm[:], func=AF.Copy),
                          reads=[pmk], writes=[hk])
                    P.add("act", lambda e, h=h, ph=ph: e.activation(out=h[:, 0:514:513], in_=ph[:, 0:2], func=AF.Copy),
                          reads=[phk], writes=[hk])
                    tt = TT[gv][slot]
                    tk = "T%d_%d" % (gv, slot)
                    ch = gv * 44 + c
                    P.add("dve", lambda e, tt=tt, h=h, ch=ch: e.tensor_scalar(
                        out=tt[:], in0=h[:, 0:512], scalar1=CW[:, ch, 0:1], scalar2=CW[:, ch, 3:4],
                        op0=ALU.mult, op1=ALU.add), reads=[hk, "CW"], writes=[tk])
                    for j in (1, 2):
                        P.add("dve", lambda e, tt=tt, h=h, ch=ch, j=j: e.scalar_tensor_tensor(
                            out=tt[:], in0=h[:, j:j + 512], scalar=CW[:, ch, j:j + 1], in1=tt[:],
                            op0=ALU.mult, op1=ALU.add), reads=[hk, "CW", tk], writes=[tk])
                tg = TT[0][slot]
                tv = TT[1][slot]
                P.add("act", lambda e, tg=tg: e.activation(out=tg[:], in_=tg[:], func=AF.Silu),
                      reads=["T0_%d" % slot], writes=["T0_%d" % slot])
                P.add("pool", lambda e, tg=tg, tv=tv, c=c, blk=blk: e.tensor_tensor(
                    out=G[:, c, blk * 512:(blk + 1) * 512], in0=tg[:], in1=tv[:], op=ALU.mult),
                    reads=["T0_%d" % slot, "T1_%d" % slot], writes=[("G", c)])
    w_down_v = w_down.rearrange("(k p) n -> p k n", p=128)
    x1_v = x1.rearrange("(t p) d -> p t d", p=128)
    y_v = y.rearrange("(t p) d -> p t d", p=128)
    Gkeys = [("G", c) for c in range(44)]
    li = 0
    for half in range(2):
        P.dma("sp", Y[:], x1_v[:, half * 4:(half + 1) * 4, :], writes=["XY"])
        for cb in range(8):
            wb = WB[li % 2]
            wkey = "WB%d" % (li % 2)
            li += 1
            wv = wb[:].rearrange("p (k n) -> p k n", k=44)
            P.dma("pool", wv[:, 0:22, :], w_down_v[:, 0:22, cb * 256:(cb + 1) * 256], writes=[wkey])
            P.dma("pool", wv[:, 22:44, :], w_down_v[:, 22:44, cb * 256:(cb + 1) * 256], writes=[wkey + "b"])
            for t4 in range(4):
                tt = half * 4 + t4
                pi = 6 + (ps_i % 2)
                ps_i += 1
                pm = PS[pi]
                pmk = "ps%d" % pi
                for k in range(44):
                    P.add("pe", lambda e, k=k, pm=pm, tt=tt, wv=wv: e.matmul(
                        pm[:, 0:256], lhsT=G[:, k, tt * 128:(tt + 1) * 128], rhs=wv[:, k, :],
                        start=(k == 0), stop=(k == 43)), reads=[wkey, wkey + "b", ("G", k)], writes=[pmk])
                ysl = Y[:, t4, cb * 256:(cb + 1) * 256]
                P.add("dve", lambda e, ysl=ysl, pm=pm: e.scalar_tensor_tensor(
                    out=ysl, in0=ysl, scalar=ALPHA, in1=pm[:, 0:256], op0=ALU.mult, op1=ALU.add),
                    reads=[pmk, "XY"], writes=["XY"])
        for t4 in range(4):
            tt = half * 4 + t4
            ln_tile(P, Y[:, t4, :], "XY", LG[:], LB[:], Y[:, t4, :], "XY", ST, MV, "ln")
            P.dma("sp", y_v[:, tt, :], Y[:, t4, :], reads=["XY"], final=True)
    P.emit()
    return nc, P


def _halo_T(xf, c):
    b, j = divmod(c, 4)
    t0 = j * TOK
    out = np.zeros((xf.shape[2], TOK + 2), np.float32)
    lo = max(t0 - 1, 0)
    hi = min(t0 + TOK + 1, T)
    out[:, lo - (t0 - 1):hi - (t0 - 1)] = xf[b, lo:hi, :].T
    return out


def _bc(v):
    return np.ascontiguousarray(np.broadcast_to(np.asarray(v, np.float32)[None, :], (128, v.shape[0])))


_NC_CACHE = {}


def _get(name, builder):
    if name not in _NC_CACHE:
        _NC_CACHE[name] = builder()[0]
    return _NC_CACHE[name]


def run_ffn(xf, w_up, conv_w, conv_b, w_down, g, b):
    nc = _get("ffn", build_ffn)
    cwb = np.concatenate([conv_w, conv_b[None, :]], axis=0)
    cwb = np.ascontiguousarray(cwb.reshape(4, 88, 128).transpose(2, 1, 0))
    w_up = np.ascontiguousarray(w_up)
    w_down = np.ascontiguousarray(w_down)
    lg, lb = _bc(g), _bc(b)
    in_maps = []
    for c in range(NCORES):
        bb, j = divmod(c, 4)
        in_maps.append({"x1": np.ascontiguousarray(xf[bb, j * TOK:(j + 1) * TOK, :]), "x1T": _halo_T(xf, c),
                        "w_up": w_up, "cwb": cwb, "w_down": w_down, "lng": lg, "lnb": lb})
    res = run_bass_kernel_spmd(nc, in_maps, core_ids=list(range(NCORES)))
    out = np.empty_like(xf)
    for c in range(NCORES):
        bb, j = divmod(c, 4)
        out[bb, j * TOK:(j + 1) * TOK, :] = res.results[c]["y"]
    return out


def build_proj_ln(KC):
    nc = bass.Bass("TRN2", target_bir_lowering=False)
    x1 = nc.dram_tensor("x1", [TOK, D], F32, kind="ExternalInput").ap()
    aT = nc.dram_tensor("aT", [KC * 128, TOK], F32, kind="ExternalInput").ap()
    w = nc.dram_tensor("w", [KC * 128, D], F32, kind="ExternalInput").ap()
    lng = nc.dram_tensor("lng", [128, D], F32, kind="ExternalInput").ap()
    lnb = nc.dram_tensor("lnb", [128, D], F32, kind="ExternalInput").ap()
    y = nc.dram_tensor("y", [TOK, D], F32, kind="ExternalOutput").ap()
    P = Prog(nc)
    G = P.sbuf("G", [128, KC, TOK], BF16)
    Y = P.sbuf("Y", [128, 8, D], F32)
    WB = [P.sbuf("WB%d" % i, [128, KC, 512], BF16) for i in range(2)]
    LG = P.sbuf("LG", [128, D], F32)
    LB = P.sbuf("LB", [128, D], F32)
    ST = P.sbuf("ST", [128, 4, 6], F32)
    MV = P.sbuf("MV", [128, 4], F32)
    PS = [P.psum("ps%d" % i, [128, 512]) for i in range(4)]
    P.dma("sp", LG[:], lng, writes=["lnp"])
    P.dma("sp", LB[:], lnb, writes=["lnp2"])
    aT_v = aT.rearrange("(k p) t -> p k t", p=128)
    for k in range(KC):
        P.dma("pool", G[:, k, :], aT_v[:, k, :], writes=[("G", k)])
    w_v = w.rearrange("(k p) n -> p k n", p=128)
    x1_v = x1.rearrange("(t p) d -> p t d", p=128)
    y_v = y.rearrange("(t p) d -> p t d", p=128)
    for tt in range(8):
        P.dma("sp", Y[:, tt, :], x1_v[:, tt, :], writes=[("Y", tt)])
    ps_i = 0
    for cb in range(4):
        wb = WB[cb % 2]
        wkey = "WB%d" % (cb % 2)
        P.dma("pool", wb[:], w_v[:, :, cb * 512:(cb + 1) * 512], writes=[wkey])
        for tt in range(8):
            pi = ps_i % 4
            ps_i += 1
            pm = PS[pi]
            pmk = "ps%d" % pi
            for k in range(KC):
                P.add("pe", lambda e, k=k, pm=pm, tt=tt, wb=wb: e.matmul(
                    pm[:], lhsT=G[:, k, tt * 128:(tt + 1) * 128], rhs=wb[:, k, :],
                    start=(k == 0), stop=(k == KC - 1)), reads=[wkey, ("G", k)], writes=[pmk])
            ysl = Y[:, tt, cb * 512:(cb + 1) * 512]
            P.add("dve", lambda e, ysl=ysl, pm=pm: e.scalar_tensor_tensor(
                out=ysl, in0=ysl, scalar=ALPHA, in1=pm[:], op0=ALU.mult, op1=ALU.add),
                reads=[pmk, ("Y", tt)], writes=[("Y", tt)])
    for tt in range(8):
        ln_tile(P, Y[:, tt, :], ("Y", tt), LG[:], LB[:], Y[:, tt, :], ("Y", tt), ST, MV, "ln")
        P.dma("sp", y_v[:, tt, :], Y[:, tt, :], reads=[("Y", tt)], final=True)
    P.emit()
    return nc, P


def _shard_tok(xf, c):
    b, j = divmod(c, 4)
    return np.ascontiguousarray(xf[b, j * TOK:(j + 1) * TOK, :])


def _gather_tok(res, name, width):
    out = np.empty((NB, T, width), np.float32)
    for c in range(NCORES):
        b, j = divmod(c, 4)
        out[b, j * TOK:(j + 1) * TOK, :] = res.results[c][name]
    return out


def run_proj_ln(xf, af, w, g, b):
    KC = af.shape[2] // 128
    nc = _get("proj_ln%d" % KC, lambda: build_proj_ln(KC))
    w = np.ascontiguousarray(w)
    lg, lb = _bc(g), _bc(b)
    in_maps = []
    for c in range(NCORES):
        in_maps.append({"x1": _shard_tok(xf, c), "aT": np.ascontiguousarray(_shard_tok(af, c).T),
                        "w": w, "lng": lg, "lnb": lb})
    res = run_bass_kernel_spmd(nc, in_maps, core_ids=list(range(NCORES)))
    return _gather_tok(res, "y", D)


def emit_proj_tm(P, XT, xkey, w_v, NCOLS, WB, PS, sink):
    ps_i = 0
    for cb in range(NCOLS // 512):
        wb = WB[cb % 2]
        wkey = "WB%d" % (cb % 2)
        P.dma("pool", wb[:], w_v[:, :, cb * 512:(cb + 1) * 512], writes=[wkey])
        for tt in range(8):
            pi = ps_i % len(PS)
            ps_i += 1
            pm = PS[pi]
            pmk = "ps%d" % pi
            for k in range(16):
                P.add("pe", lambda e, k=k, pm=pm, tt=tt, wb=wb: e.matmul(
                    pm[:], lhsT=XT[:, k, tt * 128:(tt + 1) * 128], rhs=wb[:, k, :],
                    start=(k == 0), stop=(k == 15)), reads=[wkey, xkey], writes=[pmk])
            sink(P, cb, tt, pm, pmk)


def build_proj(NCOLS):
    nc = bass.Bass("TRN2", target_bir_lowering=False)
    xT = nc.dram_tensor("xT", [D, TOK], F32, kind="ExternalInput").ap()
    w = nc.dram_tensor("w", [D, NCOLS], F32, kind="ExternalInput").ap()
    h = nc.dram_tensor("h", [TOK, NCOLS], F32, kind="ExternalOutput").ap()
    P = Prog(nc)
    XT = P.sbuf("XT", [128, 16, TOK], BF16)
    WB = [P.sbuf("WB%d" % i, [128, 16, 512], BF16) for i in range(2)]
    OB = [P.sbuf("OB%d" % i, [128, 512], F32) for i in range(4)]
    PS = [P.psum("ps%d" % i, [128, 512]) for i in range(4)]
    xT_v = xT.rearrange("(k p) t -> p k t", p=128)
    for k in range(16):
        P.dma("pool", XT[:, k, :], xT_v[:, k, :], writes=["XT"])
    w_v = w.rearrange("(k p) n -> p k n", p=128)
    h_v = h.rearrange("(t p) n -> p t n", p=128)
    cnt = [0]

    def sink(P, cb, tt, pm, pmk):
        i = cnt[0] % 4
        cnt[0] += 1
        ob = OB[i]
        P.add("act", lambda e: e.activation(out=ob[:], in_=pm[:], func=AF.Copy), reads=[pmk], writes=["OB%d" % i])
        P.dma("sp", h_v[:, tt, cb * 512:(cb + 1) * 512], ob[:], reads=["OB%d" % i], final=True)

    emit_proj_tm(P, XT, "XT", w_v, NCOLS, WB, PS, sink)
    P.emit()
    return nc, P


def run_proj(xf, w):
    NCOLS = w.shape[1]
    nc = _get("proj%d" % NCOLS, lambda: build_proj(NCOLS))
    w = np.ascontiguousarray(w)
    in_maps = [{"xT": np.ascontiguousarray(_shard_tok(xf, c).T), "w": w} for c in range(NCORES)]
    res = run_bass_kernel_spmd(nc, in_maps, core_ids=list(range(NCORES)))
    return _gather_tok(res, "h", NCOLS)


NQH, NKH, HD = 16, 4, 128
QKV = (NQH + 2 * NKH) * HD
NRM = (NQH + NKH) * HD


def build_qkv():
    nc = bass.Bass("TRN2", target_bir_lowering=False)
    xT = nc.dram_tensor("xT", [D, TOK], F32, kind="ExternalInput").ap()
    w = nc.dram_tensor("w", [D, QKV], F32, kind="ExternalInput").ap()
    cos = nc.dram_tensor("cos", [TOK, HD], F32, kind="ExternalInput").ap()
    sinp = nc.dram_tensor("sinp", [TOK, HD], F32, kind="ExternalInput").ap()
    gains = nc.dram_tensor("gains", [128, 4, HD], F32, kind="ExternalInput").ap()
    h = nc.dram_tensor("h", [TOK, QKV], F32, kind="ExternalOutput").ap()
    P = Prog(nc)
    XT = P.sbuf("XT", [128, 16, TOK], BF16)
    WB = [P.sbuf("WB%d" % i, [128, 16, 512], BF16) for i in range(2)]
    H = P.sbuf("H", [128, 8, QKV], F32)
    TMP = P.sbuf("TMP", [128, NRM], F32)
    TMP2 = P.sbuf("TMP2", [128, NRM], F32)
    CS = P.sbuf("CS", [128, 8, HD], F32)
    SN = P.sbuf("SN", [128, 8, HD], F32)
    GN = P.sbuf("GN", [128, 4, HD], F32)
    TAB = P.sbuf("TAB", [128, 4, 8, HD], F32)
    SS = P.sbuf("SS", [128, 24], F32)
    PS = [P.psum("ps%d" % i, [128, 512]) for i in range(4)]
    xT_v = xT.rearrange("(k p) t -> p k t", p=128)
    for k in range(16):
        P.dma("pool", XT[:, k, :], xT_v[:, k, :], writes=["XT"])
    P.dma("sp", CS[:], cos.rearrange("(t p) f -> p t f", p=128), writes=["CS"])
    P.dma("sp", SN[:], sinp.rearrange("(t p) f -> p t f", p=128), writes=["SN"])
    P.dma("sp", GN[:], gains, writes=["GN"])
    for i in range(4):
        src = CS if i % 2 == 0 else SN
        P.add("dve", lambda e, i=i, src=src: e.tensor_tensor(
            out=TAB[:, i, :, :], in0=src[:], in1=GN[:, i, :].unsqueeze(1).to_broadcast([128, 8, HD]), op=ALU.mult),
            reads=["CS", "SN", "GN"], writes=["TAB"])
    w_v = w.rearrange("(k p) n -> p k n", p=128)
    h_v = h.rearrange("(t p) n -> p t n", p=128)

    def sink(P, cb, tt, pm, pmk):
        P.add("act", lambda e: e.activation(out=H[:, tt, cb * 512:(cb + 1) * 512], in_=pm[:], func=AF.Copy),
              reads=[pmk], writes=[("H", tt)])

    emit_proj_tm(P, XT, "XT", w_v, QKV, WB, PS, sink)
    NH = NQH + NKH
    for tt in range(8):
        X = H[:, tt, 0:NRM]
        hk = ("H", tt)
        P.add("dve", lambda e, X=X: e.tensor_tensor(out=TMP[:], in0=X, in1=X, op=ALU.mult), reads=[hk], writes=["TMP"])
        P.add("dve", lambda e: e.tensor_reduce(out=SS[:, 0:NH], in_=TMP[:].rearrange("p (h f) -> p h f", f=HD),
                                               axis=AX.X, op=ALU.add), reads=["TMP"], writes=["SS"])
        P.add("dve", lambda e: e.tensor_scalar(out=SS[:, 0:NH], in0=SS[:, 0:NH], scalar1=1.0 / HD, scalar2=RMS_EPS,
                                               op0=ALU.mult, op1=ALU.add), reads=["SS"], writes=["SS"])
        P.add("act", lambda e: e.activation(out=SS[:, 0:NH], in_=SS[:, 0:NH], func=AF.Sqrt), reads=["SS"], writes=["SS"])
        P.add("dve", lambda e: e.reciprocal(out=SS[:, 0:NH], in_=SS[:, 0:NH]), reads=["SS"], writes=["SS"])
        P.add("dve", lambda e, X=X: e.tensor_tensor(
            out=TMP[:].rearrange("p (h f) -> p h f", f=HD), in0=X.rearrange("p (h f) -> p h f", f=HD),
            in1=SS[:, 0:NH].unsqueeze(2).to_broadcast([128, NH, HD]), op=ALU.mult), reads=[hk, "SS"], writes=["TMP"])
        for (h0, nh, ti) in ((0, NQH, 0), (NQH, NKH, 2)):
            xn = TMP[:, h0 * HD:(h0 + nh) * HD]
            xo = X[:, h0 * HD:(h0 + nh) * HD]
            t2 = TMP2[:, h0 * HD:(h0 + nh) * HD]
            P.add("dve", lambda e, xn=xn, xo=xo, nh=nh, ti=ti, tt=tt: e.tensor_tensor(
                out=xo.rearrange("p (h f) -> p h f", f=HD), in0=xn.rearrange("p (h f) -> p h f", f=HD),
                in1=TAB[:, ti, tt, :].unsqueeze(1).to_broadcast([128, nh, HD]), op=ALU.mult),
                reads=["TMP", "TAB"], writes=[hk])
            for s in range(2):
                P.add("dve", lambda e, xn=xn, t2=t2, nh=nh, ti=ti, tt=tt, s=s: e.tensor_tensor(
                    out=t2.rearrange("p (h a s f) -> p h a s f", a=2, s=2, f=32)[:, :, :, s, :],
                    in0=xn.rearrange("p (h a s f) -> p h a s f", a=2, s=2, f=32)[:, :, :, 1 - s, :],
                    in1=TAB[:, ti + 1, tt, :].rearrange("p (a s f) -> p a s f", a=2, s=2)[:, :, s, :]
                        .unsqueeze(1).to_broadcast([128, nh, 2, 32]), op=ALU.mult),
                    reads=["TMP", "TAB"], writes=["TMP2"])
            P.add("dve", lambda e, xo=xo, t2=t2: e.tensor_tensor(out=xo, in0=xo, in1=t2, op=ALU.add),
                  reads=[hk, "TMP2"], writes=[hk])
        P.dma("sp", h_v[:, tt, :], H[:, tt, :], reads=[hk], final=True)
    P.emit()
    return nc, P


def _rope_tables():
    rows = T // 64
    r = np.repeat(np.arange(rows, dtype=np.float32), 64)
    cc = np.tile(np.arange(64, dtype=np.float32), rows)
    half = HD // 2
    inv = np.exp(-np.log(np.float32(10000.0)) * np.arange(0, half, 2, dtype=np.float32) / np.float32(half)).astype(np.float32)
    ar = (r[:, None] * inv).astype(np.float32)
    ac = (cc[:, None] * inv).astype(np.float32)
    cos = np.concatenate([np.cos(ar), np.cos(ar), np.cos(ac), np.cos(ac)], axis=1).astype(np.float32)
    sinp = np.concatenate([-np.sin(ar), np.sin(ar), -np.sin(ac), np.sin(ac)], axis=1).astype(np.float32)
    return cos, sinp


def _swap32(g):
    return np.concatenate([g[32:64], g[0:32], g[96:128], g[64:96]])


def run_qkv(xf, w, gq, gk):
    nc = _get("qkv", build_qkv)
    w = np.ascontiguousarray(w)
    cos, sinp = _rope_tables()
    gains = np.stack([gq, _swap32(gq), gk, _swap32(gk)], axis=0).astype(np.float32)
    gains = np.ascontiguousarray(np.broadcast_to(gains[None], (128, 4, HD)))
    in_maps = []
    for c in range(NCORES):
        b, j = divmod(c, 4)
        in_maps.append({"xT": np.ascontiguousarray(_shard_tok(xf, c).T), "w": w,
                        "cos": np.ascontiguousarray(cos[j * TOK:(j + 1) * TOK]),
                        "sinp": np.ascontiguousarray(sinp[j * TOK:(j + 1) * TOK]), "gains": gains})
    res = run_bass_kernel_spmd(nc, in_maps, core_ids=list(range(NCORES)))
    return _gather_tok(res, "h", QKV)


VW = 130


def build_attn():
    nc = bass.Bass("TRN2", target_bir_lowering=False)
    qT = nc.dram_tensor("qT", [NQH * HD, TOK], F32, kind="ExternalInput").ap()
    kT = nc.dram_tensor("kT", [NKH * HD, T], F32, kind="ExternalInput").ap()
    v = nc.dram_tensor("v", [T, NKH * HD], F32, kind="ExternalInput").ap()
    o = nc.dram_tensor("o", [TOK, NQH * HD], F32, kind="ExternalOutput").ap()
    P = Prog(nc)
    QT = P.sbuf("QT", [128, NQH, TOK], BF16)
    KT = P.sbuf("KT", [128, NKH, T], BF16)
    VE = P.sbuf("VE", [128, 32, NKH, VW], BF16)
    PT = [P.sbuf("PT%d" % i, [128, 512], BF16) for i in range(3)]
    OB = [P.sbuf("OB%d" % i, [128, NQH * HD], F32) for i in range(2)]
    RC = P.sbuf("RC", [128, 8], F32)
    PS = [P.psum("ps%d" % i, [128, 512]) for i in range(8)]
    qT_v = qT.rearrange("(h p) t -> p h t", p=128)
    for h in range(NQH):
        P.dma("pool", QT[:, h, :], qT_v[:, h, :], writes=["QT"])
    kT_v = kT.rearrange("(h p) t -> p h t", p=128)
    for h in range(NKH):
        for half in range(2):
            P.dma("pool", KT[:, h, half * 2048:(half + 1) * 2048], kT_v[:, h, half * 2048:(half + 1) * 2048], writes=["KT"])
    P.add("dve", lambda e: e.memset(VE[:].rearrange("p c k w -> p (c k w)"), 1.0), writes=["VE"])
    v_v = v.rearrange("(c p) (k f) -> p c k f", p=128, k=NKH)
    for kv in range(NKH):
        P.dma("pool", VE[:, :, kv, 0:HD], v_v[:, :, kv, :], writes=["VE"])
    o_v = o.rearrange("(t p) n -> p t n", p=128)
    scale = float(HD) ** -0.5
    si = 0
    for qt in range(8):
        ob = OB[qt % 2]
        obk = "OB%d" % (qt % 2)
        for kv in range(NKH):
            for sc in range(32):
                sb = 4 + si % 3
                st = PS[sb]
                stk = "ps%d" % sb
                pt = PT[si % 3]
                ptk = "PT%d" % (si % 3)
                si += 1
                P.add("pe", lambda e, st=st, kv=kv, sc=sc, qt=qt: e.matmul(
                    st[:].rearrange("p (h q) -> p h q", h=4), lhsT=KT[:, kv, sc * 128:(sc + 1) * 128],
                    rhs=QT[:, 4 * kv:4 * kv + 4, qt * 128:(qt + 1) * 128], start=True, stop=True),
                    reads=["QT", "KT"], writes=[stk])
                P.add("act", lambda e, st=st, pt=pt: e.activation(out=pt[:], in_=st[:], func=AF.Exp, scale=scale),
                      reads=[stk], writes=[ptk])
                for hh in range(4):
                    P.add("pe", lambda e, hh=hh, pt=pt, sc=sc, kv=kv: e.matmul(
                        PS[hh][:, 0:VW], lhsT=pt[:, hh * 128:(hh + 1) * 128], rhs=VE[:, sc, kv, :],
                        start=(sc == 0), stop=(sc == 31)), reads=[ptk, "VE"], writes=["ps%d" % hh])
            for hh in range(4):
                hq = 4 * kv + hh
                P.add("dve", lambda e, hh=hh: e.reciprocal(out=RC[:, hh:hh + 1], in_=PS[hh][:, HD:HD + 1]),
                      reads=["ps%d" % hh], writes=[("RC", hh)])
                P.add("dve", lambda e, hh=hh, hq=hq, ob=ob: e.tensor_scalar(
                    out=ob[:, hq * HD:(hq + 1) * HD], in0=PS[hh][:, 0:HD], scalar1=RC[:, hh:hh + 1], scalar2=None,
                    op0=ALU.mult), reads=["ps%d" % hh, ("RC", hh)], writes=[obk])
        P.dma("sp", o_v[:, qt, :], ob[:], reads=[obk], final=True)
    P.emit()
    return nc, P


def run_attn(qkv):
    nc = _get("attn", build_attn)
    in_maps = []
    for c in range(NCORES):
        b, j = divmod(c, 4)
        in_maps.append({"qT": np.ascontiguousarray(qkv[b, j * TOK:(j + 1) * TOK, 0:2048].T),
                        "kT": np.ascontiguousarray(qkv[b, :, 2048:2560].T),
                        "v": np.ascontiguousarray(qkv[b, :, 2560:3072])})
    res = run_bass_kernel_spmd(nc, in_maps, core_ids=list(range(NCORES)))
    return _gather_tok(res, "o", NQH * HD)


GW = 1024


def build_gmlp():
    nc = bass.Bass("TRN2", target_bir_lowering=False)
    u = nc.dram_tensor("u", [TOK, GW], F32, kind="ExternalInput").ap()
    v = nc.dram_tensor("v", [TOK, GW], F32, kind="ExternalInput").ap()
    wsT = nc.dram_tensor("wsT", [8, 128, 128], F32, kind="ExternalInput").ap()
    biasT = nc.dram_tensor("biasT", [128, 8], F32, kind="ExternalInput").ap()
    lng = nc.dram_tensor("lng", [128, GW], F32, kind="ExternalInput").ap()
    lnb = nc.dram_tensor("lnb", [128, GW], F32, kind="ExternalInput").ap()
    ob = nc.dram_tensor("ob", [TOK, GW], F32, kind="ExternalOutput").ap()
    P = Prog(nc)
    U = P.sbuf("U", [128, 8, GW], F32)
    V = P.sbuf("V", [128, 8, GW], F32)
    VN = P.sbuf("VN", [128, 8, GW], BF16)
    WS = P.sbuf("WS", [128, 8, 128], BF16)
    BI = P.sbuf("BI", [128, 8], F32)
    LG = P.sbuf("LG", [128, GW], F32)
    LB = P.sbuf("LB", [128, GW], F32)
    ST = P.sbuf("ST", [128, 2, 6], F32)
    MV = P.sbuf("MV", [128, 4], F32)
    PS = [P.psum("ps%d" % i, [128, 512]) for i in range(4)]
    P.dma("sp", U[:], u.rearrange("(t p) n -> p t n", p=128), writes=["U"])
    P.dma("sp", V[:], v.rearrange("(t p) n -> p t n", p=128), writes=["V"])
    P.dma("pool", WS[:], wsT.rearrange("g s t -> s g t"), writes=["WS"])
    P.dma("sp", BI[:], biasT, writes=["BI"])
    P.dma("sp", LG[:], lng, writes=["lnp"])
    P.dma("sp", LB[:], lnb, writes=["lnp2"])
    ob_v = ob.rearrange("(t p) n -> p t n", p=128)
    ps_i = 0
    for n in range(8):
        y = V[:, n, :]
        yk = ("V", n)
        for c in range(2):
            P.add("dve", lambda e, c=c, y=y: e.bn_stats(out=ST[:, c, :], in_=y[:, c * 512:(c + 1) * 512]),
                  reads=["V"], writes=["st"])
        P.add("dve", lambda e: e.bn_aggr(out=MV[:, 0:2], in_=ST[:].rearrange("p a b -> p (a b)")), reads=["st"], writes=["mv"])
        P.add("dve", lambda e: e.tensor_scalar(out=MV[:, 2:3], in0=MV[:, 1:2], scalar1=LN_EPS, scalar2=None, op0=ALU.add),
              reads=["mv"], writes=["mv"])
        P.add("act", lambda e: e.activation(out=MV[:, 2:3], in_=MV[:, 2:3], func=AF.Sqrt), reads=["mv"], writes=["mv"])
        P.add("dve", lambda e: e.reciprocal(out=MV[:, 3:4], in_=MV[:, 2:3]), reads=["mv"], writes=["mv"])
        P.add("dve", lambda e, y=y: e.tensor_scalar(out=y, in0=y, scalar1=MV[:, 0:1], scalar2=MV[:, 3:4],
                                                    op0=ALU.subtract, op1=ALU.mult), reads=["V", "mv"], writes=["V"])
        P.add("dve", lambda e, y=y: e.tensor_tensor(out=y, in0=y, in1=LG[:], op=ALU.mult), reads=["V", "lnp"], writes=["V"])
        P.add("dve", lambda e, y=y, n=n: e.tensor_tensor(out=VN[:, n, :], in0=y, in1=LB[:], op=ALU.add),
              reads=["V", "lnp2"], writes=[("VN", n)])
        for g4 in range(2):
            pi = ps_i % 4
            ps_i += 1
            pm = PS[pi]
            pmk = "ps%d" % pi
            for gg in range(4):
                g = g4 * 4 + gg
                P.add("pe", lambda e, pm=pm, g=g, gg=gg, n=n: e.matmul(
                    pm[:, gg * 128:(gg + 1) * 128], lhsT=WS[:, g, :], rhs=VN[:, n, g * 128:(g + 1) * 128],
                    start=True, stop=True), reads=["WS", ("VN", n)], writes=[pmk])
            for gg in range(4):
                g = g4 * 4 + gg
                usl = U[:, n, g * 128:(g + 1) * 128]
                P.add("dve", lambda e, pm=pm, g=g, gg=gg, usl=usl: e.scalar_tensor_tensor(
                    out=usl, in0=pm[:, gg * 128:(gg + 1) * 128], scalar=BI[:, g:g + 1], in1=usl,
                    op0=ALU.add, op1=ALU.mult), reads=[pmk, "BI", ("U", n)], writes=[("U", n)])
        P.dma("sp", ob_v[:, n, :], U[:, n, :], reads=[("U", n), "U"], final=True)
    P.emit()
    return nc, P


def run_gmlp(h0, ws, bias, g, b):
    nc = _get("gmlp", build_gmlp)
    wsT = np.ascontiguousarray(ws.transpose(0, 2, 1))
    biasT = np.ascontiguousarray(bias.T)
    lg, lb = _bc(g), _bc(b)
    in_maps = []
    for c in range(NCORES):
        hc = _shard_tok(h0, c)
        in_maps.append({"u": np.ascontiguousarray(hc[:, 5120:6144]), "v": np.ascontiguousarray(hc[:, 6144:7168]),
                        "wsT": wsT, "biasT": biasT, "lng": lg, "lnb": lb})
    res = run_bass_kernel_spmd(nc, in_maps, core_ids=list(range(NCORES)))
    return _gather_tok(res, "ob", GW)


HD_ORDER = (0, 1)


def build_hgrn():
    nc = bass.Bass("TRN2", target_bir_lowering=False)
    qT = nc.dram_tensor("qT", [2, 128, T], F32, kind="ExternalInput").ap()
    ffT = nc.dram_tensor("ffT", [2, 128, T], F32, kind="ExternalInput").ap()
    fbT = nc.dram_tensor("fbT", [2, 128, T], F32, kind="ExternalInput").ap()
    iv = nc.dram_tensor("iv", [2, T, 128], F32, kind="ExternalInput").ap()
    gg = nc.dram_tensor("gg", [2, T, 128], F32, kind="ExternalInput").ap()
    lbt = nc.dram_tensor("lbt", [2, 128, 3], F32, kind="ExternalInput").ap()
    gn = nc.dram_tensor("gn", [2, 64, 128], F32, kind="ExternalInput").ap()
    masks = nc.dram_tensor("masks", [2, 128, 128], F32, kind="ExternalInput").ap()
    rmask = nc.dram_tensor("rmask", [128, 512], F32, kind="ExternalInput").ap()
    ident = nc.dram_tensor("ident", [128, 128], F32, kind="ExternalInput").ap()
    oa = nc.dram_tensor("oa", [2, T, 128], F32, kind="ExternalOutput").ap()
    P = Prog(nc)
    SEG = 512
    MK = P.sbuf("MK", [128, 2, 128], F32)
    RM = P.sbuf("RM", [128, SEG], F32)
    ID = P.sbuf("ID", [128, 128], BF16)
    GN = P.sbuf("GN", [64, 2, 128], F32)
    LBT = P.sbuf("LBT", [128, 2, 3], F32)
    LBV = P.sbuf("LBV", [128, 2, 4], F32)
    Qs = [P.sbuf("Qs%d" % i, [128, SEG], F32) for i in range(2)]
    Fs = [P.sbuf("Fs%d" % i, [128, SEG], F32) for i in range(2)]
    Vs = [P.sbuf("Vs%d" % i, [128, 4, 128], BF16) for i in range(2)]
    Gs = [P.sbuf("Gs%d" % i, [64, 8, 128], F32) for i in range(2)]
    Fg = P.sbuf("Fg", [128, SEG], F32)
    Kt = P.sbuf("Kt", [128, SEG], F32)
    Bt = P.sbuf("Bt", [128, SEG], F32)
    Tm = P.sbuf("Tm", [128, SEG], F32)
    Et = P.sbuf("Et", [128, SEG], F32)
    QS = [P.sbuf("QS%d" % i, [128, SEG], BF16) for i in range(2)]
    KS = [P.sbuf("KS%d" % i, [128, SEG], BF16) for i in range(2)]
    KP = [P.sbuf("KP%d" % i, [128, SEG], BF16) for i in range(2)]
    QB = [P.sbuf("QB%d" % i, [128, SEG], BF16) for i in range(2)]
    DEC = [P.sbuf("DEC%d" % i, [128, 8], F32) for i in range(2)]
    AT = [P.sbuf("AT%d" % i, [128, 128], BF16) for i in range(2)]
    KPT = [P.sbuf("KPT%d" % i, [128, 128], BF16) for i in range(2)]
    S32 = P.sbuf("S32", [128, 128], F32)
    SBF = P.sbuf("SBF", [128, 128], BF16)
    OF = P.sbuf("OF", [64, 64, 128], F32)
    OS = [P.sbuf("OS%d" % i, [64, 8, 128], F32) for i in range(2)]
    SQ = P.sbuf("SQ", [64, 8, 128], F32)
    SSQ = P.sbuf("SSQ", [64, 8], F32)
    PSC = [P.psum("psc%d" % i, [128, 512]) for i in range(2)]
    PTP = [P.psum("ptp%d" % i, [128, 1024], BF16) for i in range(2)]
    PO = [P.psum("po%d" % i, [128, 512]) for i in range(2)]
    PKV = [P.psum("pkv%d" % i, [128, 512]) for i in range(2)]

    P.dma("sp", MK[:], masks.rearrange("m s t -> s m t"), writes=["MK"])
    P.dma("sp", RM[:], rmask, writes=["RM"])
    P.dma("pool", ID[:], ident, writes=["ID"])
    P.dma("sp", GN[:], gn.rearrange("h p f -> p h f"), writes=["GN"])
    P.dma("sp", LBT[:], lbt.rearrange("h p f -> p h f"), writes=["LBT"])
    P.add("act", lambda e: e.activation(out=LBT[:], in_=LBT[:], func=AF.Exp), reads=["LBT"], writes=["LBT"])
    P.add("dve", lambda e: e.tensor_reduce(out=LBV[:, :, 0], in_=LBT[:], axis=AX.X, op=ALU.add), reads=["LBT"], writes=["LBV"])
    P.add("dve", lambda e: e.reciprocal(out=LBV[:, :, 1], in_=LBV[:, :, 0]), reads=["LBV"], writes=["LBV"])
    P.add("dve", lambda e: e.tensor_tensor(out=LBV[:, :, 2], in0=LBT[:, :, 0], in1=LBV[:, :, 1], op=ALU.mult),
          reads=["LBV", "LBT"], writes=["LBV"])
    P.add("dve", lambda e: e.tensor_scalar(out=LBV[:, :, 3], in0=LBV[:, :, 2], scalar1=-1.0, scalar2=1.0,
                                           op0=ALU.mult, op1=ALU.add), reads=["LBV"], writes=["LBV"])
    si = 0
    bi = 0
    ci = 0
    for hd in HD_ORDER:
        lb = LBV[:, hd, 2:3]
        oml = LBV[:, hd, 3:4]
        for dr in range(2):
            P.add("dve", lambda e: e.memset(S32[:], 0.0), writes=["S32"])
            P.add("dve", lambda e: e.memset(SBF[:], 0.0), writes=["SBF"])
            fsrc = ffT if dr == 0 else fbT
            mid = 31 if dr == 0 else 32
            last = 63 if dr == 0 else 0
            for seg in (range(8) if dr == 0 else range(7, -1, -1)):
                p = si % 2
                si += 1
                t0 = seg * SEG
                q_s, f_s, v_s, g_s = Qs[p], Fs[p], Vs[p], Gs[p]
                qk, fk, vk, gk = "Qs%d" % p, "Fs%d" % p, "Vs%d" % p, "Gs%d" % p
                P.dma("sp", q_s[:], qT[hd, :, t0:t0 + SEG], writes=[qk])
                P.dma("sp", f_s[:], fsrc[hd, :, t0:t0 + SEG], writes=[fk])
                P.dma("pool", v_s[:], iv[hd, t0:t0 + SEG, :].rearrange("(b p) v -> p b v", p=128), writes=[vk])
                if dr == 1:
                    P.dma("sp", g_s[:], gg[hd, t0:t0 + SEG, :].rearrange("(c p) v -> p c v", p=64), writes=[gk])
                qs, ks, kp, qb, dec = QS[p], KS[p], KP[p], QB[p], DEC[p]
                qsk, ksk, kpk, qbk, deck = "QS%d" % p, "KS%d" % p, "KP%d" % p, "QB%d" % p, "DEC%d" % p
                P.add("act", lambda e, f_s=f_s: e.activation(out=Fg[:], in_=f_s[:], func=AF.Sigmoid), reads=[fk], writes=["Fg"])
                P.add("dve", lambda e, oml=oml, lb=lb: e.tensor_scalar(out=Fg[:], in0=Fg[:], scalar1=oml, scalar2=lb, op0=ALU.mult, op1=ALU.add),
                      reads=["Fg", "LBV"], writes=["Fg"])
                P.add("dve", lambda e: e.tensor_scalar(out=Kt[:], in0=Fg[:], scalar1=-1.0, scalar2=1.0, op0=ALU.mult, op1=ALU.add),
                      reads=["Fg"], writes=["Kt"])
                P.add("act", lambda e: e.activation(out=Fg[:], in_=Fg[:], func=AF.Ln), reads=["Fg"], writes=["Fg"])
                P.add("dve", lambda e: e.tensor_tensor_scan(out=Bt[:], data0=RM[:], data1=Fg[:], initial=0.0,
                                                            op0=ALU.mult, op1=ALU.add), reads=["RM", "Fg"], writes=["Bt"])
                B3 = Bt[:].rearrange("p (c t) -> p c t", t=64)
                T3 = Tm[:].rearrange("p (c t) -> p c t", t=64)
                F3 = Fg[:].rearrange("p (c t) -> p c t", t=64)
                if dr == 1:
                    P.add("dve", lambda e, B3=B3, T3=T3: e.tensor_tensor(
                        out=T3, in0=B3[:, :, 63:64].to_broadcast([128, 8, 64]), in1=B3, op=ALU.subtract),
                        reads=["Bt"], writes=["Tm"])
                    P.add("dve", lambda e: e.tensor_tensor(out=Bt[:], in0=Tm[:], in1=Fg[:], op=ALU.add),
                          reads=["Tm", "Fg"], writes=["Bt"])
                P.add("dve", lambda e, B3=B3, T3=T3, mid=mid: e.tensor_tensor(
                    out=T3, in0=B3, in1=B3[:, :, mid:mid + 1].to_broadcast([128, 8, 64]), op=ALU.subtract),
                    reads=["Bt"], writes=["Tm"])
                P.add("act", lambda e: e.activation(out=Et[:], in_=Tm[:], func=AF.Exp), reads=["Tm"], writes=["Et"])
                P.add("dve", lambda e, qs=qs, q_s=q_s: e.tensor_tensor(out=qs[:], in0=q_s[:], in1=Et[:], op=ALU.mult),
                      reads=[qk, "Et"], writes=[qsk])
                P.add("act", lambda e: e.activation(out=Et[:], in_=Tm[:], func=AF.Exp, scale=-1.0), reads=["Tm"], writes=["Et"])
                P.add("dve", lambda e, ks=ks: e.tensor_tensor(out=ks[:], in0=Kt[:], in1=Et[:], op=ALU.mult),
                      reads=["Kt", "Et"], writes=[ksk])
                P.add("dve", lambda e, B3=B3, T3=T3, last=last: e.tensor_tensor(
                    out=T3, in0=B3[:, :, last:last + 1].to_broadcast([128, 8, 64]), in1=B3, op=ALU.subtract),
                    reads=["Bt"], writes=["Tm"])
                P.add("act", lambda e: e.activation(out=Et[:], in_=Tm[:], func=AF.Exp), reads=["Tm"], writes=["Et"])
                P.add("dve", lambda e, kp=kp: e.tensor_tensor(out=kp[:], in0=Kt[:], in1=Et[:], op=ALU.mult),
                      reads=["Kt", "Et"], writes=[kpk])
                P.add("act", lambda e: e.activation(out=Et[:], in_=Bt[:], func=AF.Exp), reads=["Bt"], writes=["Et"])
                P.add("dve", lambda e, qb=qb, q_s=q_s: e.tensor_tensor(out=qb[:], in0=q_s[:], in1=Et[:], op=ALU.mult),
                      reads=[qk, "Et"], writes=[qbk])
                E3v = Et[:].rearrange("p (c t) -> p c t", t=64)
                P.add("dve", lambda e, dec=dec, E3v=E3v, last=last: e.tensor_copy(out=dec[:], in_=E3v[:, :, last]),
                      reads=["Et"], writes=[deck])
                os_ = OS[p]
                osk = "OS%d" % p
                for blk in (range(4) if dr == 0 else range(3, -1, -1)):
                    b2 = bi % 2
                    bi += 1
                    psc, ptp = PSC[b2], PTP[b2]
                    at, kpt = AT[b2], KPT[b2]
                    atk, kptk = "AT%d" % b2, "KPT%d" % b2
                    cs = slice(blk * 128, (blk + 1) * 128)
                    P.add("pe", lambda e, psc=psc, ks=ks, qs=qs, cs=cs: e.matmul(
                        psc[:, 0:128], lhsT=ks[:, cs], rhs=qs[:, cs], start=True, stop=True),
                        reads=[ksk, qsk], writes=["psc%d" % b2])
                    P.add("dve", lambda e, at=at, psc=psc, dr=dr: e.tensor_tensor(
                        out=at[:], in0=psc[:, 0:128], in1=MK[:, dr, :], op=ALU.mult),
                        reads=["psc%d" % b2, "MK"], writes=[atk])
                    P.add("pe", lambda e, ptp=ptp, kp=kp, cs=cs: e.transpose(ptp[:, 0:128], kp[:, cs], ID[:]),
                          reads=[kpk, "ID"], writes=["ptp%d" % b2])
                    P.add("act", lambda e, kpt=kpt, ptp=ptp: e.activation(out=kpt[:], in_=ptp[:, 0:128], func=AF.Copy),
                          reads=["ptp%d" % b2], writes=[kptk])
                    for c in ((0, 1) if dr == 0 else (1, 0)):
                        c2 = ci % 2
                        ci += 1
                        po, pkv = PO[c2], PKV[c2]
                        cl = blk * 2 + c
                        n = seg * 8 + cl
                        rs = slice(64 * c, 64 * c + 64)
                        P.add("pe", lambda e, po=po, at=at, v_s=v_s, rs=rs, blk=blk: e.matmul(
                            po[0:64, 0:128], lhsT=at[rs, rs], rhs=v_s[rs, blk, :], start=True, stop=False),
                            reads=[atk, vk], writes=["po%d" % c2])
                        P.add("pe", lambda e, po=po, qb=qb, cl=cl: e.matmul(
                            po[0:64, 0:128], lhsT=qb[:, cl * 64:(cl + 1) * 64], rhs=SBF[:], start=False, stop=True),
                            reads=[qbk, "SBF"], writes=["po%d" % c2])
                        P.add("pe", lambda e, pkv=pkv, kpt=kpt, v_s=v_s, rs=rs, blk=blk: e.matmul(
                            pkv[:, 0:128], lhsT=kpt[rs, :], rhs=v_s[rs, blk, :], start=True, stop=True),
                            reads=[kptk, vk], writes=["pkv%d" % c2])
                        P.add("dve", lambda e, pkv=pkv, dec=dec, cl=cl: e.scalar_tensor_tensor(
                            out=S32[:], in0=S32[:], scalar=dec[:, cl:cl + 1], in1=pkv[:, 0:128], op0=ALU.mult, op1=ALU.add),
                            reads=["S32", deck, "pkv%d" % c2], writes=["S32"])
                        P.add("act", lambda e: e.activation(out=SBF[:], in_=S32[:], func=AF.Copy), reads=["S32"], writes=["SBF"])
                        if dr == 0:
                            P.add("act", lambda e, po=po, n=n: e.activation(out=OF[:, n, :], in_=po[0:64, 0:128], func=AF.Copy),
                                  reads=["po%d" % c2], writes=[("OF", n)])
                        else:
                            P.add("dve", lambda e, po=po, n=n, cl=cl, os_=os_: e.tensor_tensor(
                                out=os_[:, cl, :], in0=po[0:64, 0:128], in1=OF[:, n, :], op=ALU.add),
                                reads=["po%d" % c2, ("OF", n)], writes=[osk])
                if dr == 1:
                    P.add("dve", lambda e, os_=os_: e.tensor_tensor(out=SQ[:], in0=os_[:], in1=os_[:], op=ALU.mult),
                          reads=[osk], writes=["SQ"])
                    P.add("dve", lambda e: e.tensor_reduce(out=SSQ[:], in_=SQ[:], axis=AX.X, op=ALU.add), reads=["SQ"], writes=["SSQ"])
                    P.add("dve", lambda e: e.tensor_scalar(out=SSQ[:], in0=SSQ[:], scalar1=1.0 / 128, scalar2=RMS_EPS,
                                                           op0=ALU.mult, op1=ALU.add), reads=["SSQ"], writes=["SSQ"])
                    P.add("act", lambda e: e.activation(out=SSQ[:], in_=SSQ[:], func=AF.Sqrt), reads=["SSQ"], writes=["SSQ"])
                    P.add("dve", lambda e: e.reciprocal(out=SSQ[:], in_=SSQ[:]), reads=["SSQ"], writes=["SSQ"])
                    P.add("dve", lambda e, os_=os_: e.tensor_tensor(
                        out=os_[:], in0=os_[:], in1=SSQ[:].unsqueeze(2).to_broadcast([64, 8, 128]), op=ALU.mult),
                        reads=[osk, "SSQ"], writes=[osk])
                    P.add("dve", lambda e, os_=os_, hd=hd: e.tensor_tensor(
                        out=os_[:], in0=os_[:], in1=GN[:, hd, :].unsqueeze(1).to_broadcast([64, 8, 128]), op=ALU.mult),
                        reads=[osk, "GN"], writes=[osk])
                    P.add("act", lambda e, g_s=g_s: e.activation(out=g_s[:], in_=g_s[:], func=AF.Silu), reads=[gk], writes=[gk])
                    P.add("dve", lambda e, os_=os_, g_s=g_s: e.tensor_tensor(out=os_[:], in0=os_[:], in1=g_s[:], op=ALU.mult),
                          reads=[osk, gk], writes=[osk])
                    P.dma("sp", oa[hd, t0:t0 + SEG, :].rearrange("(c p) v -> p c v", p=64), os_[:], reads=[osk], final=True)
    P.emit()
    return nc, P


def run_hgrn(h0, lb_table, norm_g):
    nc = _get("hgrn", build_hgrn)
    idx = np.arange(128)
    same = (idx[:, None] // 64) == (idx[None, :] // 64)
    mfw = (same & (idx[:, None] <= idx[None, :])).astype(np.float32)
    mbw = (same & (idx[:, None] >= idx[None, :])).astype(np.float32)
    masks = np.stack([mfw, mbw], axis=0)
    rmask = np.ones((128, 512), np.float32)
    rmask[:, ::64] = 0.0
    ident = np.eye(128, dtype=np.float32)
    in_maps = []
    for c in range(NCORES):
        b, j = divmod(c, 4)
        hs = [2 * j, 2 * j + 1]

        def colsT(base):
            return np.ascontiguousarray(np.stack([h0[b, :, base + H * 128: base + (H + 1) * 128].T for H in hs], axis=0))

        def cols(base):
            return np.ascontiguousarray(np.stack([h0[b, :, base + H * 128: base + (H + 1) * 128] for H in hs], axis=0))

        in_maps.append({
            "qT": colsT(0), "ffT": colsT(1024), "fbT": colsT(2048), "iv": cols(3072), "gg": cols(4096),
            "lbt": np.ascontiguousarray(np.stack([lb_table[:, H * 128:(H + 1) * 128].T for H in hs], axis=0)),
            "gn": np.ascontiguousarray(np.stack([np.broadcast_to(norm_g[H * 128:(H + 1) * 128][None, :], (64, 128)) for H in hs], axis=0)),
            "masks": masks, "rmask": rmask, "ident": ident})
    res = run_bass_kernel_spmd(nc, in_maps, core_ids=list(range(NCORES)))
    out = np.empty((NB, T, GW), np.float32)
    for c in range(NCORES):
        b, j = divmod(c, 4)
        for hd in range(2):
            H = 2 * j + hd
            out[b, :, H * 128:(H + 1) * 128] = res.results[c]["oa"][hd]
    return out


def emit_hgrn(P, qT, ffT, fbT, iv, gg, lbt, gn, masks, rmask, IDB, OH, IB1):
    SEG = 512
    IB1v = IB1.rearrange("(s h t) v -> s h t v", s=8, h=2)
    MK = P.sbuf("MK", [128, 2, 128], F32)
    RM = P.sbuf("RM", [128, SEG], F32)
    GN = P.sbuf("GN", [64, 2, 128], F32)
    LBT = P.sbuf("LBT", [128, 2, 3], F32)
    LBV = P.sbuf("LBV", [128, 2, 4], F32)
    Qs = [P.sbuf("Qs%d" % i, [128, SEG], F32) for i in range(2)]
    Fs = [P.sbuf("Fs%d" % i, [128, SEG], F32) for i in range(2)]
    Vf = [P.sbuf("Vf%d" % i, [128, 4, 128], F32) for i in range(2)]
    Vs = [P.sbuf("Vs%d" % i, [128, 4, 128], BF16) for i in range(2)]
    Gs = [P.sbuf("Gs%d" % i, [64, 8, 128], F32) for i in range(2)]
    Fg = P.sbuf("Fg", [128, SEG], F32)
    Kt = P.sbuf("Kt", [128, SEG], F32)
    Bt = P.sbuf("Bt", [128, SEG], F32)
    Tm = P.sbuf("Tm", [128, SEG], F32)
    Et = P.sbuf("Et", [128, SEG], F32)
    QS = [P.sbuf("QS%d" % i, [128, SEG], BF16) for i in range(2)]
    KS = [P.sbuf("KS%d" % i, [128, SEG], BF16) for i in range(2)]
    KP = [P.sbuf("KP%d" % i, [128, SEG], BF16) for i in range(2)]
    QB = [P.sbuf("QB%d" % i, [128, SEG], BF16) for i in range(2)]
    DEC = [P.sbuf("DEC%d" % i, [128, 8], F32) for i in range(2)]
    AT = [P.sbuf("AT%d" % i, [128, 128], BF16) for i in range(2)]
    KPT = [P.sbuf("KPT%d" % i, [128, 128], BF16) for i in range(2)]
    S32 = P.sbuf("S32", [128, 128], F32)
    SBF = P.sbuf("SBF", [128, 128], BF16)
    OF = P.sbuf("OF", [64, 64, 128], F32)
    OS = [P.sbuf("OS%d" % i, [64, 8, 128], F32) for i in range(2)]
    OM = [P.sbuf("OM%d" % i, [64, 8, 128], BF16) for i in range(4)]
    SQ = P.sbuf("SQ", [64, 8, 128], F32)
    SSQ = P.sbuf("SSQ", [64, 8], F32)
    PSC = [P.psum("psc%d" % i, [128, 512]) for i in range(2)]
    PTP = [P.psum("ptp%d" % i, [128, 1024], BF16) for i in range(1)]
    PO = [P.psum("po%d" % i, [128, 512]) for i in range(2)]
    PKV = [P.psum("pkv%d" % i, [128, 512]) for i in range(2)]

    P.dma("sp", MK[:], masks.rearrange("m s t -> s m t"), writes=["MK"])
    P.dma("sp", RM[:], rmask, writes=["RM"])
    P.dma("sp", GN[:], gn.rearrange("h p f -> p h f"), writes=["GN"])
    P.dma("sp", LBT[:], lbt.rearrange("h p f -> p h f"), writes=["LBT"])
    P.add("act", lambda e: e.activation(out=LBT[:], in_=LBT[:], func=AF.Exp), reads=["LBT"], writes=["LBT"])
    P.add("dve", lambda e: e.tensor_reduce(out=LBV[:, :, 0], in_=LBT[:], axis=AX.X, op=ALU.add), reads=["LBT"], writes=["LBV"])
    P.add("dve", lambda e: e.reciprocal(out=LBV[:, :, 1], in_=LBV[:, :, 0]), reads=["LBV"], writes=["LBV"])
    P.add("dve", lambda e: e.tensor_tensor(out=LBV[:, :, 2], in0=LBT[:, :, 0], in1=LBV[:, :, 1], op=ALU.mult),
          reads=["LBV", "LBT"], writes=["LBV"])
    P.add("dve", lambda e: e.tensor_scalar(out=LBV[:, :, 3], in0=LBV[:, :, 2], scalar1=-1.0, scalar2=1.0,
                                           op0=ALU.mult, op1=ALU.add), reads=["LBV"], writes=["LBV"])
    si = bi = ci = mi = 0
    for hd in range(2):
        lb = LBV[:, hd, 2:3]
        oml = LBV[:, hd, 3:4]
        for dr in range(2):
            P.add("dve", lambda e: e.memset(S32[:], 0.0), writes=["S32"])
            P.add("dve", lambda e: e.memset(SBF[:], 0.0), writes=["SBF"])
            fsrc = ffT if dr == 0 else fbT
            mid = 31 if dr == 0 else 32
            last = 63 if dr == 0 else 0
            for seg in (range(8) if dr == 0 else range(7, -1, -1)):
                p = si % 2
                si += 1
                t0 = seg * SEG
                q_s, f_s, v_f, v_s, g_s = Qs[p], Fs[p], Vf[p], Vs[p], Gs[p]
                qk, fk, vfk, vk, gk = "Qs%d" % p, "Fs%d" % p, "Vf%d" % p, "Vs%d" % p, "Gs%d" % p
                P.dma("sp", q_s[:], qT[hd, :, t0:t0 + SEG], reads=["dram:hq"], writes=[qk])
                P.dma("sp", f_s[:], fsrc[hd, :, t0:t0 + SEG], reads=["dram:hq"], writes=[fk])
                P.dma("sp", v_f[:], iv[hd, t0:t0 + SEG, :].rearrange("(b p) v -> p b v", p=128), reads=["dram:hi"], writes=[vfk])
                P.add("pool", lambda e, v_s=v_s, v_f=v_f: e.tensor_copy(out=v_s[:], in_=v_f[:]), reads=[vfk], writes=[vk])
                if dr == 1:
                    P.dma("sp", g_s[:], gg[hd, t0:t0 + SEG, :].rearrange("(c p) v -> p c v", p=64), reads=["dram:hi"], writes=[gk])
                qs, ks, kp, qb, dec = QS[p], KS[p], KP[p], QB[p], DEC[p]
                qsk, ksk, kpk, qbk, deck = "QS%d" % p, "KS%d" % p, "KP%d" % p, "QB%d" % p, "DEC%d" % p
                P.add("act", lambda e, f_s=f_s: e.activation(out=Fg[:], in_=f_s[:], func=AF.Sigmoid), reads=[fk], writes=["Fg"])
                P.add("dve", lambda e, oml=oml, lb=lb: e.tensor_scalar(out=Fg[:], in0=Fg[:], scalar1=oml, scalar2=lb,
                                                                       op0=ALU.mult, op1=ALU.add), reads=["Fg", "LBV"], writes=["Fg"])
                P.add("dve", lambda e: e.tensor_scalar(out=Kt[:], in0=Fg[:], scalar1=-1.0, scalar2=1.0, op0=ALU.mult, op1=ALU.add),
                      reads=["Fg"], writes=["Kt"])
                P.add("act", lambda e: e.activation(out=Fg[:], in_=Fg[:], func=AF.Ln), reads=["Fg"], writes=["Fg"])
                P.add("dve", lambda e: e.tensor_tensor_scan(out=Bt[:], data0=RM[:], data1=Fg[:], initial=0.0,
                                                            op0=ALU.mult, op1=ALU.add), reads=["RM", "Fg"], writes=["Bt"])
                B3 = Bt[:].rearrange("p (c t) -> p c t", t=64)
                T3 = Tm[:].rearrange("p (c t) -> p c t", t=64)
                if dr == 1:
                    P.add("dve", lambda e, B3=B3, T3=T3: e.tensor_tensor(
                        out=T3, in0=B3[:, :, 63:64].to_broadcast([128, 8, 64]), in1=B3, op=ALU.subtract),
                        reads=["Bt"], writes=["Tm"])
                    P.add("dve", lambda e: e.tensor_tensor(out=Bt[:], in0=Tm[:], in1=Fg[:], op=ALU.add),
                          reads=["Tm", "Fg"], writes=["Bt"])
                P.add("dve", lambda e, B3=B3, T3=T3, mid=mid: e.tensor_tensor(
                    out=T3, in0=B3, in1=B3[:, :, mid:mid + 1].to_broadcast([128, 8, 64]), op=ALU.subtract),
                    reads=["Bt"], writes=["Tm"])
                P.add("act", lambda e: e.activation(out=Et[:], in_=Tm[:], func=AF.Exp), reads=["Tm"], writes=["Et"])
                P.add("dve", lambda e, qs=qs, q_s=q_s: e.tensor_tensor(out=qs[:], in0=q_s[:], in1=Et[:], op=ALU.mult),
                      reads=[qk, "Et"], writes=[qsk])
                P.add("act", lambda e: e.activation(out=Et[:], in_=Tm[:], func=AF.Exp, scale=-1.0), reads=["Tm"], writes=["Et"])
                P.add("dve", lambda e, ks=ks: e.tensor_tensor(out=ks[:], in0=Kt[:], in1=Et[:], op=ALU.mult),
                      reads=["Kt", "Et"], writes=[ksk])
                P.add("dve", lambda e, B3=B3, T3=T3, last=last: e.tensor_tensor(
                    out=T3, in0=B3[:, :, last:last + 1].to_broadcast([128, 8, 64]), in1=B3, op=ALU.subtract),
                    reads=["Bt"], writes=["Tm"])
                P.add("act", lambda e: e.activation(out=Et[:], in_=Tm[:], func=AF.Exp), reads=["Tm"], writes=["Et"])
                P.add("dve", lambda e, kp=kp: e.tensor_tensor(out=kp[:], in0=Kt[:], in1=Et[:], op=ALU.mult),
                      reads=["Kt", "Et"], writes=[kpk])
                P.add("act", lambda e: e.activation(out=Et[:], in_=Bt[:], func=AF.Exp), reads=["Bt"], writes=["Et"])
                P.add("dve", lambda e, qb=qb, q_s=q_s: e.tensor_tensor(out=qb[:], in0=q_s[:], in1=Et[:], op=ALU.mult),
                      reads=[qk, "Et"], writes=[qbk])
                E3v = Et[:].rearrange("p (c t) -> p c t", t=64)
                P.add("dve", lambda e, dec=dec, E3v=E3v, last=last: e.tensor_copy(out=dec[:], in_=E3v[:, :, last]),
                      reads=["Et"], writes=[deck])
                os_ = OS[p]
                osk = "OS%d" % p
                for blk in (range(4) if dr == 0 else range(3, -1, -1)):
                    b2 = bi % 2
                    bi += 1
                    psc, ptp = PSC[b2], PTP[0]
                    at, kpt = AT[b2], KPT[b2]
                    atk, kptk = "AT%d" % b2, "KPT%d" % b2
                    cs = slice(blk * 128, (blk + 1) * 128)
                    P.add("pe", lambda e, psc=psc, ks=ks, qs=qs, cs=cs: e.matmul(
                        psc[:, 0:128], lhsT=ks[:, cs], rhs=qs[:, cs], start=True, stop=True),
                        reads=[ksk, qsk], writes=["psc%d" % b2])
                    P.add("dve", lambda e, at=at, psc=psc, dr=dr: e.tensor_tensor(
                        out=at[:], in0=psc[:, 0:128], in1=MK[:, dr, :], op=ALU.mult),
                        reads=["psc%d" % b2, "MK"], writes=[atk])
                    P.add("pe", lambda e, ptp=ptp, kp=kp, cs=cs: e.transpose(ptp[:, 0:128], kp[:, cs], IDB[:]),
                          reads=[kpk, "IDB"], writes=["ptp0"])
                    P.add("act", lambda e, kpt=kpt, ptp=ptp: e.activation(out=kpt[:], in_=ptp[:, 0:128], func=AF.Copy),
                          reads=["ptp0"], writes=[kptk])
                    for c in ((0, 1) if dr == 0 else (1, 0)):
                        c2 = ci % 2
                        ci += 1
                        po, pkv = PO[c2], PKV[c2]
                        cl = blk * 2 + c
                        n = seg * 8 + cl
                        rs = slice(64 * c, 64 * c + 64)
                        P.add("pe", lambda e, po=po, at=at, v_s=v_s, rs=rs, blk=blk: e.matmul(
                            po[0:64, 0:128], lhsT=at[rs, rs], rhs=v_s[rs, blk, :], start=True, stop=False),
                            reads=[atk, vk], writes=["po%d" % c2])
                        P.add("pe", lambda e, po=po, qb=qb, cl=cl: e.matmul(
                            po[0:64, 0:128], lhsT=qb[:, cl * 64:(cl + 1) * 64], rhs=SBF[:], start=False, stop=True),
                            reads=[qbk, "SBF"], writes=["po%d" % c2])
                        P.add("pe", lambda e, pkv=pkv, kpt=kpt, v_s=v_s, rs=rs, blk=blk: e.matmul(
                            pkv[:, 0:128], lhsT=kpt[rs, :], rhs=v_s[rs, blk, :], start=True, stop=True),
                            reads=[kptk, vk], writes=["pkv%d" % c2])
                        P.add("dve", lambda e, pkv=pkv, dec=dec, cl=cl: e.scalar_tensor_tensor(
                            out=S32[:], in0=S32[:], scalar=dec[:, cl:cl + 1], in1=pkv[:, 0:128], op0=ALU.mult, op1=ALU.add),
                            reads=["S32", deck, "pkv%d" % c2], writes=["S32"])
                        P.add("act", lambda e: e.activation(out=SBF[:], in_=S32[:], func=AF.Copy), reads=["S32"], writes=["SBF"])
                        if dr == 0:
                            P.add("act", lambda e, po=po, n=n: e.activation(out=OF[:, n, :], in_=po[0:64, 0:128], func=AF.Copy),
                                  reads=["po%d" % c2], writes=[("OF", n)])
                        else:
                            P.add("dve", lambda e, po=po, n=n, cl=cl, os_=os_: e.tensor_tensor(
                                out=os_[:, cl, :], in0=po[0:64, 0:128], in1=OF[:, n, :], op=ALU.add),
                                reads=["po%d" % c2, ("OF", n)], writes=[osk])
                if dr == 1:
                    P.add("dve", lambda e, os_=os_: e.tensor_tensor(out=SQ[:], in0=os_[:], in1=os_[:], op=ALU.mult),
                          reads=[osk], writes=["SQ"])
                    P.add("dve", lambda e: e.tensor_reduce(out=SSQ[:], in_=SQ[:], axis=AX.X, op=ALU.add), reads=["SQ"], writes=["SSQ"])
                    P.add("dve", lambda e: e.tensor_scalar(out=SSQ[:], in0=SSQ[:], scalar1=1.0 / 128, scalar2=RMS_EPS,
                                                           op0=ALU.mult, op1=ALU.add), reads=["SSQ"], writes=["SSQ"])
                    P.add("act", lambda e: e.activation(out=SSQ[:], in_=SSQ[:], func=AF.Sqrt), reads=["SSQ"], writes=["SSQ"])
                    P.add("dve", lambda e: e.reciprocal(out=SSQ[:], in_=SSQ[:]), reads=["SSQ"], writes=["SSQ"])
                    P.add("dve", lambda e, os_=os_: e.tensor_tensor(
                        out=os_[:], in0=os_[:], in1=SSQ[:].unsqueeze(2).to_broadcast([64, 8, 128]), op=ALU.mult),
                        reads=[osk, "SSQ"], writes=[osk])
                    P.add("dve", lambda e, os_=os_, hd=hd: e.tensor_tensor(
                        out=os_[:], in0=os_[:], in1=GN[:, hd, :].unsqueeze(1).to_broadcast([64, 8, 128]), op=ALU.mult),
                        reads=[osk, "GN"], writes=[osk])
                    P.add("act", lambda e, g_s=g_s: e.activation(out=g_s[:], in_=g_s[:], func=AF.Silu), reads=[gk], writes=[gk])
                    P.add("dve", lambda e, os_=os_, g_s=g_s: e.tensor_tensor(out=os_[:], in0=os_[:], in1=g_s[:], op=ALU.mult),
                          reads=[osk, gk], writes=[osk])
                    for s_ in range(8):
                        om = OM[mi % 4]
                        omk = "OM%d" % (mi % 4)
                        mi += 1
                        P.add("pool", lambda e, om=om, os_=os_, s_=s_: e.tensor_scalar(
                            out=om[:], in0=os_[:], scalar1=OH[0:64, s_:s_ + 1], scalar2=0.0, op0=ALU.mult, op1=ALU.add),
                            reads=[osk, "OH"], writes=[omk])
                        P.dma("sp", IB1v[s_, hd, t0:t0 + SEG, :].rearrange("(c p) v -> p c v", p=64), om[:],
                              reads=[omk], writes=["dram:IB1"])


def emit_projln(P, fill_G, w, x_res, lng, lnb, x_out, IB, OB, OH, RG, dram_reads, IDF):
    P.begin_phase()
    G = P.sbuf("G", [128, 16, TOK], BF16)
    Y = P.sbuf("Y", [128, 8, D], F32)
    WB = [P.sbuf("WB%d" % i, [128, 16, 512], BF16) for i in range(2)]
    STG = [WB[i][:].rearrange("p k n -> p (k n)")[:, 0:2 * D].bitcast(F32) for i in range(2)]
    LG = P.sbuf("LG", [128, D], F32)
    LB = P.sbuf("LB", [128, D], F32)
    ST = P.sbuf("ST", [128, 4, 6], F32)
    MV = P.sbuf("MV", [128, 4], F32)
    HR = P.sbuf("HR", [2, D], F32)
    HM = [P.sbuf("HM%d" % i, [2, D], F32) for i in range(2)]
    PS = [P.psum("ps%d" % i, [128, 512]) for i in range(6)]
    P.dma("sp", LG[:], lng, writes=["lnp"])
    P.dma("sp", LB[:], lnb, writes=["lnp2"])
    fill_G(P, G, STG, PS[4:6], ["ps4", "ps5"])
    w_v = w.rearrange("(k p) n -> p k n", p=128)
    xr_v = x_res.rearrange("(t p) d -> p t d", p=128)
    xo_v = x_out.rearrange("(t p) d -> p t d", p=128)
    for tt in range(8):
        P.dma("sp", Y[:, tt, :], xr_v[:, tt, :], writes=[("Y", tt)])
    ps_i = 0
    for cb in range(4):
        wb = WB[cb % 2]
        wkey = "WB%d" % (cb % 2)
        P.dma("pool", wb[:], w_v[:, :, cb * 512:(cb + 1) * 512], writes=[wkey])
        for tt in range(8):
            pi = ps_i % 4
            ps_i += 1
            pm = PS[pi]
            pmk = "ps%d" % pi
            for k in range(16):
                P.add("pe", lambda e, k=k, pm=pm, tt=tt, wb=wb: e.matmul(
                    pm[:], lhsT=G[:, k, tt * 128:(tt + 1) * 128], rhs=wb[:, k, :],
                    start=(k == 0), stop=(k == 15)), reads=[wkey, ("G", k), "Gb"], writes=[pmk])
            ysl = Y[:, tt, cb * 512:(cb + 1) * 512]
            P.add("dve", lambda e, ysl=ysl, pm=pm: e.scalar_tensor_tensor(
                out=ysl, in0=ysl, scalar=ALPHA, in1=pm[:], op0=ALU.mult, op1=ALU.add),
                reads=[pmk, ("Y", tt)], writes=[("Y", tt)])
    for tt in range(8):
        ln_tile(P, Y[:, tt, :], ("Y", tt), LG[:], LB[:], Y[:, tt, :], ("Y", tt), ST, MV, "ln")
        P.dma("sp", xo_v[:, tt, :], Y[:, tt, :], reads=[("Y", tt)], writes=["dram:xo%d" % tt])
    P.dma("sp", HR[0:1, :], x_out[TOK - 1:TOK, :], reads=["dram:xo7"], writes=["HR"])
    P.dma("sp", HR[1:2, :], x_out[0:1, :], reads=["dram:xo0"], writes=["HR"])
    MLH = P.sbuf("MLH", [2, 8], F32)
    P.dma("sp", MLH[:], OH, writes=["MLH"])
    for s_ in range(8):
        hm = HM[s_ % 2]
        P.add("dve", lambda e, hm=hm, s_=s_: e.tensor_scalar(out=hm[:], in0=HR[:], scalar1=MLH[:, s_:s_ + 1], scalar2=None,
                                                            op0=ALU.mult), reads=["HR", "MLH"], writes=["HM%d" % (s_ % 2)])
        P.dma("sp", IB[2 * s_:2 * s_ + 2, :], hm[:], reads=["HM%d" % (s_ % 2)], writes=["dram:IBh"])
    P.cc(lambda e: e.collective_compute("AllReduce", ALU.add, replica_groups=RG, ins=[IB.opt()], outs=[OB.opt()]),
         reads=["dram:IBh"], writes=["dram:OBh"])
    P.end_phase()


def emit_ffn(P, x_in, OBh, selm, w_up, cwb, w_down, lng, lnb, dst, IDF, final=False):
    P.begin_phase()
    G = P.sbuf("G", [128, 44, TOK], BF16)
    XY = P.sbuf("XY", [128, 16 * (TOK + 2)], BF16)
    XT = XY[:].rearrange("p (k t) -> p k t", k=16)
    Y = XY[:, 0:16384].bitcast(F32).rearrange("p (t d) -> p t d", t=4)
    WB = [P.sbuf("WB%d" % i, [128, 44 * 256], BF16) for i in range(2)]
    STG = [WB[i][:, 0:4096].bitcast(F32) for i in range(2)]
    H = [[P.sbuf("H%d_%d" % (gv, s), [128, 514], F32) for s in range(2)] for gv in range(2)]
    TT = [[P.sbuf("T%d_%d" % (gv, s), [128, 512], F32) for s in range(2)] for gv in range(2)]
    CW = P.sbuf("CW", [128, 88, 4], F32)
    LG = P.sbuf("LG", [128, D], F32)
    LB = P.sbuf("LB", [128, D], F32)
    ST = P.sbuf("ST", [128, 4, 6], F32)
    MV = P.sbuf("MV", [128, 4], F32)
    PS = [P.psum("ps%d" % i, [128, 512]) for i in range(7)]
    P.dma("sp", CW[:], cwb, writes=["CW"])
    P.dma("sp", LG[:], lng, writes=["lnp"])
    P.dma("sp", LB[:], lnb, writes=["lnp2"])
    OHt = selm
    STH = G[:, 0:4, :].rearrange("p a t -> p (a t)").bitcast(F32)
    CND = [G[:, 4 + 4 * i:8 + 4 * i, :].rearrange("p a t -> p (a t)").bitcast(F32) for i in range(2)]
    gk_ = [("G", i) for i in range(12)]
    P.add("dve", lambda e: e.memset(STH, 0.0), writes=gk_)
    OBv = OBh.rearrange("(r h) d -> h r d", h=2)
    for r_ in range(8):
        cd = CND[r_ % 2]
        P.dma("sp", cd[0:2, :], OBv[:, r_, :], writes=["cnd%d" % (r_ % 2)])
        P.add("dve", lambda e, cd=cd, r_=r_: e.scalar_tensor_tensor(
            out=STH[0:2, :], in0=cd[0:2, :], scalar=OHt[0:2, r_:r_ + 1], in1=STH[0:2, :], op0=ALU.mult, op1=ALU.add),
            reads=["cnd%d" % (r_ % 2), "OH"] + gk_, writes=gk_)
    n = 0
    for tt in range(8):
        st = STG[tt % 2]
        sk = "WB%d" % (tt % 2)
        P.dma("sp", st, x_in[tt * 128:(tt + 1) * 128, :], writes=[sk])
        for c4 in range(4):
            pi = n % 3
            n += 1
            ps = PS[pi]
            for g in range(4):
                cb = c4 * 4 + g
                P.add("pe", lambda e, ps=ps, g=g, st=st, cb=cb: e.transpose(
                    ps[:, g * 128:(g + 1) * 128], st[:, cb * 128:(cb + 1) * 128], IDF[:]),
                    reads=[sk, "IDF"], writes=["ps%d" % pi])
            P.add("act", lambda e, ps=ps, c4=c4, tt=tt: e.activation(
                out=XT[:, c4 * 4:(c4 + 1) * 4, 1 + tt * 128: 1 + (tt + 1) * 128],
                in_=ps[:].rearrange("p (g t) -> p g t", g=4), func=AF.Copy), reads=["ps%d" % pi], writes=["XY"])
    for c4 in range(4):
        ps = PS[3 + c4 % 2]
        for g in range(4):
            cb = c4 * 4 + g
            P.add("pe", lambda e, ps=ps, g=g, cb=cb: e.transpose(
                ps[:, g * 128:(g + 1) * 128], STH[:, cb * 128:(cb + 1) * 128], IDF[:]),
                reads=gk_ + ["IDF"], writes=["ps%d" % (3 + c4 % 2)])
        P.add("act", lambda e, ps=ps, c4=c4: e.activation(
            out=XT[:, c4 * 4:(c4 + 1) * 4, 0:TOK + 2:TOK + 1],
            in_=ps[:].rearrange("p (g t) -> p g t", g=4)[:, :, 0:2], func=AF.Copy),
            reads=["ps%d" % (3 + c4 % 2)], writes=["XY"])
    w_up_v = w_up.rearrange("(k p) n -> p k n", p=128)
    ps_i = 0
    for grp in range(22):
        wb = WB[grp % 2]
        wv = wb[:, 0:2 * 16 * 256].rearrange("p (g k n) -> p g k n", g=2, k=16)
        wkey = "WB%d" % (grp % 2)
        for gv in range(2):
            c0 = gv * DFF + grp * 256
            P.dma("pool", wv[:, gv, :, :], w_up_v[:, :, c0:c0 + 256], writes=[wkey])
        for cc in range(2):
            c = grp * 2 + cc
            for blk in range(2):
                slot = (c * 2 + blk) % 2
                for gv in range(2):
                    pm = PS[ps_i % 3]
                    ph = PS[3 + ps_i % 2]
                    pmk = "ps%d" % (ps_i % 3)
                    phk = "ps%d" % (3 + ps_i % 2)
                    ps_i += 1
                    t0 = 1 + blk * 512
                    for k in range(16):
                        P.add("pe", lambda e, k=k, pm=pm, gv=gv, cc=cc, t0=t0, wv=wv: e.matmul(
                            pm[:], lhsT=wv[:, gv, k, cc * 128:(cc + 1) * 128], rhs=XT[:, k, t0:t0 + 512],
                            start=(k == 0), stop=(k == 15)), reads=[wkey, "XY"], writes=[pmk])
                    for k in range(16):
                        P.add("pe", lambda e, k=k, ph=ph, gv=gv, cc=cc, t0=t0, wv=wv: e.matmul(
                            ph[:, 0:2], lhsT=wv[:, gv, k, cc * 128:(cc + 1) * 128], rhs=XT[:, k, t0 - 1:t0 + 513:513],
                            start=(k == 0), stop=(k == 15)), reads=[wkey, "XY"], writes=[phk])
                    h = H[gv][slot]
                    hk = "H%d_%d" % (gv, slot)
                    P.add("act", lambda e, h=h, pm=pm: e.activation(out=h[:, 1:513], in_=pm[:], func=AF.Copy),
                          reads=[pmk], writes=[hk])
                    P.add("act", lambda e, h=h, ph=ph: e.activation(out=h[:, 0:514:513], in_=ph[:, 0:2], func=AF.Copy),
                          reads=[phk], writes=[hk])
                    tt_ = TT[gv][slot]
                    tk = "T%d_%d" % (gv, slot)
                    ch = gv * 44 + c
                    P.add("dve", lambda e, tt_=tt_, h=h, ch=ch: e.tensor_scalar(
                        out=tt_[:], in0=h[:, 0:512], scalar1=CW[:, ch, 0:1], scalar2=CW[:, ch, 3:4],
                        op0=ALU.mult, op1=ALU.add), reads=[hk, "CW"], writes=[tk])
                    for j in (1, 2):
                        P.add("dve", lambda e, tt_=tt_, h=h, ch=ch, j=j: e.scalar_tensor_tensor(
                            out=tt_[:], in0=h[:, j:j + 512], scalar=CW[:, ch, j:j + 1], in1=tt_[:],
                            op0=ALU.mult, op1=ALU.add), reads=[hk, "CW", tk], writes=[tk])
                tg = TT[0][slot]
                tv = TT[1][slot]
                P.add("act", lambda e, tg=tg: e.activation(out=tg[:], in_=tg[:], func=AF.Silu),
                      reads=["T0_%d" % slot], writes=["T0_%d" % slot])
                P.add("dve", lambda e, tg=tg, tv=tv, c=c, blk=blk: e.tensor_tensor(
                    out=G[:, c, blk * 512:(blk + 1) * 512], in0=tg[:], in1=tv[:], op=ALU.mult),
                    reads=["T0_%d" % slot, "T1_%d" % slot], writes=[("G", c)])
    w_down_v = w_down.rearrange("(k p) n -> p k n", p=128)
    x1_v = x_in.rearrange("(t p) d -> p t d", p=128)
    y_v = dst.rearrange("(t p) d -> p t d", p=128)
    li = 0
    for half in range(2):
        P.dma("sp", Y[:], x1_v[:, half * 4:(half + 1) * 4, :], writes=["XY"])
        for cb in range(8):
            wb = WB[li % 2]
            wkey = "WB%d" % (li % 2)
            li += 1
            wv = wb[:].rearrange("p (k n) -> p k n", k=44)
            P.dma("pool", wv[:, 0:22, :], w_down_v[:, 0:22, cb * 256:(cb + 1) * 256], writes=[wkey])
            P.dma("pool", wv[:, 22:44, :], w_down_v[:, 22:44, cb * 256:(cb + 1) * 256], writes=[wkey + "b"])
            for t4 in range(4):
                tt = half * 4 + t4
                pi = 5 + (ps_i % 2)
                ps_i += 1
                pm = PS[pi]
                pmk = "ps%d" % pi
                for k in range(44):
                    P.add("pe", lambda e, k=k, pm=pm, tt=tt, wv=wv: e.matmul(
                        pm[:, 0:256], lhsT=G[:, k, tt * 128:(tt + 1) * 128], rhs=wv[:, k, :],
                        start=(k == 0), stop=(k == 43)), reads=[wkey, wkey + "b", ("G", k)], writes=[pmk])
                ysl = Y[:, t4, cb * 256:(cb + 1) * 256]
                P.add("dve", lambda e, ysl=ysl, pm=pm: e.scalar_tensor_tensor(
                    out=ysl, in0=ysl, scalar=ALPHA, in1=pm[:, 0:256], op0=ALU.mult, op1=ALU.add),
                    reads=[pmk, "XY"], writes=["XY"])
        for t4 in range(4):
            tt = half * 4 + t4
            ln_tile(P, Y[:, t4, :], "XY", LG[:], LB[:], Y[:, t4, :], "XY", ST, MV, "ln")
            P.dma("sp", y_v[:, tt, :], Y[:, t4, :], reads=["XY"], final=final)
    P.end_phase()


def emit_qkv(P, x_in, w, cos, sinp, gains, q_s, IB2, OH, IDF):
    P.begin_phase()
    XTt = P.sbuf("XT", [128, 16 * TOK], BF16)
    XT = XTt[:].rearrange("p (k t) -> p k t", k=16)
    TMP = XTt[:, 0:2 * NRM].bitcast(F32)
    TMP2 = XTt[:, 2 * NRM:4 * NRM].bitcast(F32)
    WB = [P.sbuf("WB%d" % i, [128, 16, 512], BF16) for i in range(2)]
    H = P.sbuf("H", [128, 8, QKV], F32)
    GN = P.sbuf("GN", [128, 4, HD], F32)
    TAB = P.sbuf("TAB", [128, 4, 8, HD], F32)
    SS = P.sbuf("SS", [128, 24], F32)
    KVM = [P.sbuf("KVM%d" % i, [128, 1024], BF16) for i in range(4)]
    PS = [P.psum("ps%d" % i, [128, 512]) for i in range(6)]
    STG = [H[:, i, 0:D] for i in range(2)]
    n = 0
    for tt in range(8):
        st = STG[tt % 2]
        sk = ("H", tt % 2)
        P.dma("sp", st, x_in[tt * 128:(tt + 1) * 128, :], writes=[sk])
        for c4 in range(4):
            pi = 4 + n % 2
            n += 1
            ps = PS[pi]
            for g in range(4):
                cb = c4 * 4 + g
                P.add("pe", lambda e, ps=ps, g=g, st=st, cb=cb: e.transpose(
                    ps[:, g * 128:(g + 1) * 128], st[:, cb * 128:(cb + 1) * 128], IDF[:]),
                    reads=[sk, "IDF"], writes=["ps%d" % pi])
            P.add("act", lambda e, ps=ps, c4=c4, tt=tt: e.activation(
                out=XT[:, c4 * 4:(c4 + 1) * 4, tt * 128:(tt + 1) * 128],
                in_=ps[:].rearrange("p (g t) -> p g t", g=4), func=AF.Copy), reads=["ps%d" % pi], writes=["XT"])
    P.dma("sp", GN[:], gains, writes=["GN"])
    for i in range(4):
        src = cos if i % 2 == 0 else sinp
        P.dma("sp", TAB[:, i, :, :], src.rearrange("(t p) f -> p t f", p=128), writes=[("TAB", i)])
        P.add("dve", lambda e, i=i: e.tensor_tensor(
            out=TAB[:, i, :, :], in0=TAB[:, i, :, :], in1=GN[:, i, :].unsqueeze(1).to_broadcast([128, 8, HD]), op=ALU.mult),
            reads=[("TAB", i), "GN"], writes=[("TAB", i)])
    w_v = w.rearrange("(k p) n -> p k n", p=128)

    def sink(P, cb, tt, pm, pmk):
        P.add("act", lambda e: e.activation(out=H[:, tt, cb * 512:(cb + 1) * 512], in_=pm[:], func=AF.Copy),
              reads=[pmk], writes=[("H", tt)])

    emit_proj_tm(P, XT, "XT", w_v, QKV, WB, PS[0:4], sink)
    NH = NQH + NKH
    q_v = q_s.rearrange("(t p) n -> p t n", p=128)
    mi = 0
    for tt in range(8):
        X = H[:, tt, 0:NRM]
        hk = ("H", tt)
        P.add("dve", lambda e, X=X: e.tensor_tensor(out=TMP, in0=X, in1=X, op=ALU.mult), reads=[hk], writes=["XT"])
        P.add("dve", lambda e: e.tensor_reduce(out=SS[:, 0:NH], in_=TMP.rearrange("p (h f) -> p h f", f=HD),
                                               axis=AX.X, op=ALU.add), reads=["XT"], writes=["SS"])
        P.add("dve", lambda e: e.tensor_scalar(out=SS[:, 0:NH], in0=SS[:, 0:NH], scalar1=1.0 / HD, scalar2=RMS_EPS,
                                               op0=ALU.mult, op1=ALU.add), reads=["SS"], writes=["SS"])
        P.add("act", lambda e: e.activation(out=SS[:, 0:NH], in_=SS[:, 0:NH], func=AF.Sqrt), reads=["SS"], writes=["SS"])
        P.add("dve", lambda e: e.reciprocal(out=SS[:, 0:NH], in_=SS[:, 0:NH]), reads=["SS"], writes=["SS"])
        P.add("dve", lambda e, X=X: e.tensor_tensor(
            out=TMP.rearrange("p (h f) -> p h f", f=HD), in0=X.rearrange("p (h f) -> p h f", f=HD),
            in1=SS[:, 0:NH].unsqueeze(2).to_broadcast([128, NH, HD]), op=ALU.mult), reads=[hk, "SS"], writes=["XT"])
        for (h0, nh, ti) in ((0, NQH, 0), (NQH, NKH, 2)):
            xn = TMP[:, h0 * HD:(h0 + nh) * HD]
            xo = X[:, h0 * HD:(h0 + nh) * HD]
            t2 = TMP2[:, h0 * HD:(h0 + nh) * HD]
            P.add("dve", lambda e, xn=xn, xo=xo, nh=nh, ti=ti, tt=tt: e.tensor_tensor(
                out=xo.rearrange("p (h f) -> p h f", f=HD), in0=xn.rearrange("p (h f) -> p h f", f=HD),
                in1=TAB[:, ti, tt, :].unsqueeze(1).to_broadcast([128, nh, HD]), op=ALU.mult),
                reads=["XT", ("TAB", ti)], writes=[hk])
            for s in range(2):
                P.add("dve", lambda e, xn=xn, t2=t2, nh=nh, ti=ti, tt=tt, s=s: e.tensor_tensor(
                    out=t2.rearrange("p (h a s f) -> p h a s f", a=2, s=2, f=32)[:, :, :, s, :],
                    in0=xn.rearrange("p (h a s f) -> p h a s f", a=2, s=2, f=32)[:, :, :, 1 - s, :],
                    in1=TAB[:, ti + 1, tt, :].rearrange("p (a s f) -> p a s f", a=2, s=2)[:, :, s, :]
                        .unsqueeze(1).to_broadcast([128, nh, 2, 32]), op=ALU.mult),
                    reads=["XT", ("TAB", ti + 1)], writes=["XT2"])
            P.add("dve", lambda e, xo=xo, t2=t2: e.tensor_tensor(out=xo, in0=xo, in1=t2, op=ALU.add),
                  reads=[hk, "XT2"], writes=[hk])
        P.dma("sp", q_v[:, tt, :], H[:, tt, 0:NQH * HD], reads=[hk], writes=["dram:q"])
        for s_ in range(8):
            kvm = KVM[mi % 4]
            kk = "KVM%d" % (mi % 4)
            mi += 1
            P.add("pool", lambda e, kvm=kvm, tt=tt, s_=s_: e.tensor_scalar(
                out=kvm[:], in0=H[:, tt, NQH * HD:QKV], scalar1=OH[:, s_:s_ + 1], scalar2=0.0, op0=ALU.mult, op1=ALU.add),
                reads=[hk, "OH"], writes=[kk])
            P.dma("sp", IB2[s_ * TOK + tt * 128: s_ * TOK + (tt + 1) * 128, :], kvm[:], reads=[kk], writes=["dram:IB2"])
    P.end_phase()


def emit_attn(P, q_s, OB2, o_s, IDF, IDB, OHB):
    P.begin_phase()
    QT = P.sbuf("QT", [128, NQH, TOK], BF16)
    KT = P.sbuf("KT", [128, NKH, T], BF16)
    VE = P.sbuf("VE", [128, 32, NKH, VW], BF16)
    PT = [P.sbuf("PT%d" % i, [128, 512], BF16) for i in range(3)]
    OBt = [P.sbuf("OB%d" % i, [128, NQH * HD], F32) for i in range(2)]
    RC = P.sbuf("RC", [128, 8], F32)
    IDM2 = P.sbuf("IDM2", [128, 2, 128], BF16)
    CK = [P.sbuf("CK%d" % i, [128, 2, 8, 512], BF16) for i in range(2)]
    PS = [P.psum("ps%d" % i, [128, 512]) for i in range(7)]
    load_T(P, q_s, TOK, NQH * HD, QT, "QT", OBt, IDF, PS[4:7], ["ps4", "ps5", "ps6"], skeys=["OB0", "OB1"])
    for b_ in range(2):
        P.add("dve", lambda e, b_=b_: e.tensor_scalar(out=IDM2[:, b_, :], in0=IDB[:], scalar1=OHB[:, b_:b_ + 1], scalar2=None,
                                                     op0=ALU.mult), reads=["IDB", "OHB"], writes=["IDM2"])
    P.add("dve", lambda e: e.memset(VE[:].rearrange("p c k w -> p (c k w)"), 1.0), writes=["VE"])
    n = 0
    for jp in range(4):
        ck = CK[0]
        for b_ in range(2):
            P.dma("sp", ck[:, b_, :, :], OB2[(4 * b_ + jp) * TOK:(4 * b_ + jp + 1) * TOK, 0:512].rearrange("(t p) n -> p t n", p=128),
                  writes=["CK0"])
        for kv in range(NKH):
            for t4 in range(2):
                pi = 4 + n % 3
                n += 1
                ps = PS[pi]
                for g in range(4):
                    tt = t4 * 4 + g
                    for b_ in range(2):
                        P.add("pe", lambda e, ps=ps, g=g, ck=ck, b_=b_, tt=tt, kv=kv: e.matmul(
                            ps[:, g * 128:(g + 1) * 128], lhsT=ck[:, b_, tt, kv * 128:(kv + 1) * 128], rhs=IDM2[:, b_, :],
                            start=(b_ == 0), stop=(b_ == 1)), reads=["CK0", "IDM2"], writes=["ps%d" % pi])
                P.add("act", lambda e, ps=ps, kv=kv, jp=jp, t4=t4: e.activation(
                    out=KT[:, kv, jp * TOK + t4 * 512: jp * TOK + (t4 + 1) * 512], in_=ps[:], func=AF.Copy),
                    reads=["ps%d" % pi], writes=["KT"])
        cv = CK[1]
        for b_ in range(2):
            P.dma("sp", cv[:, b_, :, :], OB2[(4 * b_ + jp) * TOK:(4 * b_ + jp + 1) * TOK, 512:1024].rearrange("(t p) n -> p t n", p=128),
                  writes=["CK1"])
        vev = VE[:, jp * 8:(jp + 1) * 8, :, 0:HD]
        P.add("dve", lambda e, vev=vev, cv=cv: e.tensor_scalar(
            out=vev, in0=cv[:, 0, :, :].rearrange("p t (k f) -> p t k f", k=NKH), scalar1=OHB[:, 0:1], scalar2=None, op0=ALU.mult),
            reads=["CK1", "OHB", "VE"], writes=["VE"])
        P.add("dve", lambda e, vev=vev, cv=cv: e.scalar_tensor_tensor(
            out=vev, in0=cv[:, 1, :, :].rearrange("p t (k f) -> p t k f", k=NKH), scalar=OHB[:, 1:2], in1=vev,
            op0=ALU.mult, op1=ALU.add), reads=["CK1", "OHB", "VE"], writes=["VE"])
    o_v = o_s.rearrange("(t p) n -> p t n", p=128)
    scale = float(HD) ** -0.5
    iters = [(qt, kv, sc) for qt in range(8) for kv in range(NKH) for sc in range(32)]

    def emit_S(i):
        qt, kv, sc = iters[i]
        sb = 4 + i % 3
        st = PS[sb]
        stk = "ps%d" % sb
        pt = PT[i % 3]
        ptk = "PT%d" % (i % 3)
        P.add("pe", lambda e, st=st, kv=kv, sc=sc, qt=qt: e.matmul(
            st[:].rearrange("p (h q) -> p h q", h=4), lhsT=KT[:, kv, sc * 128:(sc + 1) * 128],
            rhs=QT[:, 4 * kv:4 * kv + 4, qt * 128:(qt + 1) * 128], start=True, stop=True),
            reads=["QT", "KT"], writes=[stk])
        P.add("act", lambda e, st=st, pt=pt: e.activation(out=pt[:], in_=st[:], func=AF.Exp, scale=scale),
              reads=[stk], writes=[ptk])

    emit_S(0)
    emit_S(1)
    for i, (qt, kv, sc) in enumerate(iters):
        pt = PT[i % 3]
        ptk = "PT%d" % (i % 3)
        ob = OBt[qt % 2]
        obk = "OB%d" % (qt % 2)
        for hh in range(4):
            P.add("pe", lambda e, hh=hh, pt=pt, sc=sc, kv=kv: e.matmul(
                PS[hh][:, 0:VW], lhsT=pt[:, hh * 128:(hh + 1) * 128], rhs=VE[:, sc, kv, :],
                start=(sc == 0), stop=(sc == 31)), reads=[ptk, "VE"], writes=["ps%d" % hh])
        if i + 2 < len(iters):
            emit_S(i + 2)
        if sc == 31:
            for hh in range(4):
                hq = 4 * kv + hh
                P.add("dve", lambda e, hh=hh: e.reciprocal(out=RC[:, hh:hh + 1], in_=PS[hh][:, HD:HD + 1]),
                      reads=["ps%d" % hh], writes=[("RC", hh)])
                P.add("dve", lambda e, hh=hh, hq=hq, ob=ob: e.tensor_scalar(
                    out=ob[:, hq * HD:(hq + 1) * HD], in0=PS[hh][:, 0:HD], scalar1=RC[:, hh:hh + 1], scalar2=None,
                    op0=ALU.mult), reads=["ps%d" % hh, ("RC", hh)], writes=[obk])
            if kv == NKH - 1:
                P.dma("sp", o_v[:, qt, :], ob[:], reads=[obk], writes=["dram:o"])
    P.end_phase()


def load_T(P, src, ntok, W, dst, dkey, ST, IDF, PS, pskeys, tok_off=0, skeys=("ldT_st0", "ldT_st1")):
    n = 0
    for tt in range(ntok // 128):
        st = ST[tt % 2]
        sk = skeys[tt % 2]
        P.dma("sp", st[:, 0:W], src[tt * 128:(tt + 1) * 128, :], writes=[sk])
        for c4 in range(W // 512):
            pi = n % len(PS)
            n += 1
            ps = PS[pi]
            for g in range(4):
                cb = c4 * 4 + g
                P.add("pe", lambda e, ps=ps, g=g, st=st, cb=cb: e.transpose(
                    ps[:, g * 128:(g + 1) * 128], st[:, cb * 128:(cb + 1) * 128], IDF[:]),
                    reads=[sk, "IDF"], writes=[pskeys[pi]])
            P.add("act", lambda e, ps=ps, c4=c4, tt=tt: e.activation(
                out=dst[:, c4 * 4:(c4 + 1) * 4, tok_off + tt * 128: tok_off + (tt + 1) * 128],
                in_=ps[:].rearrange("p (g t) -> p g t", g=4), func=AF.Copy),
                reads=[pskeys[pi]], writes=[dkey])


_EI_NAMES = []


def build_fused(dbg=False, stop_after=99):
    nc = bass.Bass("TRN2", target_bir_lowering=False)

    def EI(name, shape, dt=F32):
        return nc.dram_tensor(name, list(shape), dt, kind="ExternalInput").ap()

    def SC(name, shape, dt=F32, cc=False):
        if dbg and not cc:
            return nc.dram_tensor(name, list(shape), dt, kind="ExternalOutput").ap()
        return nc.dram_tensor(name, list(shape), dt).ap()

    names = []

    def EIs(stage, name, shape, dt=F32):
        if stage > stop_after:
            return None
        names.append(name)
        return nc.dram_tensor(name, list(shape), dt, kind="ExternalInput").ap()

    xT_b = EIs(1, "xT_b", [D, T]); xT_own = EIs(2, "xT_own", [D, TOK]); x_own = EIs(4, "x_own", [TOK, D])
    w_hq = EIs(1, "w_hq", [D, 768]); w_hi = EIs(1, "w_hi", [D, 512]); w_uv = EIs(2, "w_uv", [D, 2048])
    lbt = EIs(3, "lbt", [2, 128, 3]); gn = EIs(3, "gn", [2, 64, 128]); masks = EIs(3, "masks", [2, 128, 128])
    rmask = EIs(3, "rmask", [128, 512]); ident = EIs(1, "ident", [128, 128])
    wsT = EIs(2, "wsT", [8, 128, 128]); biasT = EIs(2, "biasT", [128, 8]); glng = EIs(2, "glng", [128, GW]); glnb = EIs(2, "glnb", [128, GW])
    w_out = [EIs(4 + 4 * l, "w_out%d" % l, [D, D]) for l in range(2)]
    ln1g = [EIs(4 + 4 * l, "ln1g%d" % l, [128, D]) for l in range(2)]; ln1b = [EIs(4 + 4 * l, "ln1b%d" % l, [128, D]) for l in range(2)]
    w_up = [EIs(5 + 4 * l, "w_up%d" % l, [D, 2 * DFF]) for l in range(2)]; cwb = [EIs(5 + 4 * l, "cwb%d" % l, [128, 88, 4]) for l in range(2)]
    w_down = [EIs(5 + 4 * l, "w_down%d" % l, [DFF, D]) for l in range(2)]
    ln2g = [EIs(5 + 4 * l, "ln2g%d" % l, [128, D]) for l in range(2)]; ln2b = [EIs(5 + 4 * l, "ln2b%d" % l, [128, D]) for l in range(2)]
    w_qkv = EIs(6, "w_qkv", [D, QKV]); cos = EIs(6, "cos", [TOK, HD]); sinp = EIs(6, "sinp", [TOK, HD]); gains = EIs(6, "gains", [128, 4, HD])
    oh8 = EIs(1, "oh8", [128, 8]); ohb = EIs(1, "ohb", [128, 2]); mlh = EIs(4, "mlh", [2, 8])
    _EI_NAMES[:] = names
    y = nc.dram_tensor("y", [TOK, D], F32, kind="ExternalOutput").ap()
    qT_s = SC("qT_s", [2, 128, T]); ffT_s = SC("ffT_s", [2, 128, T]); fbT_s = SC("fbT_s", [2, 128, T])
    iv_s = SC("iv_s", [2, T, 128]); gg_s = SC("gg_s", [2, T, 128])
    ob_s = SC("ob_s", [TOK, GW])
    IB1 = SC("IB1", [8 * 2 * T, 128], BF16, cc=True); OB1 = SC("OB1", [8 * 2 * T, 128], BF16, cc=True)
    x1_s = SC("x1_s", [TOK, D]); x2_s = SC("x2_s", [TOK, D]); q_s = SC("q_s", [TOK, NQH * HD]); o_s = SC("o_s", [TOK, D])
    x3_s = SC("x3_s", [TOK, D])
    IB2 = SC("IB2", [8 * TOK, 1024], BF16, cc=True); OB2 = SC("OB2", [8 * TOK, 1024], BF16, cc=True)
    IB3 = SC("IB3", [16, D], F32, cc=True); OB3 = SC("OB3", [16, D], F32, cc=True)
    IB4 = SC("IB4", [16, D], F32, cc=True); OB4 = SC("OB4", [16, D], F32, cc=True)
    RG = [list(range(NCORES))]

    P = Prog(nc)
    P.setup_barrier()
    IDF = P.sbuf("IDF", [128, 128], F32)
    IDB = P.sbuf("IDB", [128, 128], BF16)
    OH = P.sbuf("OH", [128, 8], F32)
    OHB = P.sbuf("OHB", [128, 2], F32)

    def done():
        P.stack.close()
        return nc, P

    P.begin_phase()
    P.dma("sp", IDF[:], ident, writes=["IDF"])
    P.dma("pool", IDB[:], ident, writes=["IDB"])
    P.dma("sp", OH[:], oh8, writes=["OH"])
    P.dma("sp", OHB[:], ohb, writes=["OHB"])
    XTb = [P.sbuf("XTb%d" % i, [128, 16, 512], BF16) for i in range(2)]
    WQ = P.sbuf("WQ", [128, 16, 768], BF16)
    WI = P.sbuf("WI", [128, 16, 512], BF16)
    OBF = [P.sbuf("OBF%d" % i, [128, 512], F32) for i in range(4)]
    PS = [P.psum("ps%d" % i, [128, 512]) for i in range(4)]
    P.dma("pool", WQ[:], w_hq.rearrange("(k p) n -> p k n", p=128), writes=["WQ"])
    P.dma("pool", WI[:], w_hi.rearrange("(k p) n -> p k n", p=128), writes=["WI"])
    xTb_v = xT_b.rearrange("(k p) t -> p k t", p=128)
    dstT = [qT_s, ffT_s, fbT_s]
    n = 0
    for tb in range(8):
        xt = XTb[tb % 2]
        xk = "XTb%d" % (tb % 2)
        for k4 in range(4):
            P.dma("pool", xt[:, k4 * 4:(k4 + 1) * 4, :], xTb_v[:, k4 * 4:(k4 + 1) * 4, tb * 512:(tb + 1) * 512], writes=[xk])
        for ch in range(6):
            pi = n % 4
            n += 1
            ps, ob = PS[pi], OBF[pi]
            for k in range(16):
                P.add("pe", lambda e, ps=ps, k=k, ch=ch, xt=xt: e.matmul(
                    ps[:], lhsT=WQ[:, k, ch * 128:(ch + 1) * 128], rhs=xt[:, k, :], start=(k == 0), stop=(k == 15)),
                    reads=["WQ", xk], writes=["ps%d" % pi])
            P.add("act", lambda e, ps=ps, ob=ob: e.activation(out=ob[:], in_=ps[:], func=AF.Copy),
                  reads=["ps%d" % pi], writes=["OBF%d" % pi])
            P.dma("sp", dstT[ch // 2][ch % 2, :, tb * 512:(tb + 1) * 512], ob[:], reads=["OBF%d" % pi], writes=["dram:hq"])
        for tt in range(4):
            pi = n % 4
            n += 1
            ps, ob = PS[pi], OBF[pi]
            for k in range(16):
                P.add("pe", lambda e, ps=ps, k=k, tt=tt, xt=xt: e.matmul(
                    ps[:], lhsT=xt[:, k, tt * 128:(tt + 1) * 128], rhs=WI[:, k, :], start=(k == 0), stop=(k == 15)),
                    reads=["WI", xk], writes=["ps%d" % pi])
            P.add("act", lambda e, ps=ps, ob=ob: e.activation(out=ob[:], in_=ps[:], func=AF.Copy),
                  reads=["ps%d" % pi], writes=["OBF%d" % pi])
            r0 = tb * 512 + tt * 128
            for q4 in range(4):
                dst = (iv_s if q4 < 2 else gg_s)[q4 % 2, r0:r0 + 128, :]
                P.dma("sp", dst, ob[:, q4 * 128:(q4 + 1) * 128], reads=["OBF%d" % pi], writes=["dram:hi"])
    P.end_phase()
    if stop_after <= 1:
        return done()

    P.begin_phase()
    emit_hgrn(P, qT_s, ffT_s, fbT_s, iv_s, gg_s, lbt, gn, masks, rmask, IDB, OH, IB1)
    cc1 = P.cc(lambda e: e.collective_compute("AllReduce", ALU.add, replica_groups=RG, ins=[IB1.opt()], outs=[OB1.opt()]),
               reads=["dram:IB1"], writes=["dram:OB1"])
    cc1.defer = True
    P.end_phase()
    P.begin_phase()
    XT = P.sbuf("XT", [128, 16, TOK], BF16)
    WB = [P.sbuf("WB%d" % i, [128, 16, 512], BF16) for i in range(2)]
    UV = P.sbuf("UV", [128, 8, 2 * GW], F32)
    VN = P.sbuf("VN", [128, 8, GW], BF16)
    WS = P.sbuf("WS", [128, 8, 128], BF16)
    BI = P.sbuf("BI", [128, 8], F32)
    LG = P.sbuf("LG", [128, GW], F32)
    LB = P.sbuf("LB", [128, GW], F32)
    ST2 = P.sbuf("ST2", [128, 2, 6], F32)
    MV = P.sbuf("MV", [128, 4], F32)
    PS = [P.psum("ps%d" % i, [128, 512]) for i in range(6)]
    xTo_v = xT_own.rearrange("(k p) t -> p k t", p=128)
    for k in range(16):
        P.dma("pool", XT[:, k, :], xTo_v[:, k, :], writes=["XT"])
    P.dma("pool", WS[:], wsT.rearrange("g s t -> s g t"), writes=["WS"])
    P.dma("sp", BI[:], biasT, writes=["BI"])
    P.dma("sp", LG[:], glng, writes=["lnp"])
    P.dma("sp", LB[:], glnb, writes=["lnp2"])

    def sink_uv(P, cb, tt, pm, pmk):
        P.add("act", lambda e: e.activation(out=UV[:, tt, cb * 512:(cb + 1) * 512], in_=pm[:], func=AF.Copy),
              reads=[pmk], writes=[("UV", tt)])

    emit_proj_tm(P, XT, "XT", w_uv.rearrange("(k p) n -> p k n", p=128), 2048, WB, PS[0:4], sink_uv)
    ob_v = ob_s.rearrange("(t p) n -> p t n", p=128)
    ps_i = 0
    for n in range(8):
        yv = UV[:, n, GW:2 * GW]
        uk = ("UV", n)
        for c in range(2):
            P.add("dve", lambda e, c=c, yv=yv: e.bn_stats(out=ST2[:, c, :], in_=yv[:, c * 512:(c + 1) * 512]),
                  reads=[uk], writes=["st"])
        P.add("dve", lambda e: e.bn_aggr(out=MV[:, 0:2], in_=ST2[:].rearrange("p a b -> p (a b)")), reads=["st"], writes=["mv"])
        P.add("dve", lambda e: e.tensor_scalar(out=MV[:, 2:3], in0=MV[:, 1:2], scalar1=LN_EPS, scalar2=None, op0=ALU.add),
              reads=["mv"], writes=["mv"])
        P.add("act", lambda e: e.activation(out=MV[:, 2:3], in_=MV[:, 2:3], func=AF.Sqrt), reads=["mv"], writes=["mv"])
        P.add("dve", lambda e: e.reciprocal(out=MV[:, 3:4], in_=MV[:, 2:3]), reads=["mv"], writes=["mv"])
        P.add("dve", lambda e, yv=yv: e.tensor_scalar(out=yv, in0=yv, scalar1=MV[:, 0:1], scalar2=MV[:, 3:4],
                                                      op0=ALU.subtract, op1=ALU.mult), reads=[uk, "mv"], writes=[uk])
        P.add("dve", lambda e, yv=yv: e.tensor_tensor(out=yv, in0=yv, in1=LG[:], op=ALU.mult), reads=[uk, "lnp"], writes=[uk])
        P.add("dve", lambda e, yv=yv, n=n: e.tensor_tensor(out=VN[:, n, :], in0=yv, in1=LB[:], op=ALU.add),
              reads=[uk, "lnp2"], writes=[("VN", n)])
        for g4 in range(2):
            pi = 4 + ps_i % 2
            ps_i += 1
            pm = PS[pi]
            pmk = "ps%d" % pi
            for gg_ in range(4):
                g = g4 * 4 + gg_
                P.add("pe", lambda e, pm=pm, g=g, gg_=gg_, n=n: e.matmul(
                    pm[:, gg_ * 128:(gg_ + 1) * 128], lhsT=WS[:, g, :], rhs=VN[:, n, g * 128:(g + 1) * 128],
                    start=True, stop=True), reads=["WS", ("VN", n)], writes=[pmk])
            for gg_ in range(4):
                g = g4 * 4 + gg_
                usl = UV[:, n, g * 128:(g + 1) * 128]
                P.add("dve", lambda e, pm=pm, g=g, gg_=gg_, usl=usl: e.scalar_tensor_tensor(
                    out=usl, in0=pm[:, gg_ * 128:(gg_ + 1) * 128], scalar=BI[:, g:g + 1], in1=usl,
                    op0=ALU.add, op1=ALU.mult), reads=[pmk, "BI", uk], writes=[uk])
        P.dma("sp", ob_v[:, n, :], UV[:, n, 0:GW], reads=[uk], writes=["dram:ob"])
    P.end_phase()
    if stop_after <= 2:
        return done()

    if stop_after <= 3:
        return done()

    OB1v = OB1.rearrange("(s h t) v -> s h t v", s=8, h=2)

    def fill_G0(P, G, ST, PS, pskeys):
        IDM = P.sbuf("IDM", [128, 8, 128], BF16)
        CT = [P.sbuf("CT%d" % i, [128, 8, 8, 128], BF16) for i in range(2)]
        for c in range(8):
            P.add("dve", lambda e, c=c: e.tensor_scalar(out=IDM[:, c, :], in0=IDB[:], scalar1=OH[:, c:c + 1], scalar2=None,
                                                        op0=ALU.mult), reads=["IDB", "OH"], writes=["IDM"])
        n = 0
        for jp in range(4):
            for hd in range(2):
                H = 2 * jp + hd
                ct = CT[H % 2]
                ck = "CT%d" % (H % 2)
                for c in range(8):
                    bp, jj = divmod(c, 4)
                    dop = P.dma("sp", ct[:, c, :, :], OB1v[4 * bp + jp, hd, jj * TOK:(jj + 1) * TOK, :].rearrange("(t p) v -> p t v", p=128),
                                writes=[ck])
                    dop.deps.add(cc1.idx)
                for t4 in range(2):
                    pi = n % len(PS)
                    n += 1
                    ps = PS[pi]
                    for g in range(4):
                        tt = t4 * 4 + g
                        for c in range(8):
                            P.add("pe", lambda e, ps=ps, g=g, ct=ct, c=c, tt=tt: e.matmul(
                                ps[:, g * 128:(g + 1) * 128], lhsT=ct[:, c, tt, :], rhs=IDM[:, c, :],
                                start=(c == 0), stop=(c == 7)), reads=[ck, "IDM"], writes=[pskeys[pi]])
                    P.add("act", lambda e, ps=ps, H=H, t4=t4: e.activation(
                        out=G[:, H, t4 * 512:(t4 + 1) * 512], in_=ps[:], func=AF.Copy), reads=[pskeys[pi]], writes=[("G", H)])
        load_T(P, ob_s, TOK, GW, G[:, 8:16, :], "Gb", ST, IDF, PS, pskeys, skeys=("WB0", "WB1"))

    emit_projln(P, fill_G0, w_out[0], x_own, ln1g[0], ln1b[0], x1_s, IB3, OB3, mlh, RG, ["dram:OB1", "dram:ob"], IDF)
    if stop_after <= 4:
        return done()

    emit_ffn(P, x1_s, OB3, OH, w_up[0], cwb[0], w_down[0], ln2g[0], ln2b[0], x2_s, IDF)
    if stop_after <= 5:
        return done()

    emit_qkv(P, x2_s, w_qkv, cos, sinp, gains, q_s, IB2, OH, IDF)
    P.begin_phase()
    P.cc(lambda e: e.collective_compute("AllReduce", ALU.add, replica_groups=RG, ins=[IB2.opt()], outs=[OB2.opt()]),
         reads=[], writes=[])
    P.end_phase()
    if stop_after <= 6:
        return done()

    emit_attn(P, q_s, OB2, o_s, IDF, IDB, OHB)
    if stop_after <= 7:
        return done()

    def fill_G1(P, G, ST, PS, pskeys):
        load_T(P, o_s, TOK, D, G, "Gb", ST, IDF, PS, pskeys, skeys=("WB0", "WB1"))

    emit_projln(P, fill_G1, w_out[1], x2_s, ln1g[1], ln1b[1], x3_s, IB4, OB4, mlh, RG, [], IDF)
    if stop_after <= 8:
        return done()

    emit_ffn(P, x3_s, OB4, OH, w_up[1], cwb[1], w_down[1], ln2g[1], ln2b[1], y, IDF, final=True)
    return done()


def _fused_inputs(inp):
    f = lambda a: np.ascontiguousarray(np.asarray(a, dtype=np.float32))
    x = f(inp["x"])
    w_in = f(inp["w_in_ab"])[0]
    lb_table = f(inp["hgrn_lb_table"])
    norm_g = f(inp["hgrn_norm_g"])[0]
    idx = np.arange(128)
    same = (idx[:, None] // 64) == (idx[None, :] // 64)
    mfw = (same & (idx[:, None] <= idx[None, :])).astype(np.float32)
    mbw = (same & (idx[:, None] >= idx[None, :])).astype(np.float32)
    masks = np.stack([mfw, mbw], axis=0)
    rmask = np.ones((128, 512), np.float32)
    rmask[:, ::64] = 0.0
    ident = np.eye(128, dtype=np.float32)
    cos, sinp = _rope_tables()
    gq, gk = f(inp["q_norm_g"])[0], f(inp["k_norm_g"])[0]
    gains = np.stack([gq, _swap32(gq), gk, _swap32(gk)], axis=0).astype(np.float32)
    gains = np.ascontiguousarray(np.broadcast_to(gains[None], (128, 4, HD)))
    shared = {
        "w_uv": f(w_in[:, 5120:7168]), "masks": masks, "rmask": rmask, "ident": ident,
        "wsT": f(f(inp["gmlp_ws"])[0].transpose(0, 2, 1)), "biasT": f(f(inp["gmlp_bias"])[0].T),
        "glng": _bc(f(inp["gmlp_ln_g"])[0]), "glnb": _bc(f(inp["gmlp_ln_b"])[0]),
        "w_out0": f(inp["w_out_ab"])[0], "w_out1": f(inp["w_out_attn"])[0],
        "w_qkv": f(inp["w_in_attn"])[0], "gains": gains,
    }
    for l in range(2):
        cw = np.concatenate([f(inp["ffn_conv_w"])[l], f(inp["ffn_conv_b"])[l][None, :]], axis=0)
        shared["cwb%d" % l] = f(cw.reshape(4, 88, 128).transpose(2, 1, 0))
        shared["w_up%d" % l] = f(inp["ffn_up"])[l]
        shared["w_down%d" % l] = f(inp["ffn_down"])[l]
        shared["ln1g%d" % l] = _bc(f(inp["ln1_g"])[l]); shared["ln1b%d" % l] = _bc(f(inp["ln1_b"])[l])
        shared["ln2g%d" % l] = _bc(f(inp["ln2_g"])[l]); shared["ln2b%d" % l] = _bc(f(inp["ln2_b"])[l])
    xT = [f(x[b].T) for b in range(NB)]
    in_maps = []
    for c in range(NCORES):
        b, j = divmod(c, 4)
        hs = [2 * j, 2 * j + 1]
        m = dict(shared)
        m["xT_b"] = xT[b]
        m["xT_own"] = f(xT[b][:, j * TOK:(j + 1) * TOK])
        m["x_own"] = f(x[b, j * TOK:(j + 1) * TOK, :])
        m["w_hq"] = f(np.concatenate([w_in[:, base + H * 128: base + (H + 1) * 128] for base in (0, 1024, 2048) for H in hs], axis=1))
        m["w_hi"] = f(np.concatenate([w_in[:, base + H * 128: base + (H + 1) * 128] for base in (3072, 4096) for H in hs], axis=1))
        m["lbt"] = f(np.stack([lb_table[:, H * 128:(H + 1) * 128].T for H in hs], axis=0))
        m["gn"] = f(np.stack([np.broadcast_to(norm_g[H * 128:(H + 1) * 128][None, :], (64, 128)) for H in hs], axis=0))
        m["cos"] = f(cos[j * TOK:(j + 1) * TOK]); m["sinp"] = f(sinp[j * TOK:(j + 1) * TOK])
        oh = np.zeros((128, 8), np.float32); oh[:, c] = 1.0
        ob_ = np.zeros((128, 2), np.float32); ob_[:, b] = 1.0
        mlh_ = np.zeros((2, 8), np.float32)
        if j < 3:
            mlh_[0, c + 1] = 1.0
        if j > 0:
            mlh_[1, c - 1] = 1.0
        m["oh8"] = oh; m["ohb"] = ob_; m["mlh"] = mlh_
        in_maps.append(m)
    return in_maps


_FUSED = {}


def kernel(**inp):
    if "nc" not in _FUSED:
        _FUSED["nc"] = build_fused(dbg=False)[0]
    in_maps = _fused_inputs(inp)
    in_maps = [{k: v for k, v in m.items() if k in _EI_NAMES} for m in in_maps]
    res = run_bass_kernel_spmd(_FUSED["nc"], in_maps, core_ids=list(range(NCORES)))
    return _gather_tok(res, "y", D)
```
